# Optimizing a Trainium2 kernel written in Bass

```python
import jax, jax.numpy as jnp
from jax import lax
import numpy as np

D_MODEL = 1024
BATCH = 2
SEQ = 16384
DEPTH = 1

HEAD_DIM = 64
N_ATTN_HEADS = 16
D_ATTN = N_ATTN_HEADS * HEAD_DIM
N_RWKV_HEADS = D_MODEL // HEAD_DIM
D_RWKV = N_RWKV_HEADS * HEAD_DIM
ROPE_DIM = HEAD_DIM // 4
ROPE_THETA = 500000.0
DILATED_GROUPS = ((128, 1), (512, 4), (2048, 16))
BLOCK = 128
DECAY_LORA = 64
ICLR_LORA = 64
GATE_LORA = 128
D_FF = -(-8 * D_MODEL // (3 * 256)) * 256
RMS_EPS = 1e-6
GN_EPS = 64e-5
N_BRANCHES = 2
D_SHIFTED = 3 * D_RWKV + DECAY_LORA + ICLR_LORA + GATE_LORA
D_IN_PROJ = 3 * D_ATTN + D_SHIFTED + N_BRANCHES * D_MODEL

kernel_name = "hybrid_dilated_attn_rwkv7_block"


def _rmsnorm(x, g):
    xf = x.astype(jnp.float32)
    y = xf * lax.rsqrt(jnp.mean(xf * xf, axis=-1, keepdims=True) + RMS_EPS)
    return (y * g.astype(jnp.float32)).astype(x.dtype)


def _partial_rotary(t, pos):
    half = ROPE_DIM // 2
    inv_freq = ROPE_THETA ** (-jnp.arange(half, dtype=jnp.float32) * (2.0 / ROPE_DIM))
    ang = pos.astype(jnp.float32)[:, None] * inv_freq[None, :]
    cos = jnp.cos(ang)[None, :, None, :]
    sin = jnp.sin(ang)[None, :, None, :]
    t1, t2, rest = t[..., :half], t[..., half:ROPE_DIM], t[..., ROPE_DIM:]
    return jnp.concatenate([t1 * cos - t2 * sin, t2 * cos + t1 * sin, rest], axis=-1)


def _dilated_window_attention(q, k, v, window, dilation):
    B, S, H, Dh = q.shape
    span = window // dilation
    chunk = dilation * BLOCK
    L = -(-S // chunk) * chunk
    n_blk = L // chunk
    pad = ((0, 0), (0, L - S), (0, 0), (0, 0))

    def to_blocks(t):
        t = jnp.pad(t, pad).reshape(B, L // dilation, dilation, H, Dh)
        t = jnp.swapaxes(t, 1, 2)
        return t.reshape(B, dilation, n_blk, BLOCK, H, Dh)

    def with_prev(t):
        prev = jnp.concatenate([jnp.zeros_like(t[:, :, :1]), t[:, :, :-1]], axis=2)
        return jnp.concatenate([prev, t], axis=3)

    qb = to_blocks(q)
    kw = with_prev(to_blocks(k))
    vw = with_prev(to_blocks(v))
    s = jnp.einsum('brnqhd,brnkhd->brnhqk', qb, kw) * (Dh ** -0.5)
    qi = jnp.arange(BLOCK)[:, None]
    kj = jnp.arange(2 * BLOCK)[None, :]
    dist = qi + BLOCK - kj
    band = (dist >= 0) & (dist <= span)
    first = (jnp.arange(n_blk)[:, None, None] > 0) | (kj[None] >= BLOCK)
    mask = (band[None] & first)[:, None]
    s = jnp.where(mask, s, -jnp.inf)
    lse = jax.nn.logsumexp(s, axis=-1)
    p = jnp.exp(s - lse[..., None])
    o = jnp.einsum('brnhqk,brnkhd->brnqhd', p, vw)

    def from_blocks(t):
        t = t.reshape(B, dilation, L // dilation, *t.shape[4:])
        t = jnp.swapaxes(t, 1, 2)
        return t.reshape(B, L, *t.shape[3:])[:, :S]

    return from_blocks(o), from_blocks(jnp.swapaxes(lse, 3, 4))


def _attention_branch(qkv, pos):
    B, S, _ = qkv.shape
    q, k, v = jnp.split(qkv.astype(jnp.float32), 3, axis=-1)
    q = _partial_rotary(q.reshape(B, S, N_ATTN_HEADS, HEAD_DIM), pos)
    k = _partial_rotary(k.reshape(B, S, N_ATTN_HEADS, HEAD_DIM), pos)
    v = v.reshape(B, S, N_ATTN_HEADS, HEAD_DIM)
    outs, lses = [], []
    for window, dilation in DILATED_GROUPS:
        o_g, lse_g = _dilated_window_attention(q, k, v, window, dilation)
        outs.append(o_g)
        lses.append(lse_g)
    wts = jax.nn.softmax(jnp.stack(lses, axis=0), axis=0)
    o = jnp.sum(wts[..., None] * jnp.stack(outs, axis=0), axis=0)
    return o.reshape(B, S, D_ATTN)


def _rwkv7_step(state, inp):
    r_t, w_t, k_t, v_t, a_t, b_t = inp
    sa = jnp.einsum('bhvk,bhk->bhv', state, a_t)
    state = (state * w_t[:, :, None, :] + sa[..., None] * b_t[:, :, None, :]
             + v_t[..., None] * k_t[:, :, None, :])
    y = jnp.einsum('bhvk,bhk->bhv', state, r_t)
    return state, y


def _rwkv7_branch(cols, mu, w0, w2, a0, a2, g2, k_k, k_a, r_k, ln_w, ln_b):
    B, S, _ = cols.shape
    f32 = jnp.float32
    prev = jnp.pad(cols, ((0, 0), (1, 0), (0, 0)))[:, :-1]
    xm = cols + (prev - cols) * mu
    splits = [D_RWKV, 2 * D_RWKV, 3 * D_RWKV, 3 * D_RWKV + DECAY_LORA,
              3 * D_RWKV + DECAY_LORA + ICLR_LORA]
    r, k, v, w_lo, a_lo, g_lo = jnp.split(xm, splits, axis=-1)
    w_log = -jax.nn.softplus(-(w0 + jnp.tanh(w_lo) @ w2).astype(f32)) - 0.5
    decay = jnp.exp(-jnp.exp(w_log))
    a = jax.nn.sigmoid((a0 + a_lo @ a2).astype(f32))
    g = (jax.nn.sigmoid(g_lo) @ g2).astype(f32)

    def heads(t):
        return t.astype(f32).reshape(B, S, N_RWKV_HEADS, HEAD_DIM)

    r, k, v, decay, a = heads(r), heads(k), heads(v), heads(decay), heads(a)
    hk = (N_RWKV_HEADS, HEAD_DIM)
    kk = k * k_k.astype(f32).reshape(hk)
    kk = kk / jnp.maximum(jnp.sqrt(jnp.sum(kk * kk, axis=-1, keepdims=True)), 1e-12)
    k = k * (1.0 + (a - 1.0) * k_a.astype(f32).reshape(hk))

    def tm(t):
        return jnp.swapaxes(t, 0, 1)

    state0 = jnp.zeros((B, N_RWKV_HEADS, HEAD_DIM, HEAD_DIM), f32)
    _, y = lax.scan(_rwkv7_step, state0,
                    (tm(r), tm(decay), tm(k), tm(v), tm(-kk), tm(kk * a)))
    y = jnp.swapaxes(y, 0, 1)
    mean = jnp.mean(y, axis=-1, keepdims=True)
    var = jnp.mean(jnp.square(y - mean), axis=-1, keepdims=True)
    yn = ((y - mean) * lax.rsqrt(var + GN_EPS) * ln_w.astype(f32).reshape(hk)
          + ln_b.astype(f32).reshape(hk))
    bonus = jnp.sum(r * k * r_k.astype(f32), axis=-1, keepdims=True) * v
    return (yn + bonus).reshape(B, S, D_RWKV) * g


def setup_inputs(seed: int = 0) -> dict:
    key = jax.random.key(seed)
    ks = jax.random.split(key, 22)
    f32 = jnp.float32
    L = DEPTH

    def nrm(k, shape, scale):
        return jax.random.normal(k, shape, f32) * scale

    return {
        "x": nrm(ks[0], (BATCH, SEQ, D_MODEL), 1.0),
        "norm_mix_g": 1.0 + nrm(ks[1], (L, D_MODEL), 0.02),
        "w_in": nrm(ks[2], (L, D_MODEL, D_IN_PROJ), D_MODEL ** -0.5),
        "shift_mu": jax.random.uniform(ks[3], (L, D_SHIFTED), f32),
        "decay_w0": jax.random.uniform(ks[4], (L, D_RWKV), f32, -6.0, 1.0),
        "decay_w2": nrm(ks[5], (L, DECAY_LORA, D_RWKV), 0.1 * DECAY_LORA ** -0.5),
        "iclr_a0": nrm(ks[6], (L, D_RWKV), 0.1),
        "iclr_a2": nrm(ks[7], (L, ICLR_LORA, D_RWKV), 0.1 * ICLR_LORA ** -0.5),
        "gate_g2": nrm(ks[8], (L, GATE_LORA, D_RWKV), GATE_LORA ** -0.5),
        "k_k": 0.85 + nrm(ks[9], (L, D_RWKV), 0.02),
        "k_a": 1.0 + nrm(ks[10], (L, D_RWKV), 0.02),
        "r_k": nrm(ks[11], (L, N_RWKV_HEADS, HEAD_DIM), 0.1),
        "ln_x_w": 1.0 + nrm(ks[12], (L, D_RWKV), 0.02),
        "ln_x_b": nrm(ks[13], (L, D_RWKV), 0.02),
        "proj_attn": nrm(ks[14], (L, D_ATTN, D_MODEL), D_ATTN ** -0.5),
        "proj_rwkv": nrm(ks[15], (L, D_RWKV, D_MODEL), D_RWKV ** -0.5),
        "w_out": nrm(ks[16], (L, D_MODEL, D_MODEL), D_MODEL ** -0.5),
        "norm_ffn_g": 1.0 + nrm(ks[17], (L, D_MODEL), 0.02),
        "ffn_w_gate": nrm(ks[18], (L, D_MODEL, D_FF), D_MODEL ** -0.5),
        "ffn_w_up": nrm(ks[19], (L, D_MODEL, D_FF), D_MODEL ** -0.5),
        "ffn_w_down": nrm(ks[20], (L, D_FF, D_MODEL), D_FF ** -0.5),
        "norm_final_g": 1.0 + nrm(ks[21], (D_MODEL,), 0.02),
    }


def reference(x, norm_mix_g, w_in, shift_mu, decay_w0, decay_w2, iclr_a0, iclr_a2,
              gate_g2, k_k, k_a, r_k, ln_x_w, ln_x_b, proj_attn, proj_rwkv, w_out,
              norm_ffn_g, ffn_w_gate, ffn_w_up, ffn_w_down, norm_final_g):
    B, S, _ = x.shape
    pos = jnp.arange(S, dtype=jnp.int32)
    for l in range(DEPTH):
        h = _rmsnorm(x, norm_mix_g[l])
        p = h @ w_in[l]
        qkv = p[..., :3 * D_ATTN]
        shifted = p[..., 3 * D_ATTN:3 * D_ATTN + D_SHIFTED]
        gate_logits = p[..., 3 * D_ATTN + D_SHIFTED:]
        o_a = _attention_branch(qkv, pos)
        o_b = _rwkv7_branch(shifted, shift_mu[l], decay_w0[l], decay_w2[l], iclr_a0[l],
                            iclr_a2[l], gate_g2[l], k_k[l], k_a[l], r_k[l],
                            ln_x_w[l], ln_x_b[l])
        gates = jax.nn.sigmoid(gate_logits.astype(jnp.float32))
        g_a, g_b = gates[..., :D_MODEL], gates[..., D_MODEL:]
        merged = (g_a * (o_a.astype(x.dtype) @ proj_attn[l]).astype(jnp.float32)
                  + g_b * (o_b.astype(x.dtype) @ proj_rwkv[l]).astype(jnp.float32))
        x = x + merged.astype(x.dtype) @ w_out[l]
        h = _rmsnorm(x, norm_ffn_g[l])
        x = x + (jax.nn.silu(h @ ffn_w_gate[l]) * (h @ ffn_w_up[l])) @ ffn_w_down[l]
    return _rmsnorm(x, norm_final_g)
```

```python
import os
import numpy as np
import concourse.bass as bass
import concourse.mybir as mybir
from concourse.bass_utils import run_bass_kernel_spmd

F32 = mybir.dt.float32
BF16 = mybir.dt.bfloat16
AF = mybir.ActivationFunctionType
ALU = mybir.AluOpType
AX = mybir.AxisListType

class Sched:
    COMPUTE = ("pe", "act", "dve", "pool")
    ALLQ = ("pe", "act", "dve", "pool", "sp")

    def __init__(self, nc, stack):
        self.nc = nc
        self.stack = stack
        self.ops = {e: [] for e in self.ALLQ}
        self.cnt = {e: 0 for e in self.COMPUTE}
        self.esem = {e: stack.enter_context(nc.semaphore("prog_" + e)) for e in self.COMPUTE}
        self.known = {e: {} for e in self.ALLQ}
        self.bufs = {}
        self.dsem = {}
        self.dcnt = {}
        self.final = []

    def _buf(self, k):
        b = self.bufs.get(k)
        if b is None:
            b = self.bufs[k] = {"w": [], "r": []}
        return b

    def _dma_sem(self, k):
        if k not in self.dsem:
            self.dsem[k] = self.stack.enter_context(self.nc.semaphore("d_" + str(len(self.dsem))))
            self.dcnt[k] = 0
        return self.dsem[k]

    def _emit(self, eng, reads, writes, fn, tok_fn):
        waits = {}
        def need(tok):
            s, v, src = tok
            if src == eng and eng == "pe":
                return
            if waits.get(s, (0,))[0] < v:
                waits[s] = (v, src)
        for k in reads:
            for t in self._buf(k)["w"]:
                need(t)
        for k in writes:
            b = self._buf(k)
            for t in b["w"]:
                need(t)
            for t in b["r"]:
                need(t)
        wl = []
        kn = self.known[eng]
        for s, (v, src) in waits.items():
            if kn.get(id(s), 0) >= v:
                continue
            kn[id(s)] = v
            wl.append((s, v))
        tok = tok_fn()
        self.ops[eng].append((wl, fn, tok))
        for k in reads:
            b = self._buf(k)
            b["r"] = [t for t in b["r"] if not (t[2] == eng and eng in self.COMPUTE)] + [tok]
        for k in writes:
            b = self._buf(k)
            b["w"] = [tok]
            b["r"] = []
        return tok

    def op(self, eng, fn, reads=(), writes=()):
        def tok_fn():
            self.cnt[eng] += 1
            return (self.esem[eng], self.cnt[eng], eng)
        return self._emit(eng, reads, writes, fn, tok_fn)

    def dma(self, q, fn, reads=(), writes=(), key=None):
        key = key if key is not None else (writes[0] if writes else reads[0])
        sem = self._dma_sem(("dma", key))
        def tok_fn():
            self.dcnt[("dma", key)] += 16
            return (sem, self.dcnt[("dma", key)], "dma")
        return self._emit(q, reads, writes, fn, tok_fn)

    def cc(self, fn, reads=(), writes=(), key="cc"):
        sem = self._dma_sem(("cc", key))
        def tok_fn():
            self.dcnt[("cc", key)] += 1
            return (sem, self.dcnt[("cc", key)], "cc")
        return self._emit("pool", reads, writes, fn, tok_fn)

    def finish(self, toks):
        self.final = list(toks)

    def barrier(self):
        for e in self.ALLQ:
            wl = []
            kn = self.known[e]
            for e2 in self.COMPUTE:
                if e2 == e or self.cnt[e2] == 0:
                    continue
                if kn.get(id(self.esem[e2]), 0) < self.cnt[e2]:
                    kn[id(self.esem[e2])] = self.cnt[e2]
                    wl.append((self.esem[e2], self.cnt[e2]))
            for k, sem in self.dsem.items():
                v = self.dcnt[k]
                if v and kn.get(id(sem), 0) < v:
                    kn[id(sem)] = v
                    wl.append((sem, v))
            self.ops[e].append((wl, None, None))
        self.bufs = {}

    def run(self):
        nc = self.nc
        engmap = {"pe": "tensor", "act": "scalar", "dve": "vector", "pool": "gpsimd", "sp": "sync"}
        with nc.Block() as block:
            for e in self.ALLQ:
                ops = self.ops[e]
                fin = self.final if e == "sp" else []
                if not ops and not fin:
                    continue
                def body(engine, ops=ops, fin=fin):
                    for wl, fn, tok in ops:
                        for s, v in wl:
                            engine.wait_ge(s, v)
                        if fn is not None:
                            fn(engine).then_inc(tok[0], 16 if tok[2] == "dma" else 1)
                    for s, v, _ in fin:
                        engine.wait_ge(s, v)
                getattr(block, engmap[e])(body)
        self.ops = {e: [] for e in self.ALLQ}
        self.final = []


D = 1024
DFF = 2816
NFT = DFF // 128
EPS = 1e-6

def L(f, *a, **k):
    return lambda e: getattr(e, f)(*a, **k)

class Ctx:
    def __init__(self, nc, S, st, pfx=""):
        self.nc, self.S, self.st, self.pfx = nc, S, st, pfx
        self.dyn = None
        self.n = 0
        self.rr = {}
    def sb(self, name, shape, dt):
        return self.st.enter_context(self.nc.sbuf_tensor("s_" + self.pfx + name, shape, dt))
    def ps(self, name, shape, dt):
        return self.st.enter_context(self.nc.psum_tensor("p_" + self.pfx + name, shape, dt))
    def dram(self, name, shape, dt):
        return self.nc.dram_tensor(name, shape, dt, kind="Internal").ap()
    def nxt(self, name, n):
        v = self.rr.get(name, 0)
        self.rr[name] = v + 1
        return v % n


def setup_common(C, ident_d):
    S = C.S
    C.ident = C.sb("ident", [128, 128], BF16)
    S.dma("pool", L("dma_start", out=C.ident[:], in_=ident_d[:, :]), writes=["ident"])
    C.stat = C.sb("stat", [128, 64], F32)
    C.NACC = 6
    C.acc = [C.ps(f"acc{i}", [128, 512], F32) for i in range(C.NACC)]
    C.pt = [C.ps(f"pt{i}", [128, 1024], BF16) for i in range(2)]
    C.hn = [C.sb(f"hn{i}", [128, 1024], BF16) for i in range(2)]
    C.junk = C.sb("junk", [128, 1024], BF16)


def load_gfull(C, name, g_d):
    S = C.S
    gcol = C.sb(name + "_col", [128, 8], F32)
    gfull = C.sb(name, [128, 8, 128], F32)
    S.dma("sp", L("dma_start", out=gcol[:], in_=g_d.rearrange("(c p) -> p c", p=128), allow_slow_non_contiguous=True), writes=[name + "_col"])
    S.op("dve", L("tensor_copy", out=gfull[:], in_=gcol[:].unsqueeze(2).to_broadcast([128, 8, 128])), reads=[name + "_col"], writes=[name])
    return gfull


def norm_transpose(C, x_ap, xkey, gfull, gkey, hT_ap, hkey):
    S = C.S
    i = C.nxt("stat", 16)
    ss, rs, rstd = C.stat[:, 3 * i:3 * i + 1], C.stat[:, 3 * i + 1:3 * i + 2], C.stat[:, 3 * i + 2:3 * i + 3]
    sk = f"stat{i}"
    b = C.nxt("hn", 2)
    hn, hnk = C.hn[b], f"hn{b}"
    pt, ptk = C.pt[b], f"pt{b}"
    S.op("act", L("activation", out=C.junk[:], in_=x_ap, func=AF.Square, accum_out=ss), reads=[xkey], writes=["junk", sk])
    S.op("act", L("activation", out=rs, in_=ss, func=AF.Sqrt, scale=1.0 / D, bias=C.epsb[:, 0:1]), reads=[sk, "epsb"], writes=[sk])
    S.op("dve", L("reciprocal", out=rstd, in_=rs), reads=[sk], writes=[sk])
    S.op("act", L("activation", out=hn[:], in_=x_ap, func=AF.Copy, scale=rstd), reads=[xkey, sk], writes=[hnk])
    for c in range(8):
        S.op("pe", L("transpose", out=pt[:, c * 128:(c + 1) * 128], in_=hn[:, c * 128:(c + 1) * 128], identity=C.ident[:]), reads=[hnk, "ident"], writes=[ptk])
    S.op("dve", L("tensor_tensor", out=hT_ap, in0=pt[:].rearrange("p (c t) -> p c t", c=8), in1=gfull[:], op=ALU.mult), reads=[ptk, gkey], writes=[hkey])
    return rstd, sk


def convert_weights(C, w):
    S = C.S
    w16 = {}
    for k, N in (("wg", 2048), ("pa", D), ("pb", D), ("wo", D), ("fg", DFF), ("fu", DFF)):
        nblk = -(-N // 512)
        w16[k] = C.dram("b16_" + k, [nblk, 128, 8, 512], BF16)
        for nb in range(nblk):
            wc = min(512, N - nb * 512)
            for c in range(8):
                S.dma("pool", L("dma_start", out=w16[k][nb, :, c, 0:wc], in_=w[k][c * 128:(c + 1) * 128, nb * 512:nb * 512 + wc]), writes=["w16_" + k], key="w16_" + k)
    w16["fd"] = C.dram("b16_fd", [2, 128, NFT, 512], BF16)
    for n2 in range(2):
        for ft in range(NFT):
            S.dma("pool", L("dma_start", out=w16["fd"][n2, :, ft, :], in_=w["fd"][ft * 128:(ft + 1) * 128, n2 * 512:(n2 + 1) * 512]), writes=["w16_fd"], key="w16_fd")
    return w16


def dsl(C, e, t0, n):
    if C.dyn is None:
        return slice(t0, t0 + n)
    if "v" not in C.dyn:
        e.reg_load(C.dyn["reg"], C.dyn["qoff"][0:1, 0:1])
        C.dyn["v"] = e.snap(C.dyn["reg"])
    return bass.ds(C.dyn["v"] + t0, n)


def ab_src(C, e, src_d, t0, n):
    if C.dyn is None:
        return src_d[:, t0:t0 + n]
    if "vm" not in C.dyn:
        e.reg_load(C.dyn["mreg"], C.dyn["qm"][0:1, 0:1])
        C.dyn["vm"] = e.snap(C.dyn["mreg"])
    CH = C.dyn["CH"]
    c0 = t0 % CH
    return src_d[bass.ds(C.dyn["vm"] + (t0 // CH), 1), :, c0:c0 + n].rearrange("o r t -> (o r) t")


def phase_b(C, TB, x_d, oaT_d, obT_d, out_d, w, w16):
    nc, S = C.nc, C.S
    TT = 512
    NT = TB // TT
    gmix = load_gfull(C, "gmix", w["g_mix"])
    gffn = load_gfull(C, "gffn", w["g_ffn"])
    gfin = C.sb("gfin", [128, D], F32)
    S.dma("sp", L("dma_start", out=gfin[:], in_=w["g_fin"].partition_broadcast(128)), writes=["gfin"])
    C.epsb = C.sb("epsb", [128, 1], F32)
    S.op("pool", L("memset", C.epsb[:], EPS), writes=["epsb"])

    xres = C.sb("xres", [128, 4, D], F32)
    hT = C.sb("hT", [128, 8, TT], BF16)
    oaT = C.sb("oaT", [128, 8, TT], BF16)
    obT = C.sb("obT", [128, 8, TT], BF16)
    mT = C.sb("mT", [128, 8, TT], BF16)
    actT = C.sb("actT", [128, NFT, TT], BF16)
    NSL = 4
    slab = [C.sb(f"slab{i}", [128, 8, 512], BF16) for i in range(NSL)]
    dslab = [C.sb(f"dslab{i}", [128, NFT, 512], BF16) for i in range(2)]
    tmp = [C.sb(f"tmp{i}", [128, 512], F32) for i in range(4)]
    outb = C.sb("outb", [128, D], F32)

    def load_slab(wk, n0):
        i = C.nxt("slab", NSL)
        S.dma("sp", L("dma_start", out=slab[i][:], in_=w16[wk][n0 // 512]), reads=["w16_" + wk], writes=[f"slab{i}"])
        return slab[i], f"slab{i}"

    def mm_group(wk_slab, wkey, ncol, rhs_t, rkey):
        a = C.nxt("acc", C.NACC)
        for c in range(8):
            S.op("pe", L("matmul", C.acc[a][:, :], lhsT=wk_slab[:, c, ncol * 128:(ncol + 1) * 128], rhs=rhs_t[:, c, :], start=(c == 0), stop=(c == 7)),
                 reads=[wkey, rkey], writes=[f"acc{a}"])
        return C.acc[a], f"acc{a}"

    out_toks = []
    for t in range(NT):
        t0 = t * TT
        S.dma("sp", (lambda e, t0=t0: e.dma_start(out=xres[:], in_=x_d[dsl(C, e, t0, TT), :].rearrange("(j p) d -> p j d", p=128))), writes=["xres"])
        S.dma("sp", (lambda e, t0=t0: e.dma_start(out=oaT[:], in_=ab_src(C, e, oaT_d, t0, TT).rearrange("(c p) t -> p c t", p=128))), reads=["recv_a"], writes=["oaT"])
        S.dma("sp", (lambda e, t0=t0: e.dma_start(out=obT[:], in_=ab_src(C, e, obT_d, t0, TT).rearrange("(c p) t -> p c t", p=128))), reads=["recv_b"], writes=["obT"])
        for j in range(4):
            norm_transpose(C, xres[:, j, :], "xres", gmix, "gmix", hT[:, :, j * 128:(j + 1) * 128], "hT")
        for q4 in range(2):
            sga, kga = load_slab("wg", q4 * 512)
            sgb, kgb = load_slab("wg", 1024 + q4 * 512)
            spa, kpa = load_slab("pa", q4 * 512)
            spb, kpb = load_slab("pb", q4 * 512)
            for n in range(4):
                ct = q4 * 4 + n
                GA, kGA = mm_group(sga, kga, n, hT, "hT")
                PA, kPA = mm_group(spa, kpa, n, oaT, "oaT")
                GB, kGB = mm_group(sgb, kgb, n, hT, "hT")
                PB, kPB = mm_group(spb, kpb, n, obT, "obT")
                S.op("act", L("activation", out=tmp[0][:], in_=GA[:], func=AF.Sigmoid), reads=[kGA], writes=["tmp0"])
                S.op("act", L("activation", out=tmp[1][:], in_=GB[:], func=AF.Sigmoid), reads=[kGB], writes=["tmp1"])
                S.op("dve", L("tensor_tensor", out=tmp[2][:], in0=PA[:], in1=tmp[0][:], op=ALU.mult), reads=[kPA, "tmp0"], writes=["tmp2"])
                S.op("dve", L("tensor_tensor", out=tmp[3][:], in0=PB[:], in1=tmp[1][:], op=ALU.mult), reads=[kPB, "tmp1"], writes=["tmp3"])
                S.op("pool", L("tensor_tensor", out=mT[:, ct, :], in0=tmp[2][:], in1=tmp[3][:], op=ALU.add), reads=["tmp2", "tmp3"], writes=["mT"])
        for n2 in range(2):
            so, ko = load_slab("wo", n2 * 512)
            for j in range(4):
                a = C.nxt("acc", C.NACC)
                for c in range(8):
                    S.op("pe", L("matmul", C.acc[a][:, :], lhsT=mT[:, c, j * 128:(j + 1) * 128], rhs=so[:, c, :], start=(c == 0), stop=(c == 7)),
                         reads=["mT", ko], writes=[f"acc{a}"])
                S.op("dve", L("tensor_tensor", out=xres[:, j, n2 * 512:(n2 + 1) * 512], in0=C.acc[a][:], in1=xres[:, j, n2 * 512:(n2 + 1) * 512], op=ALU.add),
                     reads=[f"acc{a}", "xres"], writes=["xres"])
        for j in range(4):
            norm_transpose(C, xres[:, j, :], "xres", gffn, "gffn", hT[:, :, j * 128:(j + 1) * 128], "hT")
        for f4 in range(0, DFF, 512):
            wcols = min(512, DFF - f4)
            i1 = C.nxt("slab", NSL)
            S.dma("sp", L("dma_start", out=slab[i1][:, :, 0:wcols], in_=w16["fg"][f4 // 512][:, :, 0:wcols]), reads=["w16_fg"], writes=[f"slab{i1}"])
            i2 = C.nxt("slab", NSL)
            S.dma("sp", L("dma_start", out=slab[i2][:, :, 0:wcols], in_=w16["fu"][f4 // 512][:, :, 0:wcols]), reads=["w16_fu"], writes=[f"slab{i2}"])
            for n in range(wcols // 128):
                ft = f4 // 128 + n
                G, kG = mm_group(slab[i1], f"slab{i1}", n, hT, "hT")
                U, kU = mm_group(slab[i2], f"slab{i2}", n, hT, "hT")
                tb = C.nxt("tmpf", 2)
                S.op("act", L("activation", out=tmp[tb][:], in_=G[:], func=AF.Silu), reads=[kG], writes=[f"tmp{tb}"])
                S.op("dve", L("tensor_tensor", out=actT[:, ft, :], in0=U[:], in1=tmp[tb][:], op=ALU.mult), reads=[kU, f"tmp{tb}"], writes=["actT"])
        for n2 in range(2):
            di = C.nxt("dslab", 2)
            S.dma("sp", L("dma_start", out=dslab[di][:], in_=w16["fd"][n2]), reads=["w16_fd"], writes=[f"dslab{di}"])
            for j in range(4):
                a = C.nxt("acc", C.NACC)
                for ft in range(NFT):
                    S.op("pe", L("matmul", C.acc[a][:, :], lhsT=actT[:, ft, j * 128:(j + 1) * 128], rhs=dslab[di][:, ft, :], start=(ft == 0), stop=(ft == NFT - 1)),
                         reads=["actT", f"dslab{di}"], writes=[f"acc{a}"])
                S.op("dve", L("tensor_tensor", out=xres[:, j, n2 * 512:(n2 + 1) * 512], in0=C.acc[a][:], in1=xres[:, j, n2 * 512:(n2 + 1) * 512], op=ALU.add),
                     reads=[f"acc{a}", "xres"], writes=["xres"])
        for j in range(4):
            i = C.nxt("stat", 16)
            ss, rs, rstd = C.stat[:, 3 * i:3 * i + 1], C.stat[:, 3 * i + 1:3 * i + 2], C.stat[:, 3 * i + 2:3 * i + 3]
            sk = f"stat{i}"
            S.op("act", L("activation", out=C.junk[:], in_=xres[:, j, :], func=AF.Square, accum_out=ss), reads=["xres"], writes=["junk", sk])
            S.op("act", L("activation", out=rs, in_=ss, func=AF.Sqrt, scale=1.0 / D, bias=C.epsb[:, 0:1]), reads=[sk, "epsb"], writes=[sk])
            S.op("dve", L("reciprocal", out=rstd, in_=rs), reads=[sk], writes=[sk])
            S.op("dve", L("scalar_tensor_tensor", out=outb[:], in0=xres[:, j, :], scalar=rstd, in1=gfin[:], op0=ALU.mult, op1=ALU.mult), reads=["xres", sk, "gfin"], writes=["outb"])
            tk = S.dma("sp", L("dma_start", out=out_d[t0 + j * 128:t0 + (j + 1) * 128, :], in_=outb[:]), reads=["outb"], key="outd")
        out_toks = [tk]
    return out_toks


DILS = (1, 4, 16)

def blk_geom(di, b):
    d = DILS[di]
    if d == 1:
        return 128 * b, 1
    if d == 4:
        return 512 * (b // 4) + (b % 4), 4
    return b, 16

def prev_blk(di, b):
    d = DILS[di]
    if d == 1:
        return ("cur", b - 1) if b >= 1 else ("prev", 0)
    if d == 4:
        return ("cur", b - 4) if b >= 4 else ("prev", b)
    return ("prev", b)

def prev_src_blk(di, j):
    d = DILS[di]
    if d == 1:
        return 15
    if d == 4:
        return 12 + j
    return j
NPREV = (1, 4, 16)


def phase_a1(C, SA, x_d, wqkv_d, g_d, cosF_d, sinF_d, rperm_d, mask_d, out_ap, out_piece=2048):
    nc, S = C.nc, C.S
    MC = 2048
    NMC = SA // MC
    gmix = load_gfull(C, "gmix", g_d)
    C.epsb = C.sb("epsb", [128, 1], F32)
    S.op("pool", L("memset", C.epsb[:], EPS), writes=["epsb"])
    W = C.sb("Wqkv", [128, 8, 768], BF16)
    for c in range(8):
        S.dma("pool", L("dma_start", out=W[:, c, :], in_=wqkv_d[c * 128:(c + 1) * 128, :]), writes=["W"])
    rperm = C.sb("rperm", [128, 128], BF16)
    S.dma("pool", L("dma_start", out=rperm[:], in_=rperm_d[:, :]), writes=["rperm"])
    mask = C.sb("mask", [128, 512], BF16)
    S.dma("pool", L("dma_start", out=mask[:], in_=mask_d[:, :]), writes=["mask"])
    ones65 = C.sb("ones65", [128, 64], F32)
    S.op("pool", L("memset", ones65[:], 1.0), writes=["ones65"])

    xt = [C.sb(f"xt{i}", [128, D], F32) for i in range(2)]
    hT = C.sb("hT", [128, 8, 512], BF16)
    cosF = C.sb("cosF", [128, 512], F32)
    sinF = C.sb("sinF", [128, 512], F32)
    qraw = C.sb("qraw", [128, 512], BF16)
    t1 = C.sb("t1", [128, 512], F32)
    t2 = C.sb("t2", [128, 512], F32)
    qT = C.sb("qT", [128, 2, MC], BF16)
    kT = [C.sb(f"kT{i}", [128, 2, MC], BF16) for i in range(2)]
    vT = C.sb("vT", [128, 2, MC], BF16)
    Vc = [C.sb(f"Vc{di}", [128, 16, 4, 128], BF16) for di in range(3)]
    Vp = [C.sb(f"Vp{di}", [128, NPREV[di], 4, 128], BF16) for di in range(3)]
    ones3 = ones65[:].unsqueeze(1).to_broadcast([128, 4, 64])
    for di in range(3):
        for b in range(16):
            S.op("dve", L("tensor_copy", out=Vc[di][:, b, :, 64:128], in_=ones3), reads=["ones65"], writes=[f"Vc{di}"])
        for b in range(NPREV[di]):
            S.op("dve", L("tensor_copy", out=Vp[di][:, b, :, 64:128], in_=ones3), reads=["ones65"], writes=[f"Vp{di}"])
    oacc = [C.sb(f"oacc{i}", [128, MC], F32) for i in range(2)]
    pexp = [C.sb(f"pexp{i}", [128, 512], BF16) for i in range(2)]
    pm = [C.sb(f"pm{i}", [128, 512], BF16) for i in range(2)]
    rden = C.sb("rden", [128, 512], F32)
    rden2 = C.sb("rden2", [64, 512], F32)
    oout = [C.sb(f"oout{i}", [64, MC], BF16) for i in range(2)]

    toks = []
    for m in range(NMC):
        slot = m % 2
        kTc, kTk = kT[slot], f"kT{slot}"
        kTp, kTpk = kT[1 - slot], f"kT{1 - slot}"
        for sc in range(4):
            t0 = m * MC + sc * 512
            l0 = sc * 512
            for j in range(4):
                xb = C.nxt("xt", 2)
                S.dma("sp", L("dma_start", out=xt[xb][:], in_=x_d[t0 + j * 128:t0 + (j + 1) * 128, :]), writes=[f"xt{xb}"])
                norm_transpose(C, xt[xb][:], f"xt{xb}", gmix, "gmix", hT[:, :, j * 128:(j + 1) * 128], "hT")
            S.dma("act", L("dma_start", out=cosF[:], in_=cosF_d[:, t0:t0 + 512]), writes=["cosF"])
            S.dma("act", L("dma_start", out=sinF[:], in_=sinF_d[:, t0:t0 + 512]), writes=["sinF"])
            for ti in range(6):
                a = C.nxt("acc", C.NACC)
                for c in range(8):
                    S.op("pe", L("matmul", C.acc[a][:, :], lhsT=W[:, c, ti * 128:(ti + 1) * 128], rhs=hT[:, c, :], start=(c == 0), stop=(c == 7)),
                         reads=["W", "hT"], writes=[f"acc{a}"])
                hp = ti % 2
                if ti >= 4:
                    S.op("act", L("activation", out=vT[:, hp, l0:l0 + 512], in_=C.acc[a][:], func=AF.Copy), reads=[f"acc{a}"], writes=["vT"])
                    continue
                S.op("dve", L("tensor_copy", out=qraw[:], in_=C.acc[a][:]), reads=[f"acc{a}"], writes=["qraw"])
                a2 = C.nxt("acc", C.NACC)
                S.op("pe", L("matmul", C.acc[a2][:, :], lhsT=rperm[:], rhs=qraw[:], start=True, stop=True), reads=["rperm", "qraw"], writes=[f"acc{a2}"])
                S.op("dve", L("tensor_tensor", out=t1[:], in0=C.acc[a][:], in1=cosF[:], op=ALU.mult), reads=[f"acc{a}", "cosF"], writes=["t1"])
                S.op("dve", L("tensor_tensor", out=t2[:], in0=C.acc[a2][:], in1=sinF[:], op=ALU.mult), reads=[f"acc{a2}", "sinF"], writes=["t2"])
                if ti < 2:
                    dst, dk = qT[:, hp, l0:l0 + 512], "qT"
                else:
                    dst, dk = kTc[:, hp, l0:l0 + 512], kTk
                S.op("pool", L("tensor_tensor", out=dst, in0=t1[:], in1=t2[:], op=ALU.add), reads=["t1", "t2"], writes=[dk])
        for di in range(3):
            for b0 in range(0, 16, 4):
                pb = C.nxt("hn", 2)
                pt, ptk = C.pt[pb], f"pt{pb}"
                for bb in range(4):
                    base, st_ = blk_geom(di, b0 + bb)
                    for hp in range(2):
                        S.op("pe", L("transpose", out=pt[:, (bb * 2 + hp) * 128:(bb * 2 + hp + 1) * 128],
                                     in_=vT[:, hp, base:base + 127 * st_ + 1:st_], identity=C.ident[:]), reads=["vT", "ident"], writes=[ptk])
                for bb in range(4):
                    S.op("act", L("activation", out=Vc[di][:, b0 + bb, :, 0:64], in_=pt[:, bb * 256:(bb + 1) * 256].rearrange("p (h c) -> p h c", h=4), func=AF.Copy),
                         reads=[ptk], writes=[f"Vc{di}"])
        for h in range(4):
            hp, r0 = h // 2, 64 * (h % 2)
            ob = C.nxt("oacc", 2)
            oa, oak = oacc[ob], f"oacc{ob}"
            for di in range(3):
                d = DILS[di]
                for b0 in range(0, 16, 4):
                    pa = C.nxt("acc", C.NACC)
                    po, pok = C.acc[pa], f"acc{pa}"
                    for pr in range(2):
                        sa_ = C.nxt("acc", C.NACC)
                        sc_, sck = C.acc[sa_], f"acc{sa_}"
                        info = []
                        for qq in range(2):
                            b = b0 + pr * 2 + qq
                            base, st_ = blk_geom(di, b)
                            qsl = qT[r0:r0 + 64, hp, base:base + 127 * st_ + 1:st_]
                            where, pbk = prev_blk(di, b)
                            has_prev = not (where == "prev" and m == 0)
                            if has_prev:
                                if where == "cur":
                                    pbase, pst = blk_geom(di, pbk)
                                    ksl, kk_ = kTc[r0:r0 + 64, hp, pbase:pbase + 127 * pst + 1:pst], kTk
                                    vsl, vk_ = Vc[di][:, pbk, h, :], f"Vc{di}"
                                else:
                                    pbase, pst = blk_geom(di, prev_src_blk(di, pbk))
                                    ksl, kk_ = kTp[r0:r0 + 64, hp, pbase:pbase + 127 * pst + 1:pst], kTpk
                                    vsl, vk_ = Vp[di][:, pbk, h, :], f"Vp{di}"
                                S.op("pe", L("matmul", sc_[:, qq * 256:qq * 256 + 128], lhsT=ksl, rhs=qsl, start=True, stop=True), reads=[kk_, "qT"], writes=[sck])
                            else:
                                vsl, vk_ = None, None
                            S.op("pe", L("matmul", sc_[:, qq * 256 + 128:qq * 256 + 256], lhsT=kTc[r0:r0 + 64, hp, base:base + 127 * st_ + 1:st_], rhs=qsl, start=True, stop=True),
                                 reads=[kTk, "qT"], writes=[sck])
                            info.append((b, has_prev, vsl, vk_))
                        pi = C.nxt("pexp", 2)
                        S.op("act", L("activation", out=pexp[pi][:], in_=sc_[:], func=AF.Exp, scale=0.125), reads=[sck], writes=[f"pexp{pi}"])
                        S.op("pool" if pi == 0 else "dve", L("tensor_tensor", out=pm[pi][:], in0=pexp[pi][:], in1=mask[:], op=ALU.mult), reads=[f"pexp{pi}", "mask"], writes=[f"pm{pi}"])
                        for qq in range(2):
                            b, has_prev, vsl, vk_ = info[qq]
                            col = (pr * 2 + qq) * 128
                            if has_prev:
                                S.op("pe", L("matmul", po[:, col:col + 128], lhsT=vsl, rhs=pm[pi][:, qq * 256:qq * 256 + 128], start=True, stop=False), reads=[vk_, f"pm{pi}"], writes=[pok])
                            S.op("pe", L("matmul", po[:, col:col + 128], lhsT=Vc[di][:, b, h, :], rhs=pm[pi][:, qq * 256 + 128:qq * 256 + 256], start=(not has_prev), stop=True),
                                 reads=[f"Vc{di}", f"pm{pi}"], writes=[pok])
                    if d == 1:
                        dst = oa[:, 128 * b0:128 * b0 + 512]
                        S.op("act", L("activation", out=dst, in_=po[:, :], func=AF.Copy), reads=[pok], writes=[oak])
                    else:
                        if d == 4:
                            dst = oa[:, 128 * b0:128 * b0 + 512].rearrange("p (i r) -> p r i", r=4)
                        else:
                            dst = oa[:, :].rearrange("p (i r) -> p r i", r=16)[:, b0:b0 + 4, :]
                        S.op("dve", L("tensor_tensor", out=dst, in0=po[:, :].rearrange("p (r i) -> p r i", r=4), in1=dst, op=ALU.add), reads=[pok, oak], writes=[oak])
            oo, ook = oout[ob], f"oout{ob}"
            for q4 in range(4):
                S.op("dve", L("reciprocal", out=rden[64:128, :], in_=oa[64:128, q4 * 512:(q4 + 1) * 512]), reads=[oak], writes=["rden"])
                S.op("act", L("activation", out=rden2[:], in_=rden[64:128, :], func=AF.Copy), reads=["rden"], writes=["rden2"])
                S.op("dve", L("tensor_tensor", out=oo[:, q4 * 512:(q4 + 1) * 512], in0=oa[0:64, q4 * 512:(q4 + 1) * 512], in1=rden2[:], op=ALU.mult), reads=[oak, "rden2"], writes=[ook])
            for p0 in range(0, MC, out_piece):
                tk = S.dma("sp", L("dma_start", out=out_ap(slice(h * 64, (h + 1) * 64), m * MC + p0, out_piece), in_=oo[:, p0:p0 + out_piece]), reads=[ook], key="oaT_d" + str(ob))
                toks.append(tk)
        for di in range(3):
            n = NPREV[di]
            sb0 = prev_src_blk(di, 0)
            for jj in range(n):
                S.op("dve", L("tensor_copy", out=Vp[di][:, jj, :, 0:64], in_=Vc[di][:, sb0 + jj, :, 0:64]), reads=[f"Vc{di}"], writes=[f"Vp{di}"])
    return toks[-8:]


C0 = 0.6065306597126334
GN_EPS = 64e-5
PV_MU, PV_W0, PV_A0, PV_KK, PV_KA, PV_RK, PV_LNW, PV_LNB, PV_1MKA, PV_N = 0, 8, 10, 12, 14, 16, 18, 20, 22, 24


def phase_a2(C, SA, x_d, wr_d, g_d, pv_d, w2a2_d, g2_d, cst_d, out_ap):
    nc, S = C.nc, C.S
    NSC = SA // 512
    gmix = load_gfull(C, "gmix", g_d)
    C.epsb = C.sb("epsb", [128, 1], F32)
    S.op("pool", L("memset", C.epsb[:], EPS), writes=["epsb"])
    gnb = C.sb("gnb", [128, 1], F32)
    S.op("pool", L("memset", gnb[:], GN_EPS), writes=["gnb"])
    W = C.sb("Wr", [128, 8, 1024], BF16)
    for c in range(8):
        S.dma("pool", L("dma_start", out=W[:, c, :], in_=wr_d[c * 128:(c + 1) * 128, :]), writes=["W"])
    pv = C.sb("pv", [128, PV_N], F32)
    S.dma("sp", L("dma_start", out=pv[:, 0:22], in_=pv_d[:, 0:22]), writes=["pv"])
    S.op("dve", L("tensor_scalar", out=pv[:, PV_1MKA:PV_1MKA + 2], in0=pv[:, PV_KA:PV_KA + 2], scalar1=-1.0, scalar2=1.0, op0=ALU.mult, op1=ALU.add), reads=["pv"], writes=["pv"])
    w2a2 = C.sb("w2a2", [128, 256], BF16)
    S.dma("pool", L("dma_start", out=w2a2[:], in_=w2a2_d[:, :]), writes=["w2a2"])
    g2 = C.sb("g2", [128, 256], BF16)
    S.dma("pool", L("dma_start", out=g2[:], in_=g2_d[:, :]), writes=["g2"])
    cst = C.sb("cst", [128, 1280], F32)
    S.dma("sp", L("dma_start", out=cst[:], in_=cst_d[:, :]), writes=["cst"])
    mreset = cst[:, 0:512]
    blkones = cst[:, 512:640]
    identf = cst[:, 1216:1280]
    maskA = C.sb("maskA", [128, 512], BF16)
    S.op("dve", L("tensor_copy", out=maskA[:], in_=cst[:, 640:1152]), reads=["cst"], writes=["maskA"])
    maskL = C.sb("maskL", [128, 8, 64], BF16)
    S.op("dve", L("tensor_copy", out=maskL[:], in_=cst[:, 1152:1216].unsqueeze(1).to_broadcast([128, 8, 64])), reads=["cst"], writes=["maskL"])
    identb = C.sb("identb", [128, 4, 64], BF16)
    S.op("dve", L("tensor_copy", out=identb[:], in_=identf.unsqueeze(1).to_broadcast([128, 4, 64])), reads=["cst"], writes=["identb"])

    xt = [C.sb(f"xt{i}", [128, D], F32) for i in range(2)]
    hT = C.sb("hT", [128, 8, 512], BF16)
    colsb = [C.sb(f"colsb{i}", [128, 513], F32) for i in range(8)]
    for i in range(8):
        S.op("pool", L("memset", colsb[i][:, 0:1], 0.0), writes=[f"colsb{i}"])
    xm = [C.sb(f"xm{i}", [128, 512], F32) for i in range(8)]
    tmpd = C.sb("tmpd", [128, 512], F32)
    lo1b = C.sb("lo1b", [128, 512], BF16)
    sg = C.sb("sg", [128, 512], BF16)

    def f32t(n):
        return C.sb(n, [128, 512], F32)
    def b16t(n):
        return C.sb(n, [128, 512], BF16)
    HPF32 = ("lwp", "ai", "gt", "kk", "sc1", "sc2", "kf", "bvec", "cwp", "cwm", "Epos", "Eprev", "Eneg", "yt")
    HPB16 = ("bT", "kTt", "BhT", "KhT", "vb")
    bufs = []
    for hp_ in range(2):
        d = {}
        for n in HPF32:
            d[n] = f32t(f"{n}_{hp_}")
        for n in HPB16:
            d[n] = b16t(f"{n}_{hp_}")
        d["AR"] = C.sb(f"AR_{hp_}", [128, 8, 128], BF16)
        d["AT"] = C.sb(f"AT_{hp_}", [128, 8, 256], BF16)
        d["PTt"] = C.sb(f"PTt_{hp_}", [128, 8, 64], BF16)
        d["Pl"] = [C.sb(f"Pl{i}_{hp_}", [128, 4, 64], BF16) for i in range(2)]
        d["PTl"] = [C.sb(f"PTl{i}_{hp_}", [128, 4, 64], BF16) for i in range(2)]
        d["Z"] = C.sb(f"Z_{hp_}", [128, 8, 64], BF16)
        for n in ("Bh_tok", "Kh_tok", "V_tok"):
            d[n] = C.sb(f"{n}_{hp_}", [128, 8, 64], BF16)
        d["YY"] = C.sb(f"YY_{hp_}", [128, 8, 128], BF16)
        d["XX"] = C.sb(f"XX_{hp_}", [128, 8, 128], BF16)
        d["MN"] = C.sb(f"MN_{hp_}", [128, 8, 128], F32)
        d["QcT"] = C.sb(f"QcT_{hp_}", [128, 8, 64], F32)
        bufs.append(d)
    HPKEYS = set(HPF32) | set(HPB16) | {"AR", "AT", "PTt", "Pl0", "Pl1", "PTl0", "PTl1", "Z", "Bh_tok", "Kh_tok", "V_tok", "YY", "XX", "MN", "QcT"}
    Hs = [C.sb(f"Hs{hp}", [128, 9, 64], F32) for hp in range(2)]
    for hp in range(2):
        S.op("pool", L("memset", Hs[hp][:], 0.0), writes=[f"Hs{hp}"])
    obuf = [C.sb(f"obuf{i}", [128, 512], BF16) for i in range(2)]

    def acc():
        a = C.nxt("acc", C.NACC)
        return C.acc[a], f"acc{a}"

    toks = []
    for sci in range(NSC):
        t0 = sci * 512
        for j in range(4):
            xb = C.nxt("xt", 2)
            S.dma("sp", L("dma_start", out=xt[xb][:], in_=x_d[t0 + j * 128:t0 + (j + 1) * 128, :]), writes=[f"xt{xb}"])
            norm_transpose(C, xt[xb][:], f"xt{xb}", gmix, "gmix", hT[:, :, j * 128:(j + 1) * 128], "hT")
        for ti in range(8):
            A, Ak = acc()
            for c in range(8):
                S.op("pe", L("matmul", A[:, :], lhsT=W[:, c, ti * 128:(ti + 1) * 128], rhs=hT[:, c, :], start=(c == 0), stop=(c == 7)), reads=["W", "hT"], writes=[Ak])
            cb, cbk = colsb[ti], f"colsb{ti}"
            S.op("act", L("activation", out=cb[:, 1:513], in_=A[:], func=AF.Copy), reads=[Ak], writes=[cbk])
            S.op("dve", L("tensor_tensor", out=tmpd[:], in0=cb[:, 0:512], in1=cb[:, 1:513], op=ALU.subtract), reads=[cbk], writes=["tmpd"])
            S.op("dve", L("scalar_tensor_tensor", out=xm[ti][:], in0=tmpd[:], scalar=pv[:, PV_MU + ti:PV_MU + ti + 1], in1=cb[:, 1:513], op0=ALU.mult, op1=ALU.add),
                 reads=["tmpd", "pv", cbk], writes=[f"xm{ti}"])
            S.op("pool", L("tensor_copy", out=cb[:, 0:1], in_=cb[:, 512:513]), reads=[cbk], writes=[cbk])
        S.op("act", L("activation", out=lo1b[0:64, :], in_=xm[6][0:64, :], func=AF.Tanh), reads=["xm6"], writes=["lo1b"])
        S.op("act", L("activation", out=lo1b[64:128, :], in_=xm[6][64:128, :], func=AF.Copy), reads=["xm6"], writes=["lo1b"])
        S.op("act", L("activation", out=sg[:], in_=xm[7][:], func=AF.Sigmoid), reads=["xm7"], writes=["sg"])
        def hp_body(hp):
            B_ = bufs[hp]
            lwp, ai, gt, kk, sc1, sc2, kf, bvec, cwp, cwm, Epos, Eprev, Eneg, yt = [B_[n] for n in HPF32]
            kkn, bonus = kk, lwp
            bT, kTt, BhT, KhT, vb = [B_[n] for n in HPB16]
            AR, AT, PTt, Pl, PTl, Z, Bh_tok, Kh_tok, V_tok, YY, XX, MN, QcT = [B_[n] for n in ("AR", "AT", "PTt", "Pl", "PTl", "Z", "Bh_tok", "Kh_tok", "V_tok", "YY", "XX", "MN", "QcT")]
            km = lambda k: (k + "_" + str(hp)) if k in HPKEYS else k
            def op(eng, fn, reads=(), writes=()):
                return S.op(eng, fn, [km(k) for k in reads], [km(k) for k in writes])
            r_, k_, v_ = xm[hp], xm[2 + hp], xm[4 + hp]
            rk_, kk_, vk_ = f"xm{hp}", f"xm{2 + hp}", f"xm{4 + hp}"
            col = lambda base: pv[:, base + hp:base + hp + 1]
            A, Ak = acc()
            op("pe", L("matmul", A[:, :], lhsT=w2a2[0:64, hp * 128:(hp + 1) * 128], rhs=lo1b[0:64, :], start=True, stop=True), reads=["w2a2", "lo1b"], writes=[Ak])
            op("act", L("activation", out=lwp[:], in_=A[:], func=AF.Sigmoid, bias=col(PV_W0)), reads=[Ak, "pv"], writes=["lwp"])
            A, Ak = acc()
            op("pe", L("matmul", A[:, :], lhsT=w2a2[64:128, hp * 128:(hp + 1) * 128], rhs=lo1b[64:128, :], start=True, stop=True), reads=["w2a2", "lo1b"], writes=[Ak])
            op("act", L("activation", out=ai[:], in_=A[:], func=AF.Sigmoid, bias=col(PV_A0)), reads=[Ak, "pv"], writes=["ai"])
            A, Ak = acc()
            op("pe", L("matmul", A[:, :], lhsT=g2[:, hp * 128:(hp + 1) * 128], rhs=sg[:], start=True, stop=True), reads=["g2", "sg"], writes=[Ak])
            op("act", L("activation", out=gt[:], in_=A[:], func=AF.Copy), reads=[Ak], writes=["gt"])
            yield
            op("dve", L("tensor_scalar", out=kk[:], in0=k_[:], scalar1=col(PV_KK), scalar2=None, op0=ALU.mult), reads=[kk_, "pv"], writes=["kk"])
            op("pool", L("tensor_tensor", out=sc1[:], in0=kk[:], in1=kk[:], op=ALU.mult), reads=["kk"], writes=["sc1"])
            A, Ak = acc()
            op("pe", L("matmul", A[:, :], lhsT=blkones, rhs=sc1[:], start=True, stop=True), reads=["cst", "sc1"], writes=[Ak])
            op("act", L("activation", out=sc2[:], in_=A[:], func=AF.Sqrt), reads=[Ak], writes=["sc2"])
            op("dve", L("tensor_scalar", out=sc2[:], in0=sc2[:], scalar1=1e-12, scalar2=None, op0=ALU.max), reads=["sc2"], writes=["sc2"])
            op("dve", L("reciprocal", out=sc1[:], in_=sc2[:]), reads=["sc2"], writes=["sc1"])
            op("dve", L("tensor_tensor", out=kkn[:], in0=kk[:], in1=sc1[:], op=ALU.mult), reads=["kk", "sc1"], writes=["kk"])
            yield
            op("dve", L("tensor_scalar", out=sc2[:], in0=ai[:], scalar1=col(PV_KA), scalar2=col(PV_1MKA), op0=ALU.mult, op1=ALU.add), reads=["ai", "pv"], writes=["sc2"])
            op("pool", L("tensor_tensor", out=kf[:], in0=sc2[:], in1=k_[:], op=ALU.mult), reads=["sc2", kk_], writes=["kf"])
            op("pool", L("tensor_tensor", out=bvec[:], in0=kkn[:], in1=ai[:], op=ALU.mult), reads=["kk", "ai"], writes=["bvec"])
            yield
            op("dve", L("tensor_tensor_scan", out=cwp[:], data0=mreset, data1=lwp[:], initial=0.0, op0=ALU.mult, op1=ALU.add), reads=["cst", "lwp"], writes=["cwp"])
            op("pool", L("tensor_tensor", out=cwm[:], in0=cwp[:], in1=lwp[:], op=ALU.subtract), reads=["cwp", "lwp"], writes=["cwm"])
            op("act", L("activation", out=Epos[:], in_=cwp[:], func=AF.Exp, scale=-C0), reads=["cwp"], writes=["Epos"])
            op("act", L("activation", out=Eprev[:], in_=cwm[:], func=AF.Exp, scale=-C0), reads=["cwm"], writes=["Eprev"])
            op("act", L("activation", out=Eneg[:], in_=cwp[:], func=AF.Exp, scale=C0), reads=["cwp"], writes=["Eneg"])
            v3 = lambda ap: ap.rearrange("p (c t) -> p c t", t=64)
            op("dve", L("scalar_tensor_tensor", out=AR[:, :, 0:64], in0=v3(kkn[:]), scalar=-1.0, in1=v3(Eprev[:]), op0=ALU.mult, op1=ALU.mult), reads=["kk", "Eprev"], writes=["AR"])
            op("pool", L("tensor_tensor", out=AR[:, :, 64:128], in0=v3(r_[:]), in1=v3(Epos[:]), op=ALU.mult), reads=[rk_, "Epos"], writes=["AR"])
            op("pool", L("tensor_tensor", out=bT[:], in0=bvec[:], in1=Eneg[:], op=ALU.mult), reads=["bvec", "Eneg"], writes=["bT"])
            op("dve", L("tensor_tensor", out=kTt[:], in0=kf[:], in1=Eneg[:], op=ALU.mult), reads=["kf", "Eneg"], writes=["kTt"])
            wcb = v3(Epos[:])[:, :, 63:64].to_broadcast([128, 8, 64])
            op("dve", L("tensor_tensor", out=v3(BhT[:]), in0=v3(bT[:]), in1=wcb, op=ALU.mult), reads=["bT", "Epos"], writes=["BhT"])
            op("pool", L("tensor_tensor", out=v3(KhT[:]), in0=v3(kTt[:]), in1=wcb, op=ALU.mult), reads=["kTt", "Epos"], writes=["KhT"])
            op("act", L("activation", out=vb[:], in_=v_[:], func=AF.Copy), reads=[vk_], writes=["vb"])
            yield
            op("dve", L("tensor_tensor", out=sc1[:], in0=r_[:], in1=kf[:], op=ALU.mult), reads=[rk_, "kf"], writes=["sc1"])
            op("dve", L("tensor_scalar", out=sc1[:], in0=sc1[:], scalar1=col(PV_RK), scalar2=None, op0=ALU.mult), reads=["sc1", "pv"], writes=["sc1"])
            A, Ak = acc()
            op("pe", L("matmul", A[:, :], lhsT=blkones, rhs=sc1[:], start=True, stop=True), reads=["cst", "sc1"], writes=[Ak])
            op("dve", L("tensor_tensor", out=bonus[:], in0=A[:], in1=v_[:], op=ALU.mult), reads=[Ak, vk_], writes=["lwp"])
            yield
            for src, sk, dst, dk, dsl in ((BhT, "BhT", Bh_tok, "Bh_tok", None), (KhT, "KhT", Kh_tok, "Kh_tok", None), (vb, "vb", V_tok, "V_tok", None), (None, "AR", YY, "YY", 1)):
                pb = C.nxt("hn", 2)
                pt, ptk = C.pt[pb], f"pt{pb}"
                for c in range(8):
                    for hd in range(2):
                        r0 = 64 * hd
                        in_ap = src[r0:r0 + 64, c * 64:(c + 1) * 64] if src is not None else AR[r0:r0 + 64, c, 0:64]
                        op("pe", L("transpose", out=pt[r0:r0 + 64, c * 64:(c + 1) * 64], in_=in_ap, identity=C.ident[r0:r0 + 64, r0:r0 + 64]), reads=[sk, "ident"], writes=[ptk])
                o = dst[:] if dsl is None else dst[:, :, 64:128]
                op("act", L("activation", out=o, in_=pt[:, 0:512].rearrange("p (c t) -> p c t", t=64), func=AF.Copy), reads=[ptk], writes=[dk])
                yield
            yield
            for c in range(0, 8, 2):
                A, Ak = acc()
                for cc in range(2):
                    for hd in range(2):
                        r0 = 64 * hd
                        op("pe", L("matmul", A[r0:r0 + 64, cc * 256:cc * 256 + 128], lhsT=bT[r0:r0 + 64, (c + cc) * 64:(c + cc + 1) * 64], rhs=AR[r0:r0 + 64, c + cc, :], start=True, stop=True), reads=["bT", "AR"], writes=[Ak])
                        op("pe", L("matmul", A[r0:r0 + 64, cc * 256 + 128:cc * 256 + 256], lhsT=kTt[r0:r0 + 64, (c + cc) * 64:(c + cc + 1) * 64], rhs=AR[r0:r0 + 64, c + cc, :], start=True, stop=True), reads=["kTt", "AR"], writes=[Ak])
                op("dve", L("tensor_tensor", out=AT[:, c:c + 2, :], in0=A[:].rearrange("p (c t) -> p c t", c=2), in1=maskA[:].rearrange("p (c t) -> p c t", c=2), op=ALU.mult), reads=[Ak, "maskA"], writes=["AT"])
            A, Ak = acc()
            for c in range(8):
                for hd in range(2):
                    r0 = 64 * hd
                    op("pe", L("matmul", A[r0:r0 + 64, c * 64:(c + 1) * 64], lhsT=AR[r0:r0 + 64, c, 0:64], rhs=bT[r0:r0 + 64, c * 64:(c + 1) * 64], start=True, stop=True), reads=["AR", "bT"], writes=[Ak])
            op("dve", L("tensor_tensor", out=PTt[:], in0=A[:].rearrange("p (c t) -> p c t", t=64), in1=maskL[:], op=ALU.mult), reads=[Ak, "maskL"], writes=["PTt"])
            yield
            for half in range(2):
                c0 = half * 4
                Pv, Pk = (lambda c: AT[:, c0 + c, 0:64]), "AT"
                PTv, PTk = (lambda c: PTt[:, c0 + c, :]), "PTt"
                Zh = Z[:, c0:c0 + 4, :]
                op("pool", L("tensor_tensor", out=Zh, in0=AT[:, c0:c0 + 4, 0:64], in1=identb[:], op=ALU.add), reads=["AT", "identb"], writes=["Z"])
                for lvl in range(6):
                    AP_, APk = acc()
                    if lvl > 0:
                        AZ, AZk = acc()
                    for c in range(4):
                        for hd in range(2):
                            r0 = 64 * hd
                            p_, pt_ = Pv(c)[r0:r0 + 64], PTv(c)[r0:r0 + 64]
                            if lvl > 0:
                                op("pe", L("matmul", AZ[r0:r0 + 64, c * 64:(c + 1) * 64], lhsT=pt_, rhs=Z[r0:r0 + 64, c0 + c, :], start=True, stop=True), reads=[PTk, "Z"], writes=[AZk])
                            if lvl < 5:
                                op("pe", L("matmul", AP_[r0:r0 + 64, c * 128:c * 128 + 64], lhsT=pt_, rhs=p_, start=True, stop=True), reads=[PTk, Pk], writes=[APk])
                                op("pe", L("matmul", AP_[r0:r0 + 64, c * 128 + 64:c * 128 + 128], lhsT=p_, rhs=pt_, start=True, stop=True), reads=[PTk, Pk], writes=[APk])
                    if lvl > 0:
                        op("dve", L("tensor_tensor", out=Zh, in0=AZ[:, 0:256].rearrange("p (c t) -> p c t", t=64), in1=Zh, op=ALU.add), reads=[AZk, "Z"], writes=["Z"])
                    if lvl < 5:
                        nb = lvl % 2
                        op("act", L("activation", out=Pl[nb][:], in_=AP_[:].rearrange("p (c u t) -> p c u t", c=4, u=2)[:, :, 0, :], func=AF.Copy), reads=[APk], writes=[f"Pl{nb}"])
                        op("act", L("activation", out=PTl[nb][:], in_=AP_[:].rearrange("p (c u t) -> p c u t", c=4, u=2)[:, :, 1, :], func=AF.Copy), reads=[APk], writes=[f"PTl{nb}"])
                        Pv, Pk = (lambda c, nb=nb: Pl[nb][:, c, :]), f"Pl{nb}"
                        PTv, PTk = (lambda c, nb=nb: PTl[nb][:, c, :]), f"PTl{nb}"
                    yield
            yield
            A, Ak = acc()
            for c in range(8):
                for hd in range(2):
                    r0 = 64 * hd
                    op("pe", L("matmul", A[r0:r0 + 64, c * 64:(c + 1) * 64], lhsT=AT[r0:r0 + 64, c, 128:192], rhs=V_tok[r0:r0 + 64, c, :], start=True, stop=True), reads=["AT", "V_tok"], writes=[Ak])
            op("act", L("activation", out=YY[:, :, 0:64], in_=A[:].rearrange("p (c t) -> p c t", t=64), func=AF.Copy), reads=[Ak], writes=["YY"])
            yield
            for c in range(0, 8, 4):
                A, Ak = acc()
                for cc in range(4):
                    for hd in range(2):
                        r0 = 64 * hd
                        op("pe", L("matmul", A[r0:r0 + 64, cc * 128:(cc + 1) * 128], lhsT=Z[r0:r0 + 64, c + cc, :], rhs=YY[r0:r0 + 64, c + cc, :], start=True, stop=True), reads=["Z", "YY"], writes=[Ak])
                op("dve", L("tensor_copy", out=XX[:, c:c + 4, :], in_=A[:].rearrange("p (c t) -> p c t", t=128)), reads=[Ak], writes=["XX"])
            yield
            for c in range(0, 8, 4):
                A, Ak = acc()
                for cc in range(4):
                    for hd in range(2):
                        r0 = 64 * hd
                        ci = c + cc
                        op("pe", L("matmul", A[r0:r0 + 64, cc * 128:cc * 128 + 64], lhsT=XX[r0:r0 + 64, ci, 64:128], rhs=Bh_tok[r0:r0 + 64, ci, :], start=True, stop=True), reads=["XX", "Bh_tok"], writes=[Ak])
                        op("pe", L("matmul", A[r0:r0 + 64, cc * 128 + 64:cc * 128 + 128], lhsT=Bh_tok[r0:r0 + 64, ci, :], rhs=XX[r0:r0 + 64, ci, 0:64], start=True, stop=False), reads=["XX", "Bh_tok"], writes=[Ak])
                        op("pe", L("matmul", A[r0:r0 + 64, cc * 128 + 64:cc * 128 + 128], lhsT=Kh_tok[r0:r0 + 64, ci, :], rhs=V_tok[r0:r0 + 64, ci, :], start=False, stop=True), reads=["Kh_tok", "V_tok"], writes=[Ak])
                op("act", L("activation", out=MN[:, c:c + 4, :], in_=A[:].rearrange("p (c t) -> p c t", t=128), func=AF.Copy), reads=[Ak], writes=["MN"])
            for c in range(8):
                op("dve", L("scalar_tensor_tensor", out=MN[:, c, 0:64], in0=identf, scalar=Epos[:, c * 64 + 63:c * 64 + 64], in1=MN[:, c, 0:64], op0=ALU.mult, op1=ALU.add), reads=["cst", "Epos", "MN"], writes=["MN"])
            yield
            A, Ak = acc()
            for c in range(8):
                for hd in range(2):
                    r0 = 64 * hd
                    op("pe", L("matmul", A[r0:r0 + 64, c * 64:(c + 1) * 64], lhsT=XX[r0:r0 + 64, c, 64:128], rhs=AT[r0:r0 + 64, c, 64:128], start=True, stop=True), reads=["XX", "AT"], writes=[Ak])
            op("dve", L("tensor_tensor", out=QcT[:], in0=A[:].rearrange("p (c t) -> p c t", t=64), in1=AR[:, :, 64:128], op=ALU.add), reads=[Ak, "AR"], writes=["QcT"])
            yield
            H, Hk = Hs[hp], f"Hs{hp}"
            for c in range(8):
                A, Ak = acc()
                for hd in range(2):
                    r0 = 64 * hd
                    op("pe", L("matmul", A[r0:r0 + 64, 0:64], lhsT=MN[r0:r0 + 64, c, 0:64], rhs=H[r0:r0 + 64, c, :], start=True, stop=True), reads=["MN", Hk], writes=[Ak])
                op("dve", L("tensor_tensor", out=H[:, c + 1, :], in0=A[:, 0:64], in1=MN[:, c, 64:128], op=ALU.add), reads=[Ak, "MN"], writes=[Hk])
                yield
            yield
            A, Ak = acc()
            for c in range(8):
                for hd in range(2):
                    r0 = 64 * hd
                    o = A[r0:r0 + 64, c * 64:(c + 1) * 64]
                    op("pe", L("matmul", o, lhsT=H[r0:r0 + 64, c, :], rhs=QcT[r0:r0 + 64, c, :], start=True, stop=False), reads=[Hk, "QcT"], writes=[Ak])
                    op("pe", L("matmul", o, lhsT=XX[r0:r0 + 64, c, 0:64], rhs=AT[r0:r0 + 64, c, 64:128], start=False, stop=False), reads=["XX", "AT"], writes=[Ak])
                    op("pe", L("matmul", o, lhsT=V_tok[r0:r0 + 64, c, :], rhs=AT[r0:r0 + 64, c, 192:256], start=False, stop=True), reads=["V_tok", "AT"], writes=[Ak])
            op("act", L("activation", out=yt[:], in_=A[:], func=AF.Copy), reads=[Ak], writes=["yt"])
            op("pool", L("tensor_copy", out=H[:, 0, :], in_=H[:, 8, :]), reads=[Hk], writes=[Hk])
            yield
            op("pool", L("tensor_tensor", out=sc1[:], in0=yt[:], in1=yt[:], op=ALU.mult), reads=["yt"], writes=["sc1"])
            A1, A1k = acc()
            op("pe", L("matmul", A1[:, :], lhsT=blkones, rhs=yt[:], start=True, stop=True), reads=["cst", "yt"], writes=[A1k])
            A2, A2k = acc()
            op("pe", L("matmul", A2[:, :], lhsT=blkones, rhs=sc1[:], start=True, stop=True), reads=["cst", "sc1"], writes=[A2k])
            op("act", L("activation", out=sc2[:], in_=A1[:], func=AF.Copy, scale=1.0 / 64), reads=[A1k], writes=["sc2"])
            op("pool", L("tensor_tensor", out=sc1[:], in0=sc2[:], in1=sc2[:], op=ALU.mult), reads=["sc2"], writes=["sc1"])
            op("dve", L("scalar_tensor_tensor", out=sc1[:], in0=A2[:], scalar=1.0 / 64, in1=sc1[:], op0=ALU.mult, op1=ALU.subtract), reads=[A2k, "sc1"], writes=["sc1"])
            op("act", L("activation", out=sc1[:], in_=sc1[:], func=AF.Sqrt, bias=gnb[:, 0:1]), reads=["sc1", "gnb"], writes=["sc1"])
            op("dve", L("reciprocal", out=cwm[:], in_=sc1[:]), reads=["sc1"], writes=["cwm"])
            op("pool", L("tensor_tensor", out=yt[:], in0=yt[:], in1=sc2[:], op=ALU.subtract), reads=["yt", "sc2"], writes=["yt"])
            op("dve", L("tensor_tensor", out=yt[:], in0=yt[:], in1=cwm[:], op=ALU.mult), reads=["yt", "cwm"], writes=["yt"])
            op("dve", L("tensor_scalar", out=yt[:], in0=yt[:], scalar1=col(PV_LNW), scalar2=col(PV_LNB), op0=ALU.mult, op1=ALU.add), reads=["yt", "pv"], writes=["yt"])
            op("pool", L("tensor_tensor", out=yt[:], in0=yt[:], in1=bonus[:], op=ALU.add), reads=["yt", "lwp"], writes=["yt"])
            ob = C.nxt("obuf", 2)
            op("pool", L("tensor_tensor", out=obuf[ob][:], in0=yt[:], in1=gt[:], op=ALU.mult), reads=["yt", "gt"], writes=[f"obuf{ob}"])
            tk = S.dma("sp", L("dma_start", out=out_ap(slice(hp * 128, (hp + 1) * 128), t0, 512), in_=obuf[ob][:]), reads=[f"obuf{ob}"], key=f"obT_d{ob}")
            toks.append(tk)
        gens = [hp_body(0), hp_body(1)]
        while gens:
            for g_ in list(gens):
                try:
                    next(g_)
                except StopIteration:
                    gens.remove(g_)
    return toks[-2:]


import contextlib

def attn_consts(SA):
    half = 8
    inv = (500000.0 ** (-np.arange(half, dtype=np.float32) * np.float32(2.0 / 16))).astype(np.float32)
    ang = (np.arange(SA, dtype=np.float32)[None, :] * inv[:, None]).astype(np.float32)
    cosF = np.ones((128, SA), np.float32); sinF = np.zeros((128, SA), np.float32)
    rperm = np.zeros((128, 128), np.float32)
    for hb in (0, 64):
        cosF[hb:hb + 8] = np.cos(ang); cosF[hb + 8:hb + 16] = np.cos(ang)
        sinF[hb:hb + 8] = -np.sin(ang); sinF[hb + 8:hb + 16] = np.sin(ang)
        for i in range(8):
            rperm[hb + i + 8, hb + i] = 1.0
            rperm[hb + i, hb + i + 8] = 1.0
    j = np.arange(128)[:, None]; i = np.arange(128)[None, :]
    mprev = (j >= i).astype(np.float32); mcur = (j <= i).astype(np.float32)
    mask = np.ascontiguousarray(np.concatenate([mprev, mcur, mprev, mcur], 1))
    return cosF, sinF, rperm, mask

def rwkv_consts():
    p = np.arange(128)
    mreset = np.ones((128, 512), np.float32); mreset[:, ::64] = 0.0
    blk = (p[:, None] // 64 == p[None, :] // 64).astype(np.float32)
    s = (p % 64)[:, None]; t = np.arange(64)[None, :]
    MU = (s < t).astype(np.float32); MUI = (s <= t).astype(np.float32)
    maskA = np.concatenate([MU, MUI, MU, MUI, MU, MUI, MU, MUI], 1)
    ML = (t < s).astype(np.float32)
    identf = (s == t).astype(np.float32)
    return np.ascontiguousarray(np.concatenate([mreset, blk, maskA, ML, identf], 1))

def rwkv_inputs(inp, hg):
    w_in = inp["w_in"][0]
    cs = slice(hg * 256, (hg + 1) * 256)
    sh = 3072
    cols = np.concatenate([np.arange(sh + hg * 256, sh + hg * 256 + 256), np.arange(sh + 1024 + hg * 256, sh + 1024 + hg * 256 + 256),
                           np.arange(sh + 2048 + hg * 256, sh + 2048 + hg * 256 + 256), np.arange(sh + 3072, sh + 3072 + 256)])
    wr = np.ascontiguousarray(w_in[:, cols])
    mu = inp["shift_mu"][0][cols - sh]
    pv = np.zeros((128, 22), np.float32)
    pv[:, 0:8] = mu.reshape(8, 128).T
    for k, name in ((8, "decay_w0"), (10, "iclr_a0"), (12, "k_k"), (14, "k_a"), (16, "r_k"), (18, "ln_x_w"), (20, "ln_x_b")):
        v = inp[name][0].reshape(-1)[cs]
        pv[:, k:k + 2] = v.reshape(2, 128).T
    w2a2 = np.ascontiguousarray(np.concatenate([inp["decay_w2"][0][:, cs], inp["iclr_a2"][0][:, cs]], 0))
    g2 = np.ascontiguousarray(inp["gate_g2"][0][:, cs])
    return {"wr": wr, "pv": pv, "w2a2": w2a2, "g2": g2}

def attn_inputs(inp, hg):
    w_in = inp["w_in"][0]
    cols = np.concatenate([w_in[:, hg * 256:(hg + 1) * 256], w_in[:, 1024 + hg * 256:1024 + (hg + 1) * 256], w_in[:, 2048 + hg * 256:2048 + (hg + 1) * 256]], 1)
    return {"wqkv": np.ascontiguousarray(cols)}


def build_p1(SA):
    nc = bass.Bass("TRN2", target_bir_lowering=False)
    di = lambda n, s, d=F32: nc.dram_tensor(n, s, d, kind="ExternalInput").ap()
    x_d = di("x", [SA, 1024]); ident = di("ident", [128, 128]); g = di("g_mix", [1024])
    wqkv = di("wqkv", [1024, 768]); cosF = di("cosF", [128, SA]); sinF = di("sinF", [128, SA]); rperm = di("rperm", [128, 128]); mask = di("mask", [128, 512])
    wr = di("wr", [1024, 1024]); pv = di("pv", [128, 22]); w2a2 = di("w2a2", [128, 256]); g2 = di("g2", [128, 256]); cst = di("cst", [128, 1280])
    oaT = nc.dram_tensor("oaT", [256, SA], BF16, kind="ExternalOutput").ap()
    obT = nc.dram_tensor("obT", [256, SA], BF16, kind="ExternalOutput").ap()
    with contextlib.ExitStack() as st0:
        S = Sched(nc, st0)
        with contextlib.ExitStack() as st:
            C = Ctx(nc, S, st, "a1_")
            setup_common(C, ident)
            toks1 = phase_a1(C, SA, x_d, wqkv, g, cosF, sinF, rperm, mask, (lambda rows, g0, n: oaT[rows, g0:g0 + n]))
            S.run()
        S.barrier()
        with contextlib.ExitStack() as st:
            C = Ctx(nc, S, st, "a2_")
            setup_common(C, ident)
            toks2 = phase_a2(C, SA, x_d, wr, g, pv, w2a2, g2, cst, (lambda rows, g0, n: obT[rows, g0:g0 + n]))
            S.finish(list(toks1) + list(toks2))
            S.run()
    return nc


def build_p2(TB):
    nc = bass.Bass("TRN2", target_bir_lowering=False)
    di = lambda n, s, d=F32: nc.dram_tensor(n, s, d, kind="ExternalInput").ap()
    x_d = di("x", [TB, 1024]); oa = di("oaT", [1024, TB], BF16); ob = di("obT", [1024, TB], BF16)
    ident = di("ident", [128, 128])
    w = {"wg": di("wg", [1024, 2048]), "pa": di("pa", [1024, 1024]), "pb": di("pb", [1024, 1024]), "wo": di("wo", [1024, 1024]),
         "fg": di("fg", [1024, 2816]), "fu": di("fu", [1024, 2816]), "fd": di("fd", [2816, 1024]),
         "g_mix": di("g_mix", [1024]), "g_ffn": di("g_ffn", [1024]), "g_fin": di("g_fin", [1024])}
    out_d = nc.dram_tensor("out", [TB, 1024], F32, kind="ExternalOutput").ap()
    with contextlib.ExitStack() as st:
        S = Sched(nc, st)
        C = Ctx(nc, S, st)
        setup_common(C, ident)
        w16 = convert_weights(C, w)
        toks = phase_b(C, TB, x_d, oa, ob, out_d, w, w16)
        S.finish(toks)
        S.run()
    return nc


def build_fused(SA, TB):
    nc = bass.Bass("TRN2", target_bir_lowering=False)
    di = lambda n, s, d=F32: nc.dram_tensor(n, s, d, kind="ExternalInput").ap()
    x_d = di("x", [SA, 1024]); ident = di("ident", [128, 128]); g = di("g_mix", [1024])
    wqkv = di("wqkv", [1024, 768]); cosF = di("cosF", [128, SA]); sinF = di("sinF", [128, SA]); rperm = di("rperm", [128, 128]); mask = di("mask", [128, 512])
    wr = di("wr", [1024, 1024]); pv = di("pv", [128, 22]); w2a2 = di("w2a2", [128, 256]); g2 = di("g2", [128, 256]); cst = di("cst", [128, 1280])
    qoff = di("qoff", [1, 1], mybir.dt.int32)
    w = {"wg": di("wg", [1024, 2048]), "pa": di("pa", [1024, 1024]), "pb": di("pb", [1024, 1024]), "wo": di("wo", [1024, 1024]),
         "fg": di("fg", [1024, 2816]), "fu": di("fu", [1024, 2816]), "fd": di("fd", [2816, 1024]),
         "g_mix": g, "g_ffn": di("g_ffn", [1024]), "g_fin": di("g_fin", [1024])}
    out_d = nc.dram_tensor("out", [TB, 1024], F32, kind="ExternalOutput").ap()
    NG = 4
    CH = min(2048, TB)
    NCH = SA // CH
    send_a = nc.dram_tensor("send_a", [NCH, 256, CH], BF16, kind="Internal").ap()
    send_b = nc.dram_tensor("send_b", [NCH, 256, CH], BF16, kind="Internal").ap()
    recv_a = nc.dram_tensor("recv_a", [NCH, NG * 256, CH], BF16, kind="Internal").ap()
    recv_b = nc.dram_tensor("recv_b", [NCH, NG * 256, CH], BF16, kind="Internal").ap()
    qm = di("qm", [1, 1], mybir.dt.int32)
    groups = [[0, 1, 2, 3], [4, 5, 6, 7]]
    def chunk_ap(t):
        return lambda rows, g0, n: t[g0 // CH, rows, g0 % CH:g0 % CH + n]
    with contextlib.ExitStack() as st0:
        S = Sched(nc, st0)
        qreg = st0.enter_context(nc.sync.register("qreg"))
        mreg = st0.enter_context(nc.sync.register("mreg"))
        C0 = Ctx(nc, S, st0, "w_")
        w16 = convert_weights(C0, w)
        with contextlib.ExitStack() as st:
            C = Ctx(nc, S, st, "a1_")
            setup_common(C, ident)
            phase_a1(C, SA, x_d, wqkv, g, cosF, sinF, rperm, mask, chunk_ap(send_a), out_piece=min(CH, 2048))
            S.run()
        S.barrier()
        for m in range(NCH):
            S.cc(L("collective_compute", "AllGather", ALU.bypass, replica_groups=groups, ins=[send_a[m]], outs=[recv_a[m]]), key="cca")
        with contextlib.ExitStack() as st:
            C = Ctx(nc, S, st, "a2_")
            setup_common(C, ident)
            phase_a2(C, SA, x_d, wr, g, pv, w2a2, g2, cst, chunk_ap(send_b))
            S.run()
        S.barrier()
        for m in range(NCH):
            S.cc(L("collective_compute", "AllGather", ALU.bypass, replica_groups=groups, ins=[send_b[m]], outs=[recv_b[m]]), key="ccb")
        S.barrier()
        with contextlib.ExitStack() as st:
            C = Ctx(nc, S, st, "b_")
            C.dyn = {"reg": qreg, "qoff": qoff, "mreg": mreg, "qm": qm, "CH": CH}
            setup_common(C, ident)
            toks = phase_b(C, TB, x_d, recv_a, recv_b, out_d, w, w16)
            S.finish(toks)
            S.run()
    return nc


def kernel_unfused(**inputs):
    inp = {k: np.asarray(v) for k, v in inputs.items()}
    x = inp["x"]
    NBATCH, SEQ, _ = x.shape
    NQ = 8 // NBATCH
    TB = SEQ // NQ
    cosF, sinF, rperm, mask = attn_consts(SEQ)
    cst = rwkv_consts()
    ident = np.eye(128, dtype=np.float32)
    nc1 = build_p1(SEQ)
    maps1 = []
    for c in range(8):
        b, hg = c // NQ, c % NQ
        m = {"x": np.ascontiguousarray(x[b]), "ident": ident, "g_mix": inp["norm_mix_g"][0], "cosF": cosF, "sinF": sinF, "rperm": rperm, "mask": mask, "cst": cst}
        m.update(attn_inputs(inp, hg))
        m.update(rwkv_inputs(inp, hg))
        maps1.append(m)
    res1 = run_bass_kernel_spmd(nc1, maps1, core_ids=list(range(8)))
    nc2 = build_p2(TB)
    wg = np.ascontiguousarray(inp["w_in"][0][:, 8448 - 2048:])
    maps2 = []
    for c in range(8):
        b, q = c // NQ, c % NQ
        oa = np.concatenate([res1.results[b * NQ + hg]["oaT"][:, q * TB:(q + 1) * TB] for hg in range(NQ)], 0)
        ob = np.concatenate([res1.results[b * NQ + hg]["obT"][:, q * TB:(q + 1) * TB] for hg in range(NQ)], 0)
        maps2.append({"x": np.ascontiguousarray(x[b, q * TB:(q + 1) * TB]), "oaT": np.ascontiguousarray(oa), "obT": np.ascontiguousarray(ob), "ident": ident,
                      "wg": wg, "pa": inp["proj_attn"][0], "pb": inp["proj_rwkv"][0], "wo": inp["w_out"][0],
                      "fg": inp["ffn_w_gate"][0], "fu": inp["ffn_w_up"][0], "fd": inp["ffn_w_down"][0],
                      "g_mix": inp["norm_mix_g"][0], "g_ffn": inp["norm_ffn_g"][0], "g_fin": inp["norm_final_g"]})
    res2 = run_bass_kernel_spmd(nc2, maps2, core_ids=list(range(8)))
    out = np.zeros((NBATCH, SEQ, 1024), np.float32)
    for c in range(8):
        b, q = c // NQ, c % NQ
        out[b, q * TB:(q + 1) * TB] = res2.results[c]["out"]
    return out


def kernel(**inputs):
    inp = {k: np.asarray(v) for k, v in inputs.items()}
    x = inp["x"]
    NBATCH, SEQ, _ = x.shape
    NQ = 8 // NBATCH
    TB = SEQ // NQ
    cosF, sinF, rperm, mask = attn_consts(SEQ)
    cst = rwkv_consts()
    ident = np.eye(128, dtype=np.float32)
    nc = build_fused(SEQ, TB)
    wg = np.ascontiguousarray(inp["w_in"][0][:, 8448 - 2048:])
    maps = []
    for c in range(8):
        b, q = c // NQ, c % NQ
        m = {"x": np.ascontiguousarray(x[b]), "ident": ident, "g_mix": inp["norm_mix_g"][0], "cosF": cosF, "sinF": sinF, "rperm": rperm, "mask": mask, "cst": cst,
             "qoff": np.array([[q * TB]], np.int32), "qm": np.array([[q * (TB // min(2048, TB))]], np.int32),
             "wg": wg, "pa": inp["proj_attn"][0], "pb": inp["proj_rwkv"][0], "wo": inp["w_out"][0],
             "fg": inp["ffn_w_gate"][0], "fu": inp["ffn_w_up"][0], "fd": inp["ffn_w_down"][0],
             "g_ffn": inp["norm_ffn_g"][0], "g_fin": inp["norm_final_g"]}
        m.update(attn_inputs(inp, q))
        m.update(rwkv_inputs(inp, q))
        maps.append(m)
    res = run_bass_kernel_spmd(nc, maps, core_ids=list(range(8)))
    out = np.zeros((NBATCH, SEQ, 1024), np.float32)
    for c in range(8):
        b, q = c // NQ, c % NQ
        out[b, q * TB:(q + 1) * TB] = res.results[c]["out"]
    return out
```

```python
import os
import numpy as np
import concourse.bass as bass
import concourse.mybir as mybir
from concourse.bass_utils import run_bass_kernel_spmd

F32 = mybir.dt.float32
BF16 = mybir.dt.bfloat16
AF = mybir.ActivationFunctionType
ALU = mybir.AluOpType
AX = mybir.AxisListType

class Sched:
    COMPUTE = ("pe", "act", "dve", "pool")
    ALLQ = ("pe", "act", "dve", "pool", "sp")

    def __init__(self, nc, stack):
        self.nc = nc
        self.stack = stack
        self.ops = {e: [] for e in self.ALLQ}
        self.cnt = {e: 0 for e in self.COMPUTE}
        self.esem = {e: stack.enter_context(nc.semaphore("prog_" + e)) for e in self.COMPUTE}
        self.known = {e: {} for e in self.ALLQ}
        self.bufs = {}
        self.dsem = {}
        self.dcnt = {}
        self.final = []

    def _buf(self, k):
        b = self.bufs.get(k)
        if b is None:
            b = self.bufs[k] = {"w": [], "r": []}
        return b

    def _dma_sem(self, k):
        if k not in self.dsem:
            self.dsem[k] = self.stack.enter_context(self.nc.semaphore("d_" + str(len(self.dsem))))
            self.dcnt[k] = 0
        return self.dsem[k]

    def _emit(self, eng, reads, writes, fn, tok_fn):
        waits = {}
        def need(tok):
            s, v, src = tok
            if src == eng and eng == "pe":
                return
            if waits.get(s, (0,))[0] < v:
                waits[s] = (v, src)
        for k in reads:
            for t in self._buf(k)["w"]:
                need(t)
        for k in writes:
            b = self._buf(k)
            for t in b["w"]:
                need(t)
            for t in b["r"]:
                need(t)
        wl = []
        kn = self.known[eng]
        for s, (v, src) in waits.items():
            if kn.get(id(s), 0) >= v:
                continue
            kn[id(s)] = v
            wl.append((s, v))
        tok = tok_fn()
        self.ops[eng].append((wl, fn, tok))
        for k in reads:
            b = self._buf(k)
            b["r"] = [t for t in b["r"] if not (t[2] == eng and eng in self.COMPUTE)] + [tok]
        for k in writes:
            b = self._buf(k)
            b["w"] = [tok]
            b["r"] = []
        return tok

    def op(self, eng, fn, reads=(), writes=()):
        def tok_fn():
            self.cnt[eng] += 1
            return (self.esem[eng], self.cnt[eng], eng)
        return self._emit(eng, reads, writes, fn, tok_fn)

    def dma(self, q, fn, reads=(), writes=(), key=None):
        key = key if key is not None else (writes[0] if writes else reads[0])
        sem = self._dma_sem(("dma", key))
        def tok_fn():
            self.dcnt[("dma", key)] += 16
            return (sem, self.dcnt[("dma", key)], "dma")
        return self._emit(q, reads, writes, fn, tok_fn)

    def cc(self, fn, reads=(), writes=(), key="cc"):
        sem = self._dma_sem(("cc", key))
        def tok_fn():
            self.dcnt[("cc", key)] += 1
            return (sem, self.dcnt[("cc", key)], "cc")
        return self._emit("pool", reads, writes, fn, tok_fn)

    def finish(self, toks):
        self.final = list(toks)

    def barrier(self):
        for e in self.ALLQ:
            wl = []
            kn = self.known[e]
            for e2 in self.COMPUTE:
                if e2 == e or self.cnt[e2] == 0:
                    continue
                if kn.get(id(self.esem[e2]), 0) < self.cnt[e2]:
                    kn[id(self.esem[e2])] = self.cnt[e2]
                    wl.append((self.esem[e2], self.cnt[e2]))
            for k, sem in self.dsem.items():
                v = self.dcnt[k]
                if v and kn.get(id(sem), 0) < v:
                    kn[id(sem)] = v
                    wl.append((sem, v))
            self.ops[e].append((wl, None, None))
        self.bufs = {}

    def run(self):
        nc = self.nc
        engmap = {"pe": "tensor", "act": "scalar", "dve": "vector", "pool": "gpsimd", "sp": "sync"}
        with nc.Block() as block:
            for e in self.ALLQ:
                ops = self.ops[e]
                fin = self.final if e == "sp" else []
                if not ops and not fin:
                    continue
                def body(engine, ops=ops, fin=fin):
                    for wl, fn, tok in ops:
                        for s, v in wl:
                            engine.wait_ge(s, v)
                        if fn is not None:
                            fn(engine).then_inc(tok[0], 16 if tok[2] == "dma" else 1)
                    for s, v, _ in fin:
                        engine.wait_ge(s, v)
                getattr(block, engmap[e])(body)
        self.ops = {e: [] for e in self.ALLQ}
        self.final = []


D = 1024
DFF = 2816
NFT = DFF // 128
EPS = 1e-6

def L(f, *a, **k):
    return lambda e: getattr(e, f)(*a, **k)

class Ctx:
    def __init__(self, nc, S, st, pfx=""):
        self.nc, self.S, self.st, self.pfx = nc, S, st, pfx
        self.dyn = None
        self.n = 0
        self.rr = {}
    def sb(self, name, shape, dt):
        return self.st.enter_context(self.nc.sbuf_tensor("s_" + self.pfx + name, shape, dt))
    def ps(self, name, shape, dt):
        return self.st.enter_context(self.nc.psum_tensor("p_" + self.pfx + name, shape, dt))
    def dram(self, name, shape, dt):
        return self.nc.dram_tensor(name, shape, dt, kind="Internal").ap()
    def nxt(self, name, n):
        v = self.rr.get(name, 0)
        self.rr[name] = v + 1
        return v % n


def setup_common(C, ident_d):
    S = C.S
    C.ident = C.sb("ident", [128, 128], BF16)
    S.dma("pool", L("dma_start", out=C.ident[:], in_=ident_d[:, :]), writes=["ident"])
    C.stat = C.sb("stat", [128, 64], F32)
    C.NACC = 6
    C.acc = [C.ps(f"acc{i}", [128, 512], F32) for i in range(C.NACC)]
    C.pt = [C.ps(f"pt{i}", [128, 1024], BF16) for i in range(2)]
    C.hn = [C.sb(f"hn{i}", [128, 1024], BF16) for i in range(2)]
    C.junk = C.sb("junk", [128, 1024], BF16)


def load_gfull(C, name, g_d):
    S = C.S
    gcol = C.sb(name + "_col", [128, 8], F32)
    gfull = C.sb(name, [128, 8, 128], F32)
    S.dma("sp", L("dma_start", out=gcol[:], in_=g_d.rearrange("(c p) -> p c", p=128), allow_slow_non_contiguous=True), writes=[name + "_col"])
    S.op("dve", L("tensor_copy", out=gfull[:], in_=gcol[:].unsqueeze(2).to_broadcast([128, 8, 128])), reads=[name + "_col"], writes=[name])
    return gfull


def norm_transpose(C, x_ap, xkey, gfull, gkey, hT_ap, hkey):
    S = C.S
    i = C.nxt("stat", 16)
    ss, rs, rstd = C.stat[:, 3 * i:3 * i + 1], C.stat[:, 3 * i + 1:3 * i + 2], C.stat[:, 3 * i + 2:3 * i + 3]
    sk = f"stat{i}"
    b = C.nxt("hn", 2)
    hn, hnk = C.hn[b], f"hn{b}"
    pt, ptk = C.pt[b], f"pt{b}"
    S.op("act", L("activation", out=C.junk[:], in_=x_ap, func=AF.Square, accum_out=ss), reads=[xkey], writes=["junk", sk])
    S.op("act", L("activation", out=rs, in_=ss, func=AF.Sqrt, scale=1.0 / D, bias=C.epsb[:, 0:1]), reads=[sk, "epsb"], writes=[sk])
    S.op("dve", L("reciprocal", out=rstd, in_=rs), reads=[sk], writes=[sk])
    S.op("act", L("activation", out=hn[:], in_=x_ap, func=AF.Copy, scale=rstd), reads=[xkey, sk], writes=[hnk])
    for c in range(8):
        S.op("pe", L("transpose", out=pt[:, c * 128:(c + 1) * 128], in_=hn[:, c * 128:(c + 1) * 128], identity=C.ident[:]), reads=[hnk, "ident"], writes=[ptk])
    S.op("dve", L("tensor_tensor", out=hT_ap, in0=pt[:].rearrange("p (c t) -> p c t", c=8), in1=gfull[:], op=ALU.mult), reads=[ptk, gkey], writes=[hkey])
    return rstd, sk


def convert_weights(C, w):
    S = C.S
    w16 = {}
    for k, N in (("wg", 2048), ("pa", D), ("pb", D), ("wo", D), ("fg", DFF), ("fu", DFF)):
        nblk = -(-N // 512)
        w16[k] = C.dram("b16_" + k, [nblk, 128, 8, 512], BF16)
        for nb in range(nblk):
            wc = min(512, N - nb * 512)
            for c in range(8):
                S.dma("pool", L("dma_start", out=w16[k][nb, :, c, 0:wc], in_=w[k][c * 128:(c + 1) * 128, nb * 512:nb * 512 + wc]), writes=["w16_" + k], key="w16_" + k)
    w16["fd"] = C.dram("b16_fd", [2, 128, NFT, 512], BF16)
    for n2 in range(2):
        for ft in range(NFT):
            S.dma("pool", L("dma_start", out=w16["fd"][n2, :, ft, :], in_=w["fd"][ft * 128:(ft + 1) * 128, n2 * 512:(n2 + 1) * 512]), writes=["w16_fd"], key="w16_fd")
    return w16


def dsl(C, e, t0, n):
    if C.dyn is None:
        return slice(t0, t0 + n)
    if "v" not in C.dyn:
        e.reg_load(C.dyn["reg"], C.dyn["qoff"][0:1, 0:1])
        C.dyn["v"] = e.snap(C.dyn["reg"])
    return bass.ds(C.dyn["v"] + t0, n)


def ab_src(C, e, src_d, t0, n):
    if C.dyn is None:
        return src_d[:, t0:t0 + n]
    if "vm" not in C.dyn:
        e.reg_load(C.dyn["mreg"], C.dyn["qm"][0:1, 0:1])
        C.dyn["vm"] = e.snap(C.dyn["mreg"])
    CH = C.dyn["CH"]
    c0 = t0 % CH
    return src_d[bass.ds(C.dyn["vm"] + (t0 // CH), 1), :, c0:c0 + n].rearrange("o r t -> (o r) t")


def phase_b(C, TB, x_d, oaT_d, obT_d, out_d, w, w16):
    nc, S = C.nc, C.S
    TT = 512
    NT = TB // TT
    gmix = load_gfull(C, "gmix", w["g_mix"])
    gffn = load_gfull(C, "gffn", w["g_ffn"])
    gfin = C.sb("gfin", [128, D], F32)
    S.dma("sp", L("dma_start", out=gfin[:], in_=w["g_fin"].partition_broadcast(128)), writes=["gfin"])
    C.epsb = C.sb("epsb", [128, 1], F32)
    S.op("pool", L("memset", C.epsb[:], EPS), writes=["epsb"])

    xres = C.sb("xres", [128, 4, D], F32)
    hT = C.sb("hT", [128, 8, TT], BF16)
    oaT = C.sb("oaT", [128, 8, TT], BF16)
    obT = C.sb("obT", [128, 8, TT], BF16)
    mT = C.sb("mT", [128, 8, TT], BF16)
    actT = C.sb("actT", [128, NFT, TT], BF16)
    NSL = 4
    slab = [C.sb(f"slab{i}", [128, 8, 512], BF16) for i in range(NSL)]
    dslab = [C.sb(f"dslab{i}", [128, NFT, 512], BF16) for i in range(2)]
    tmp = [C.sb(f"tmp{i}", [128, 512], F32) for i in range(4)]
    outb = C.sb("outb", [128, D], F32)

    def load_slab(wk, n0):
        i = C.nxt("slab", NSL)
        S.dma("sp", L("dma_start", out=slab[i][:], in_=w16[wk][n0 // 512]), reads=["w16_" + wk], writes=[f"slab{i}"])
        return slab[i], f"slab{i}"

    def mm_group(wk_slab, wkey, ncol, rhs_t, rkey):
        a = C.nxt("acc", C.NACC)
        for c in range(8):
            S.op("pe", L("matmul", C.acc[a][:, :], lhsT=wk_slab[:, c, ncol * 128:(ncol + 1) * 128], rhs=rhs_t[:, c, :], start=(c == 0), stop=(c == 7)),
                 reads=[wkey, rkey], writes=[f"acc{a}"])
        return C.acc[a], f"acc{a}"

    out_toks = []
    for t in range(NT):
        t0 = t * TT
        S.dma("sp", (lambda e, t0=t0: e.dma_start(out=xres[:], in_=x_d[dsl(C, e, t0, TT), :].rearrange("(j p) d -> p j d", p=128))), writes=["xres"])
        S.dma("sp", (lambda e, t0=t0: e.dma_start(out=oaT[:], in_=ab_src(C, e, oaT_d, t0, TT).rearrange("(c p) t -> p c t", p=128))), reads=["recv_a"], writes=["oaT"])
        S.dma("sp", (lambda e, t0=t0: e.dma_start(out=obT[:], in_=ab_src(C, e, obT_d, t0, TT).rearrange("(c p) t -> p c t", p=128))), reads=["recv_b"], writes=["obT"])
        for j in range(4):
            norm_transpose(C, xres[:, j, :], "xres", gmix, "gmix", hT[:, :, j * 128:(j + 1) * 128], "hT")
        for q4 in range(2):
            sga, kga = load_slab("wg", q4 * 512)
            sgb, kgb = load_slab("wg", 1024 + q4 * 512)
            spa, kpa = load_slab("pa", q4 * 512)
            spb, kpb = load_slab("pb", q4 * 512)
            for n in range(4):
                ct = q4 * 4 + n
                GA, kGA = mm_group(sga, kga, n, hT, "hT")
                PA, kPA = mm_group(spa, kpa, n, oaT, "oaT")
                GB, kGB = mm_group(sgb, kgb, n, hT, "hT")
                PB, kPB = mm_group(spb, kpb, n, obT, "obT")
                S.op("act", L("activation", out=tmp[0][:], in_=GA[:], func=AF.Sigmoid), reads=[kGA], writes=["tmp0"])
                S.op("act", L("activation", out=tmp[1][:], in_=GB[:], func=AF.Sigmoid), reads=[kGB], writes=["tmp1"])
                S.op("dve", L("tensor_tensor", out=tmp[2][:], in0=PA[:], in1=tmp[0][:], op=ALU.mult), reads=[kPA, "tmp0"], writes=["tmp2"])
                S.op("dve", L("tensor_tensor", out=tmp[3][:], in0=PB[:], in1=tmp[1][:], op=ALU.mult), reads=[kPB, "tmp1"], writes=["tmp3"])
                S.op("pool", L("tensor_tensor", out=mT[:, ct, :], in0=tmp[2][:], in1=tmp[3][:], op=ALU.add), reads=["tmp2", "tmp3"], writes=["mT"])
        for n2 in range(2):
            so, ko = load_slab("wo", n2 * 512)
            for j in range(4):
                a = C.nxt("acc", C.NACC)
                for c in range(8):
                    S.op("pe", L("matmul", C.acc[a][:, :], lhsT=mT[:, c, j * 128:(j + 1) * 128], rhs=so[:, c, :], start=(c == 0), stop=(c == 7)),
                         reads=["mT", ko], writes=[f"acc{a}"])
                S.op("dve", L("tensor_tensor", out=xres[:, j, n2 * 512:(n2 + 1) * 512], in0=C.acc[a][:], in1=xres[:, j, n2 * 512:(n2 + 1) * 512], op=ALU.add),
                     reads=[f"acc{a}", "xres"], writes=["xres"])
        for j in range(4):
            norm_transpose(C, xres[:, j, :], "xres", gffn, "gffn", hT[:, :, j * 128:(j + 1) * 128], "hT")
        for f4 in range(0, DFF, 512):
            wcols = min(512, DFF - f4)
            i1 = C.nxt("slab", NSL)
            S.dma("sp", L("dma_start", out=slab[i1][:, :, 0:wcols], in_=w16["fg"][f4 // 512][:, :, 0:wcols]), reads=["w16_fg"], writes=[f"slab{i1}"])
            i2 = C.nxt("slab", NSL)
            S.dma("sp", L("dma_start", out=slab[i2][:, :, 0:wcols], in_=w16["fu"][f4 // 512][:, :, 0:wcols]), reads=["w16_fu"], writes=[f"slab{i2}"])
            for n in range(wcols // 128):
                ft = f4 // 128 + n
                G, kG = mm_group(slab[i1], f"slab{i1}", n, hT, "hT")
                U, kU = mm_group(slab[i2], f"slab{i2}", n, hT, "hT")
                tb = C.nxt("tmpf", 2)
                S.op("act", L("activation", out=tmp[tb][:], in_=G[:], func=AF.Silu), reads=[kG], writes=[f"tmp{tb}"])
                S.op("dve", L("tensor_tensor", out=actT[:, ft, :], in0=U[:], in1=tmp[tb][:], op=ALU.mult), reads=[kU, f"tmp{tb}"], writes=["actT"])
        for n2 in range(2):
            di = C.nxt("dslab", 2)
            S.dma("sp", L("dma_start", out=dslab[di][:], in_=w16["fd"][n2]), reads=["w16_fd"], writes=[f"dslab{di}"])
            for j in range(4):
                a = C.nxt("acc", C.NACC)
                for ft in range(NFT):
                    S.op("pe", L("matmul", C.acc[a][:, :], lhsT=actT[:, ft, j * 128:(j + 1) * 128], rhs=dslab[di][:, ft, :], start=(ft == 0), stop=(ft == NFT - 1)),
                         reads=["actT", f"dslab{di}"], writes=[f"acc{a}"])
                S.op("dve", L("tensor_tensor", out=xres[:, j, n2 * 512:(n2 + 1) * 512], in0=C.acc[a][:], in1=xres[:, j, n2 * 512:(n2 + 1) * 512], op=ALU.add),
                     reads=[f"acc{a}", "xres"], writes=["xres"])
        for j in range(4):
            i = C.nxt("stat", 16)
            ss, rs, rstd = C.stat[:, 3 * i:3 * i + 1], C.stat[:, 3 * i + 1:3 * i + 2], C.stat[:, 3 * i + 2:3 * i + 3]
            sk = f"stat{i}"
            S.op("act", L("activation", out=C.junk[:], in_=xres[:, j, :], func=AF.Square, accum_out=ss), reads=["xres"], writes=["junk", sk])
            S.op("act", L("activation", out=rs, in_=ss, func=AF.Sqrt, scale=1.0 / D, bias=C.epsb[:, 0:1]), reads=[sk, "epsb"], writes=[sk])
            S.op("dve", L("reciprocal", out=rstd, in_=rs), reads=[sk], writes=[sk])
            S.op("dve", L("scalar_tensor_tensor", out=outb[:], in0=xres[:, j, :], scalar=rstd, in1=gfin[:], op0=ALU.mult, op1=ALU.mult), reads=["xres", sk, "gfin"], writes=["outb"])
            tk = S.dma("sp", L("dma_start", out=out_d[t0 + j * 128:t0 + (j + 1) * 128, :], in_=outb[:]), reads=["outb"], key="outd")
        out_toks = [tk]
    return out_toks


DILS = (1, 4, 16)

def blk_geom(di, b):
    d = DILS[di]
    if d == 1:
        return 128 * b, 1
    if d == 4:
        return 512 * (b // 4) + (b % 4), 4
    return b, 16

def prev_blk(di, b):
    d = DILS[di]
    if d == 1:
        return ("cur", b - 1) if b >= 1 else ("prev", 0)
    if d == 4:
        return ("cur", b - 4) if b >= 4 else ("prev", b)
    return ("prev", b)

def prev_src_blk(di, j):
    d = DILS[di]
    if d == 1:
        return 15
    if d == 4:
        return 12 + j
    return j
NPREV = (1, 4, 16)


def phase_a1(C, SA, x_d, wqkv_d, g_d, cosF_d, sinF_d, rperm_d, mask_d, out_ap, out_piece=2048):
    nc, S = C.nc, C.S
    MC = 2048
    NMC = SA // MC
    gmix = load_gfull(C, "gmix", g_d)
    C.epsb = C.sb("epsb", [128, 1], F32)
    S.op("pool", L("memset", C.epsb[:], EPS), writes=["epsb"])
    W = C.sb("Wqkv", [128, 8, 768], BF16)
    for c in range(8):
        S.dma("pool", L("dma_start", out=W[:, c, :], in_=wqkv_d[c * 128:(c + 1) * 128, :]), writes=["W"])
    rperm = C.sb("rperm", [128, 128], BF16)
    S.dma("pool", L("dma_start", out=rperm[:], in_=rperm_d[:, :]), writes=["rperm"])
    mask = C.sb("mask", [128, 512], BF16)
    S.dma("pool", L("dma_start", out=mask[:], in_=mask_d[:, :]), writes=["mask"])
    ones65 = C.sb("ones65", [128, 64], F32)
    S.op("pool", L("memset", ones65[:], 1.0), writes=["ones65"])

    xt = [C.sb(f"xt{i}", [128, D], F32) for i in range(2)]
    hT = C.sb("hT", [128, 8, 512], BF16)
    cosF = C.sb("cosF", [128, 512], F32)
    sinF = C.sb("sinF", [128, 512], F32)
    qraw = C.sb("qraw", [128, 512], BF16)
    t1 = C.sb("t1", [128, 512], F32)
    t2 = C.sb("t2", [128, 512], F32)
    qT = C.sb("qT", [128, 2, MC], BF16)
    kT = [C.sb(f"kT{i}", [128, 2, MC], BF16) for i in range(2)]
    vT = C.sb("vT", [128, 2, MC], BF16)
    Vc = [C.sb(f"Vc{di}", [128, 16, 4, 128], BF16) for di in range(3)]
    Vp = [C.sb(f"Vp{di}", [128, NPREV[di], 4, 128], BF16) for di in range(3)]
    ones3 = ones65[:].unsqueeze(1).to_broadcast([128, 4, 64])
    for di in range(3):
        for b in range(16):
            S.op("dve", L("tensor_copy", out=Vc[di][:, b, :, 64:128], in_=ones3), reads=["ones65"], writes=[f"Vc{di}"])
        for b in range(NPREV[di]):
            S.op("dve", L("tensor_copy", out=Vp[di][:, b, :, 64:128], in_=ones3), reads=["ones65"], writes=[f"Vp{di}"])
    oacc = [C.sb(f"oacc{i}", [128, MC], F32) for i in range(2)]
    pexp = [C.sb(f"pexp{i}", [128, 512], BF16) for i in range(2)]
    pm = [C.sb(f"pm{i}", [128, 512], BF16) for i in range(2)]
    rden = C.sb("rden", [128, 512], F32)
    rden2 = C.sb("rden2", [64, 512], F32)
    oout = [C.sb(f"oout{i}", [64, MC], BF16) for i in range(2)]

    toks = []
    for m in range(NMC):
        slot = m % 2
        kTc, kTk = kT[slot], f"kT{slot}"
        kTp, kTpk = kT[1 - slot], f"kT{1 - slot}"
        for sc in range(4):
            t0 = m * MC + sc * 512
            l0 = sc * 512
            for j in range(4):
                xb = C.nxt("xt", 2)
                S.dma("sp", L("dma_start", out=xt[xb][:], in_=x_d[t0 + j * 128:t0 + (j + 1) * 128, :]), writes=[f"xt{xb}"])
                norm_transpose(C, xt[xb][:], f"xt{xb}", gmix, "gmix", hT[:, :, j * 128:(j + 1) * 128], "hT")
            S.dma("act", L("dma_start", out=cosF[:], in_=cosF_d[:, t0:t0 + 512]), writes=["cosF"])
            S.dma("act", L("dma_start", out=sinF[:], in_=sinF_d[:, t0:t0 + 512]), writes=["sinF"])
            for ti in range(6):
                a = C.nxt("acc", C.NACC)
                for c in range(8):
                    S.op("pe", L("matmul", C.acc[a][:, :], lhsT=W[:, c, ti * 128:(ti + 1) * 128], rhs=hT[:, c, :], start=(c == 0), stop=(c == 7)),
                         reads=["W", "hT"], writes=[f"acc{a}"])
                hp = ti % 2
                if ti >= 4:
                    S.op("act", L("activation", out=vT[:, hp, l0:l0 + 512], in_=C.acc[a][:], func=AF.Copy), reads=[f"acc{a}"], writes=["vT"])
                    continue
                S.op("dve", L("tensor_copy", out=qraw[:], in_=C.acc[a][:]), reads=[f"acc{a}"], writes=["qraw"])
                a2 = C.nxt("acc", C.NACC)
                S.op("pe", L("matmul", C.acc[a2][:, :], lhsT=rperm[:], rhs=qraw[:], start=True, stop=True), reads=["rperm", "qraw"], writes=[f"acc{a2}"])
                S.op("dve", L("tensor_tensor", out=t1[:], in0=C.acc[a][:], in1=cosF[:], op=ALU.mult), reads=[f"acc{a}", "cosF"], writes=["t1"])
                S.op("dve", L("tensor_tensor", out=t2[:], in0=C.acc[a2][:], in1=sinF[:], op=ALU.mult), reads=[f"acc{a2}", "sinF"], writes=["t2"])
                if ti < 2:
                    dst, dk = qT[:, hp, l0:l0 + 512], "qT"
                else:
                    dst, dk = kTc[:, hp, l0:l0 + 512], kTk
                S.op("pool", L("tensor_tensor", out=dst, in0=t1[:], in1=t2[:], op=ALU.add), reads=["t1", "t2"], writes=[dk])
        for di in range(3):
            for b0 in range(0, 16, 4):
                pb = C.nxt("hn", 2)
                pt, ptk = C.pt[pb], f"pt{pb}"
                for bb in range(4):
                    base, st_ = blk_geom(di, b0 + bb)
                    for hp in range(2):
                        S.op("pe", L("transpose", out=pt[:, (bb * 2 + hp) * 128:(bb * 2 + hp + 1) * 128],
                                     in_=vT[:, hp, base:base + 127 * st_ + 1:st_], identity=C.ident[:]), reads=["vT", "ident"], writes=[ptk])
                for bb in range(4):
                    S.op("act", L("activation", out=Vc[di][:, b0 + bb, :, 0:64], in_=pt[:, bb * 256:(bb + 1) * 256].rearrange("p (h c) -> p h c", h=4), func=AF.Copy),
                         reads=[ptk], writes=[f"Vc{di}"])
        for h in range(4):
            hp, r0 = h // 2, 64 * (h % 2)
            ob = C.nxt("oacc", 2)
            oa, oak = oacc[ob], f"oacc{ob}"
            jobs = [(di, b0, pr) for di in range(3) for b0 in range(0, 16, 4) for pr in range(2)]
            state = {}

            def front(job):
                di, b0, pr = job
                if pr == 0:
                    pa = C.nxt("acc", C.NACC)
                    state[(di, b0)] = (C.acc[pa], f"acc{pa}")
                sa_ = C.nxt("acc", C.NACC)
                sc_, sck = C.acc[sa_], f"acc{sa_}"
                info = []
                for qq in range(2):
                    b = b0 + pr * 2 + qq
                    base, st_ = blk_geom(di, b)
                    qsl = qT[r0:r0 + 64, hp, base:base + 127 * st_ + 1:st_]
                    where, pbk = prev_blk(di, b)
                    has_prev = not (where == "prev" and m == 0)
                    if has_prev:
                        if where == "cur":
                            pbase, pst = blk_geom(di, pbk)
                            ksl, kk_ = kTc[r0:r0 + 64, hp, pbase:pbase + 127 * pst + 1:pst], kTk
                            vsl, vk_ = Vc[di][:, pbk, h, :], f"Vc{di}"
                        else:
                            pbase, pst = blk_geom(di, prev_src_blk(di, pbk))
                            ksl, kk_ = kTp[r0:r0 + 64, hp, pbase:pbase + 127 * pst + 1:pst], kTpk
                            vsl, vk_ = Vp[di][:, pbk, h, :], f"Vp{di}"
                        S.op("pe", L("matmul", sc_[:, qq * 256:qq * 256 + 128], lhsT=ksl, rhs=qsl, start=True, stop=True), reads=[kk_, "qT"], writes=[sck])
                    else:
                        vsl, vk_ = None, None
                    S.op("pe", L("matmul", sc_[:, qq * 256 + 128:qq * 256 + 256], lhsT=kTc[r0:r0 + 64, hp, base:base + 127 * st_ + 1:st_], rhs=qsl, start=True, stop=True),
                         reads=[kTk, "qT"], writes=[sck])
                    info.append((b, has_prev, vsl, vk_))
                pi = C.nxt("pexp", 2)
                S.op("act", L("activation", out=pexp[pi][:], in_=sc_[:], func=AF.Exp, scale=0.125), reads=[sck], writes=[f"pexp{pi}"])
                S.op("pool" if pi == 0 else "dve", L("tensor_tensor", out=pm[pi][:], in0=pexp[pi][:], in1=mask[:], op=ALU.mult), reads=[f"pexp{pi}", "mask"], writes=[f"pm{pi}"])
                state[job] = (pi, info)

            def back(job):
                di, b0, pr = job
                d = DILS[di]
                po, pok = state[(di, b0)]
                pi, info = state.pop(job)
                for qq in range(2):
                    b, has_prev, vsl, vk_ = info[qq]
                    col = (pr * 2 + qq) * 128
                    if has_prev:
                        S.op("pe", L("matmul", po[:, col:col + 128], lhsT=vsl, rhs=pm[pi][:, qq * 256:qq * 256 + 128], start=True, stop=False), reads=[vk_, f"pm{pi}"], writes=[pok])
                    S.op("pe", L("matmul", po[:, col:col + 128], lhsT=Vc[di][:, b, h, :], rhs=pm[pi][:, qq * 256 + 128:qq * 256 + 256], start=(not has_prev), stop=True),
                         reads=[f"Vc{di}", f"pm{pi}"], writes=[pok])
                if pr == 1:
                    if d == 1:
                        dst = oa[:, 128 * b0:128 * b0 + 512]
                        S.op("act", L("activation", out=dst, in_=po[:, :], func=AF.Copy), reads=[pok], writes=[oak])
                    else:
                        if d == 4:
                            dst = oa[:, 128 * b0:128 * b0 + 512].rearrange("p (i r) -> p r i", r=4)
                        else:
                            dst = oa[:, :].rearrange("p (i r) -> p r i", r=16)[:, b0:b0 + 4, :]
                        S.op("dve", L("tensor_tensor", out=dst, in0=po[:, :].rearrange("p (r i) -> p r i", r=4), in1=dst, op=ALU.add), reads=[pok, oak], writes=[oak])
                    state.pop((di, b0))

            for i in range(len(jobs) + 1):
                if i < len(jobs):
                    front(jobs[i])
                if i >= 1:
                    back(jobs[i - 1])
            oo, ook = oout[ob], f"oout{ob}"
            for q4 in range(4):
                S.op("dve", L("reciprocal", out=rden[64:128, :], in_=oa[64:128, q4 * 512:(q4 + 1) * 512]), reads=[oak], writes=["rden"])
                S.op("act", L("activation", out=rden2[:], in_=rden[64:128, :], func=AF.Copy), reads=["rden"], writes=["rden2"])
                S.op("dve", L("tensor_tensor", out=oo[:, q4 * 512:(q4 + 1) * 512], in0=oa[0:64, q4 * 512:(q4 + 1) * 512], in1=rden2[:], op=ALU.mult), reads=[oak, "rden2"], writes=[ook])
            for p0 in range(0, MC, out_piece):
                tk = S.dma("sp", L("dma_start", out=out_ap(slice(h * 64, (h + 1) * 64), m * MC + p0, out_piece), in_=oo[:, p0:p0 + out_piece]), reads=[ook], key="oaT_d" + str(ob))
                toks.append(tk)
        for di in range(3):
            n = NPREV[di]
            sb0 = prev_src_blk(di, 0)
            for jj in range(n):
                S.op("dve", L("tensor_copy", out=Vp[di][:, jj, :, 0:64], in_=Vc[di][:, sb0 + jj, :, 0:64]), reads=[f"Vc{di}"], writes=[f"Vp{di}"])
    return toks[-8:]


C0 = 0.6065306597126334
GN_EPS = 64e-5
PV_MU, PV_W0, PV_A0, PV_KK, PV_KA, PV_RK, PV_LNW, PV_LNB, PV_1MKA, PV_N = 0, 8, 10, 12, 14, 16, 18, 20, 22, 24


def phase_a2(C, SA, x_d, wr_d, g_d, pv_d, w2a2_d, g2_d, cst_d, out_ap):
    nc, S = C.nc, C.S
    NSC = SA // 512
    gmix = load_gfull(C, "gmix", g_d)
    C.epsb = C.sb("epsb", [128, 1], F32)
    S.op("pool", L("memset", C.epsb[:], EPS), writes=["epsb"])
    gnb = C.sb("gnb", [128, 1], F32)
    S.op("pool", L("memset", gnb[:], GN_EPS), writes=["gnb"])
    W = C.sb("Wr", [128, 8, 1024], BF16)
    for c in range(8):
        S.dma("pool", L("dma_start", out=W[:, c, :], in_=wr_d[c * 128:(c + 1) * 128, :]), writes=["W"])
    pv = C.sb("pv", [128, PV_N], F32)
    S.dma("sp", L("dma_start", out=pv[:, 0:22], in_=pv_d[:, 0:22]), writes=["pv"])
    S.op("dve", L("tensor_scalar", out=pv[:, PV_1MKA:PV_1MKA + 2], in0=pv[:, PV_KA:PV_KA + 2], scalar1=-1.0, scalar2=1.0, op0=ALU.mult, op1=ALU.add), reads=["pv"], writes=["pv"])
    w2a2 = C.sb("w2a2", [128, 256], BF16)
    S.dma("pool", L("dma_start", out=w2a2[:], in_=w2a2_d[:, :]), writes=["w2a2"])
    g2 = C.sb("g2", [128, 256], BF16)
    S.dma("pool", L("dma_start", out=g2[:], in_=g2_d[:, :]), writes=["g2"])
    cst = C.sb("cst", [128, 1280], F32)
    S.dma("sp", L("dma_start", out=cst[:], in_=cst_d[:, :]), writes=["cst"])
    mreset = cst[:, 0:512]
    blkones = cst[:, 512:640]
    identf = cst[:, 1216:1280]
    maskA = C.sb("maskA", [128, 512], BF16)
    S.op("dve", L("tensor_copy", out=maskA[:], in_=cst[:, 640:1152]), reads=["cst"], writes=["maskA"])
    maskL = C.sb("maskL", [128, 8, 64], BF16)
    S.op("dve", L("tensor_copy", out=maskL[:], in_=cst[:, 1152:1216].unsqueeze(1).to_broadcast([128, 8, 64])), reads=["cst"], writes=["maskL"])
    identb = C.sb("identb", [128, 4, 64], BF16)
    S.op("dve", L("tensor_copy", out=identb[:], in_=identf.unsqueeze(1).to_broadcast([128, 4, 64])), reads=["cst"], writes=["identb"])

    xt = [C.sb(f"xt{i}", [128, D], F32) for i in range(2)]
    hT = C.sb("hT", [128, 8, 512], BF16)
    colsb = [C.sb(f"colsb{i}", [128, 513], F32) for i in range(8)]
    for i in range(8):
        S.op("pool", L("memset", colsb[i][:, 0:1], 0.0), writes=[f"colsb{i}"])
    xm = [C.sb(f"xm{i}", [128, 512], F32) for i in range(8)]
    tmpd = C.sb("tmpd", [128, 512], F32)
    lo1b = C.sb("lo1b", [128, 512], BF16)
    sg = C.sb("sg", [128, 512], BF16)

    def f32t(n):
        return C.sb(n, [128, 512], F32)
    def b16t(n):
        return C.sb(n, [128, 512], BF16)
    HPF32 = ("lwp", "ai", "gt", "kk", "sc1", "sc2", "kf", "bvec", "cwp", "cwm", "Epos", "Eprev", "Eneg", "yt")
    HPB16 = ("bT", "kTt", "BhT", "KhT", "vb")
    bufs = []
    for hp_ in range(2):
        d = {}
        for n in HPF32:
            d[n] = f32t(f"{n}_{hp_}")
        for n in HPB16:
            d[n] = b16t(f"{n}_{hp_}")
        d["AR"] = C.sb(f"AR_{hp_}", [128, 8, 128], BF16)
        d["AT"] = C.sb(f"AT_{hp_}", [128, 8, 256], BF16)
        d["PTt"] = C.sb(f"PTt_{hp_}", [128, 8, 64], BF16)
        d["Pl"] = [C.sb(f"Pl{i}_{hp_}", [128, 4, 64], BF16) for i in range(2)]
        d["PTl"] = [C.sb(f"PTl{i}_{hp_}", [128, 4, 64], BF16) for i in range(2)]
        d["Z"] = C.sb(f"Z_{hp_}", [128, 8, 64], BF16)
        for n in ("Bh_tok", "Kh_tok", "V_tok"):
            d[n] = C.sb(f"{n}_{hp_}", [128, 8, 64], BF16)
        d["YY"] = C.sb(f"YY_{hp_}", [128, 8, 128], BF16)
        d["XX"] = C.sb(f"XX_{hp_}", [128, 8, 128], BF16)
        d["MN"] = C.sb(f"MN_{hp_}", [128, 8, 128], F32)
        d["QcT"] = C.sb(f"QcT_{hp_}", [128, 8, 64], F32)
        bufs.append(d)
    HPKEYS = set(HPF32) | set(HPB16) | {"AR", "AT", "PTt", "Pl0", "Pl1", "PTl0", "PTl1", "Z", "Bh_tok", "Kh_tok", "V_tok", "YY", "XX", "MN", "QcT"}
    Hs = [C.sb(f"Hs{hp}", [128, 9, 64], F32) for hp in range(2)]
    for hp in range(2):
        S.op("pool", L("memset", Hs[hp][:], 0.0), writes=[f"Hs{hp}"])
    obuf = [C.sb(f"obuf{i}", [128, 512], BF16) for i in range(2)]

    def acc():
        a = C.nxt("acc", C.NACC)
        return C.acc[a], f"acc{a}"

    toks = []
    for sci in range(NSC):
        t0 = sci * 512
        for j in range(4):
            xb = C.nxt("xt", 2)
            S.dma("sp", L("dma_start", out=xt[xb][:], in_=x_d[t0 + j * 128:t0 + (j + 1) * 128, :]), writes=[f"xt{xb}"])
            norm_transpose(C, xt[xb][:], f"xt{xb}", gmix, "gmix", hT[:, :, j * 128:(j + 1) * 128], "hT")
        for ti in range(8):
            A, Ak = acc()
            for c in range(8):
                S.op("pe", L("matmul", A[:, :], lhsT=W[:, c, ti * 128:(ti + 1) * 128], rhs=hT[:, c, :], start=(c == 0), stop=(c == 7)), reads=["W", "hT"], writes=[Ak])
            cb, cbk = colsb[ti], f"colsb{ti}"
            S.op("act", L("activation", out=cb[:, 1:513], in_=A[:], func=AF.Copy), reads=[Ak], writes=[cbk])
            S.op("dve", L("tensor_tensor", out=tmpd[:], in0=cb[:, 0:512], in1=cb[:, 1:513], op=ALU.subtract), reads=[cbk], writes=["tmpd"])
            S.op("dve", L("scalar_tensor_tensor", out=xm[ti][:], in0=tmpd[:], scalar=pv[:, PV_MU + ti:PV_MU + ti + 1], in1=cb[:, 1:513], op0=ALU.mult, op1=ALU.add),
                 reads=["tmpd", "pv", cbk], writes=[f"xm{ti}"])
            S.op("pool", L("tensor_copy", out=cb[:, 0:1], in_=cb[:, 512:513]), reads=[cbk], writes=[cbk])
        S.op("act", L("activation", out=lo1b[0:64, :], in_=xm[6][0:64, :], func=AF.Tanh), reads=["xm6"], writes=["lo1b"])
        S.op("act", L("activation", out=lo1b[64:128, :], in_=xm[6][64:128, :], func=AF.Copy), reads=["xm6"], writes=["lo1b"])
        S.op("act", L("activation", out=sg[:], in_=xm[7][:], func=AF.Sigmoid), reads=["xm7"], writes=["sg"])
        def hp_body(hp):
            B_ = bufs[hp]
            lwp, ai, gt, kk, sc1, sc2, kf, bvec, cwp, cwm, Epos, Eprev, Eneg, yt = [B_[n] for n in HPF32]
            kkn, bonus = kk, lwp
            bT, kTt, BhT, KhT, vb = [B_[n] for n in HPB16]
            AR, AT, PTt, Pl, PTl, Z, Bh_tok, Kh_tok, V_tok, YY, XX, MN, QcT = [B_[n] for n in ("AR", "AT", "PTt", "Pl", "PTl", "Z", "Bh_tok", "Kh_tok", "V_tok", "YY", "XX", "MN", "QcT")]
            km = lambda k: (k + "_" + str(hp)) if k in HPKEYS else k
            def op(eng, fn, reads=(), writes=()):
                return S.op(eng, fn, [km(k) for k in reads], [km(k) for k in writes])
            r_, k_, v_ = xm[hp], xm[2 + hp], xm[4 + hp]
            rk_, kk_, vk_ = f"xm{hp}", f"xm{2 + hp}", f"xm{4 + hp}"
            col = lambda base: pv[:, base + hp:base + hp + 1]
            A, Ak = acc()
            op("pe", L("matmul", A[:, :], lhsT=w2a2[0:64, hp * 128:(hp + 1) * 128], rhs=lo1b[0:64, :], start=True, stop=True), reads=["w2a2", "lo1b"], writes=[Ak])
            op("act", L("activation", out=lwp[:], in_=A[:], func=AF.Sigmoid, bias=col(PV_W0)), reads=[Ak, "pv"], writes=["lwp"])
            A, Ak = acc()
            op("pe", L("matmul", A[:, :], lhsT=w2a2[64:128, hp * 128:(hp + 1) * 128], rhs=lo1b[64:128, :], start=True, stop=True), reads=["w2a2", "lo1b"], writes=[Ak])
            op("act", L("activation", out=ai[:], in_=A[:], func=AF.Sigmoid, bias=col(PV_A0)), reads=[Ak, "pv"], writes=["ai"])
            A, Ak = acc()
            op("pe", L("matmul", A[:, :], lhsT=g2[:, hp * 128:(hp + 1) * 128], rhs=sg[:], start=True, stop=True), reads=["g2", "sg"], writes=[Ak])
            op("act", L("activation", out=gt[:], in_=A[:], func=AF.Copy), reads=[Ak], writes=["gt"])
            yield
            op("dve", L("tensor_scalar", out=kk[:], in0=k_[:], scalar1=col(PV_KK), scalar2=None, op0=ALU.mult), reads=[kk_, "pv"], writes=["kk"])
            op("pool", L("tensor_tensor", out=sc1[:], in0=kk[:], in1=kk[:], op=ALU.mult), reads=["kk"], writes=["sc1"])
            A, Ak = acc()
            op("pe", L("matmul", A[:, :], lhsT=blkones, rhs=sc1[:], start=True, stop=True), reads=["cst", "sc1"], writes=[Ak])
            op("act", L("activation", out=sc2[:], in_=A[:], func=AF.Sqrt), reads=[Ak], writes=["sc2"])
            op("dve", L("tensor_scalar", out=sc2[:], in0=sc2[:], scalar1=1e-12, scalar2=None, op0=ALU.max), reads=["sc2"], writes=["sc2"])
            op("dve", L("reciprocal", out=sc1[:], in_=sc2[:]), reads=["sc2"], writes=["sc1"])
            op("dve", L("tensor_tensor", out=kkn[:], in0=kk[:], in1=sc1[:], op=ALU.mult), reads=["kk", "sc1"], writes=["kk"])
            yield
            op("dve", L("tensor_scalar", out=sc2[:], in0=ai[:], scalar1=col(PV_KA), scalar2=col(PV_1MKA), op0=ALU.mult, op1=ALU.add), reads=["ai", "pv"], writes=["sc2"])
            op("pool", L("tensor_tensor", out=kf[:], in0=sc2[:], in1=k_[:], op=ALU.mult), reads=["sc2", kk_], writes=["kf"])
            op("pool", L("tensor_tensor", out=bvec[:], in0=kkn[:], in1=ai[:], op=ALU.mult), reads=["kk", "ai"], writes=["bvec"])
            yield
            op("dve", L("tensor_tensor_scan", out=cwp[:], data0=mreset, data1=lwp[:], initial=0.0, op0=ALU.mult, op1=ALU.add), reads=["cst", "lwp"], writes=["cwp"])
            op("pool", L("tensor_tensor", out=cwm[:], in0=cwp[:], in1=lwp[:], op=ALU.subtract), reads=["cwp", "lwp"], writes=["cwm"])
            op("act", L("activation", out=Epos[:], in_=cwp[:], func=AF.Exp, scale=-C0), reads=["cwp"], writes=["Epos"])
            op("act", L("activation", out=Eprev[:], in_=cwm[:], func=AF.Exp, scale=-C0), reads=["cwm"], writes=["Eprev"])
            op("act", L("activation", out=Eneg[:], in_=cwp[:], func=AF.Exp, scale=C0), reads=["cwp"], writes=["Eneg"])
            v3 = lambda ap: ap.rearrange("p (c t) -> p c t", t=64)
            op("dve", L("scalar_tensor_tensor", out=AR[:, :, 0:64], in0=v3(kkn[:]), scalar=-1.0, in1=v3(Eprev[:]), op0=ALU.mult, op1=ALU.mult), reads=["kk", "Eprev"], writes=["AR"])
            op("pool", L("tensor_tensor", out=AR[:, :, 64:128], in0=v3(r_[:]), in1=v3(Epos[:]), op=ALU.mult), reads=[rk_, "Epos"], writes=["AR"])
            op("pool", L("tensor_tensor", out=bT[:], in0=bvec[:], in1=Eneg[:], op=ALU.mult), reads=["bvec", "Eneg"], writes=["bT"])
            op("dve", L("tensor_tensor", out=kTt[:], in0=kf[:], in1=Eneg[:], op=ALU.mult), reads=["kf", "Eneg"], writes=["kTt"])
            wcb = v3(Epos[:])[:, :, 63:64].to_broadcast([128, 8, 64])
            op("dve", L("tensor_tensor", out=v3(BhT[:]), in0=v3(bT[:]), in1=wcb, op=ALU.mult), reads=["bT", "Epos"], writes=["BhT"])
            op("pool", L("tensor_tensor", out=v3(KhT[:]), in0=v3(kTt[:]), in1=wcb, op=ALU.mult), reads=["kTt", "Epos"], writes=["KhT"])
            op("act", L("activation", out=vb[:], in_=v_[:], func=AF.Copy), reads=[vk_], writes=["vb"])
            yield
            op("dve", L("tensor_tensor", out=sc1[:], in0=r_[:], in1=kf[:], op=ALU.mult), reads=[rk_, "kf"], writes=["sc1"])
            op("dve", L("tensor_scalar", out=sc1[:], in0=sc1[:], scalar1=col(PV_RK), scalar2=None, op0=ALU.mult), reads=["sc1", "pv"], writes=["sc1"])
            A, Ak = acc()
            op("pe", L("matmul", A[:, :], lhsT=blkones, rhs=sc1[:], start=True, stop=True), reads=["cst", "sc1"], writes=[Ak])
            op("dve", L("tensor_tensor", out=bonus[:], in0=A[:], in1=v_[:], op=ALU.mult), reads=[Ak, vk_], writes=["lwp"])
            yield
            for src, sk, dst, dk, dsl in ((BhT, "BhT", Bh_tok, "Bh_tok", None), (KhT, "KhT", Kh_tok, "Kh_tok", None), (vb, "vb", V_tok, "V_tok", None), (None, "AR", YY, "YY", 1)):
                pb = C.nxt("hn", 2)
                pt, ptk = C.pt[pb], f"pt{pb}"
                for c in range(8):
                    for hd in range(2):
                        r0 = 64 * hd
                        in_ap = src[r0:r0 + 64, c * 64:(c + 1) * 64] if src is not None else AR[r0:r0 + 64, c, 0:64]
                        op("pe", L("transpose", out=pt[r0:r0 + 64, c * 64:(c + 1) * 64], in_=in_ap, identity=C.ident[r0:r0 + 64, r0:r0 + 64]), reads=[sk, "ident"], writes=[ptk])
                o = dst[:] if dsl is None else dst[:, :, 64:128]
                op("act", L("activation", out=o, in_=pt[:, 0:512].rearrange("p (c t) -> p c t", t=64), func=AF.Copy), reads=[ptk], writes=[dk])
                yield
            yield
            for c in range(0, 8, 2):
                A, Ak = acc()
                for cc in range(2):
                    for hd in range(2):
                        r0 = 64 * hd
                        op("pe", L("matmul", A[r0:r0 + 64, cc * 256:cc * 256 + 128], lhsT=bT[r0:r0 + 64, (c + cc) * 64:(c + cc + 1) * 64], rhs=AR[r0:r0 + 64, c + cc, :], start=True, stop=True), reads=["bT", "AR"], writes=[Ak])
                        op("pe", L("matmul", A[r0:r0 + 64, cc * 256 + 128:cc * 256 + 256], lhsT=kTt[r0:r0 + 64, (c + cc) * 64:(c + cc + 1) * 64], rhs=AR[r0:r0 + 64, c + cc, :], start=True, stop=True), reads=["kTt", "AR"], writes=[Ak])
                op("dve", L("tensor_tensor", out=AT[:, c:c + 2, :], in0=A[:].rearrange("p (c t) -> p c t", c=2), in1=maskA[:].rearrange("p (c t) -> p c t", c=2), op=ALU.mult), reads=[Ak, "maskA"], writes=["AT"])
            A, Ak = acc()
            for c in range(8):
                for hd in range(2):
                    r0 = 64 * hd
                    op("pe", L("matmul", A[r0:r0 + 64, c * 64:(c + 1) * 64], lhsT=AR[r0:r0 + 64, c, 0:64], rhs=bT[r0:r0 + 64, c * 64:(c + 1) * 64], start=True, stop=True), reads=["AR", "bT"], writes=[Ak])
            op("dve", L("tensor_tensor", out=PTt[:], in0=A[:].rearrange("p (c t) -> p c t", t=64), in1=maskL[:], op=ALU.mult), reads=[Ak, "maskL"], writes=["PTt"])
            yield
            for half in range(2):
                c0 = half * 4
                Pv, Pk = (lambda c: AT[:, c0 + c, 0:64]), "AT"
                PTv, PTk = (lambda c: PTt[:, c0 + c, :]), "PTt"
                Zh = Z[:, c0:c0 + 4, :]
                op("pool", L("tensor_tensor", out=Zh, in0=AT[:, c0:c0 + 4, 0:64], in1=identb[:], op=ALU.add), reads=["AT", "identb"], writes=["Z"])
                for lvl in range(6):
                    AP_, APk = acc()
                    if lvl > 0:
                        AZ, AZk = acc()
                    for c in range(4):
                        for hd in range(2):
                            r0 = 64 * hd
                            p_, pt_ = Pv(c)[r0:r0 + 64], PTv(c)[r0:r0 + 64]
                            if lvl > 0:
                                op("pe", L("matmul", AZ[r0:r0 + 64, c * 64:(c + 1) * 64], lhsT=pt_, rhs=Z[r0:r0 + 64, c0 + c, :], start=True, stop=True), reads=[PTk, "Z"], writes=[AZk])
                            if lvl < 5:
                                op("pe", L("matmul", AP_[r0:r0 + 64, c * 128:c * 128 + 64], lhsT=pt_, rhs=p_, start=True, stop=True), reads=[PTk, Pk], writes=[APk])
                                op("pe", L("matmul", AP_[r0:r0 + 64, c * 128 + 64:c * 128 + 128], lhsT=p_, rhs=pt_, start=True, stop=True), reads=[PTk, Pk], writes=[APk])
                    if lvl > 0:
                        op("dve", L("tensor_tensor", out=Zh, in0=AZ[:, 0:256].rearrange("p (c t) -> p c t", t=64), in1=Zh, op=ALU.add), reads=[AZk, "Z"], writes=["Z"])
                    if lvl < 5:
                        nb = lvl % 2
                        op("act", L("activation", out=Pl[nb][:], in_=AP_[:].rearrange("p (c u t) -> p c u t", c=4, u=2)[:, :, 0, :], func=AF.Copy), reads=[APk], writes=[f"Pl{nb}"])
                        op("act", L("activation", out=PTl[nb][:], in_=AP_[:].rearrange("p (c u t) -> p c u t", c=4, u=2)[:, :, 1, :], func=AF.Copy), reads=[APk], writes=[f"PTl{nb}"])
                        Pv, Pk = (lambda c, nb=nb: Pl[nb][:, c, :]), f"Pl{nb}"
                        PTv, PTk = (lambda c, nb=nb: PTl[nb][:, c, :]), f"PTl{nb}"
                    yield
            yield
            A, Ak = acc()
            for c in range(8):
                for hd in range(2):
                    r0 = 64 * hd
                    op("pe", L("matmul", A[r0:r0 + 64, c * 64:(c + 1) * 64], lhsT=AT[r0:r0 + 64, c, 128:192], rhs=V_tok[r0:r0 + 64, c, :], start=True, stop=True), reads=["AT", "V_tok"], writes=[Ak])
            op("act", L("activation", out=YY[:, :, 0:64], in_=A[:].rearrange("p (c t) -> p c t", t=64), func=AF.Copy), reads=[Ak], writes=["YY"])
            yield
            for c in range(0, 8, 4):
                A, Ak = acc()
                for cc in range(4):
                    for hd in range(2):
                        r0 = 64 * hd
                        op("pe", L("matmul", A[r0:r0 + 64, cc * 128:(cc + 1) * 128], lhsT=Z[r0:r0 + 64, c + cc, :], rhs=YY[r0:r0 + 64, c + cc, :], start=True, stop=True), reads=["Z", "YY"], writes=[Ak])
                op("dve", L("tensor_copy", out=XX[:, c:c + 4, :], in_=A[:].rearrange("p (c t) -> p c t", t=128)), reads=[Ak], writes=["XX"])
            yield
            for c in range(0, 8, 4):
                A, Ak = acc()
                for cc in range(4):
                    for hd in range(2):
                        r0 = 64 * hd
                        ci = c + cc
                        op("pe", L("matmul", A[r0:r0 + 64, cc * 128:cc * 128 + 64], lhsT=XX[r0:r0 + 64, ci, 64:128], rhs=Bh_tok[r0:r0 + 64, ci, :], start=True, stop=True), reads=["XX", "Bh_tok"], writes=[Ak])
                        op("pe", L("matmul", A[r0:r0 + 64, cc * 128 + 64:cc * 128 + 128], lhsT=Bh_tok[r0:r0 + 64, ci, :], rhs=XX[r0:r0 + 64, ci, 0:64], start=True, stop=False), reads=["XX", "Bh_tok"], writes=[Ak])
                        op("pe", L("matmul", A[r0:r0 + 64, cc * 128 + 64:cc * 128 + 128], lhsT=Kh_tok[r0:r0 + 64, ci, :], rhs=V_tok[r0:r0 + 64, ci, :], start=False, stop=True), reads=["Kh_tok", "V_tok"], writes=[Ak])
                op("act", L("activation", out=MN[:, c:c + 4, :], in_=A[:].rearrange("p (c t) -> p c t", t=128), func=AF.Copy), reads=[Ak], writes=["MN"])
            for c in range(8):
                op("dve", L("scalar_tensor_tensor", out=MN[:, c, 0:64], in0=identf, scalar=Epos[:, c * 64 + 63:c * 64 + 64], in1=MN[:, c, 0:64], op0=ALU.mult, op1=ALU.add), reads=["cst", "Epos", "MN"], writes=["MN"])
            yield
            A, Ak = acc()
            for c in range(8):
                for hd in range(2):
                    r0 = 64 * hd
                    op("pe", L("matmul", A[r0:r0 + 64, c * 64:(c + 1) * 64], lhsT=XX[r0:r0 + 64, c, 64:128], rhs=AT[r0:r0 + 64, c, 64:128], start=True, stop=True), reads=["XX", "AT"], writes=[Ak])
            op("dve", L("tensor_tensor", out=QcT[:], in0=A[:].rearrange("p (c t) -> p c t", t=64), in1=AR[:, :, 64:128], op=ALU.add), reads=[Ak, "AR"], writes=["QcT"])
            yield
            H, Hk = Hs[hp], f"Hs{hp}"
            for c in range(8):
                A, Ak = acc()
                for hd in range(2):
                    r0 = 64 * hd
                    op("pe", L("matmul", A[r0:r0 + 64, 0:64], lhsT=MN[r0:r0 + 64, c, 0:64], rhs=H[r0:r0 + 64, c, :], start=True, stop=True), reads=["MN", Hk], writes=[Ak])
                op("dve", L("tensor_tensor", out=H[:, c + 1, :], in0=A[:, 0:64], in1=MN[:, c, 64:128], op=ALU.add), reads=[Ak, "MN"], writes=[Hk])
                yield
            yield
            A, Ak = acc()
            for c in range(8):
                for hd in range(2):
                    r0 = 64 * hd
                    o = A[r0:r0 + 64, c * 64:(c + 1) * 64]
                    op("pe", L("matmul", o, lhsT=H[r0:r0 + 64, c, :], rhs=QcT[r0:r0 + 64, c, :], start=True, stop=False), reads=[Hk, "QcT"], writes=[Ak])
                    op("pe", L("matmul", o, lhsT=XX[r0:r0 + 64, c, 0:64], rhs=AT[r0:r0 + 64, c, 64:128], start=False, stop=False), reads=["XX", "AT"], writes=[Ak])
                    op("pe", L("matmul", o, lhsT=V_tok[r0:r0 + 64, c, :], rhs=AT[r0:r0 + 64, c, 192:256], start=False, stop=True), reads=["V_tok", "AT"], writes=[Ak])
            op("act", L("activation", out=yt[:], in_=A[:], func=AF.Copy), reads=[Ak], writes=["yt"])
            op("pool", L("tensor_copy", out=H[:, 0, :], in_=H[:, 8, :]), reads=[Hk], writes=[Hk])
            yield
            op("pool", L("tensor_tensor", out=sc1[:], in0=yt[:], in1=yt[:], op=ALU.mult), reads=["yt"], writes=["sc1"])
            A1, A1k = acc()
            op("pe", L("matmul", A1[:, :], lhsT=blkones, rhs=yt[:], start=True, stop=True), reads=["cst", "yt"], writes=[A1k])
            A2, A2k = acc()
            op("pe", L("matmul", A2[:, :], lhsT=blkones, rhs=sc1[:], start=True, stop=True), reads=["cst", "sc1"], writes=[A2k])
            op("act", L("activation", out=sc2[:], in_=A1[:], func=AF.Copy, scale=1.0 / 64), reads=[A1k], writes=["sc2"])
            op("pool", L("tensor_tensor", out=sc1[:], in0=sc2[:], in1=sc2[:], op=ALU.mult), reads=["sc2"], writes=["sc1"])
            op("dve", L("scalar_tensor_tensor", out=sc1[:], in0=A2[:], scalar=1.0 / 64, in1=sc1[:], op0=ALU.mult, op1=ALU.subtract), reads=[A2k, "sc1"], writes=["sc1"])
            op("act", L("activation", out=sc1[:], in_=sc1[:], func=AF.Sqrt, bias=gnb[:, 0:1]), reads=["sc1", "gnb"], writes=["sc1"])
            op("dve", L("reciprocal", out=cwm[:], in_=sc1[:]), reads=["sc1"], writes=["cwm"])
            op("pool", L("tensor_tensor", out=yt[:], in0=yt[:], in1=sc2[:], op=ALU.subtract), reads=["yt", "sc2"], writes=["yt"])
            op("dve", L("tensor_tensor", out=yt[:], in0=yt[:], in1=cwm[:], op=ALU.mult), reads=["yt", "cwm"], writes=["yt"])
            op("dve", L("tensor_scalar", out=yt[:], in0=yt[:], scalar1=col(PV_LNW), scalar2=col(PV_LNB), op0=ALU.mult, op1=ALU.add), reads=["yt", "pv"], writes=["yt"])
            op("pool", L("tensor_tensor", out=yt[:], in0=yt[:], in1=bonus[:], op=ALU.add), reads=["yt", "lwp"], writes=["yt"])
            ob = C.nxt("obuf", 2)
            op("pool", L("tensor_tensor", out=obuf[ob][:], in0=yt[:], in1=gt[:], op=ALU.mult), reads=["yt", "gt"], writes=[f"obuf{ob}"])
            tk = S.dma("sp", L("dma_start", out=out_ap(slice(hp * 128, (hp + 1) * 128), t0, 512), in_=obuf[ob][:]), reads=[f"obuf{ob}"], key=f"obT_d{ob}")
            toks.append(tk)
        gens = [hp_body(0), hp_body(1)]
        while gens:
            for g_ in list(gens):
                try:
                    next(g_)
                except StopIteration:
                    gens.remove(g_)
    return toks[-2:]


import contextlib

def attn_consts(SA):
    half = 8
    inv = (500000.0 ** (-np.arange(half, dtype=np.float32) * np.float32(2.0 / 16))).astype(np.float32)
    ang = (np.arange(SA, dtype=np.float32)[None, :] * inv[:, None]).astype(np.float32)
    cosF = np.ones((128, SA), np.float32); sinF = np.zeros((128, SA), np.float32)
    rperm = np.zeros((128, 128), np.float32)
    for hb in (0, 64):
        cosF[hb:hb + 8] = np.cos(ang); cosF[hb + 8:hb + 16] = np.cos(ang)
        sinF[hb:hb + 8] = -np.sin(ang); sinF[hb + 8:hb + 16] = np.sin(ang)
        for i in range(8):
            rperm[hb + i + 8, hb + i] = 1.0
            rperm[hb + i, hb + i + 8] = 1.0
    j = np.arange(128)[:, None]; i = np.arange(128)[None, :]
    mprev = (j >= i).astype(np.float32); mcur = (j <= i).astype(np.float32)
    mask = np.ascontiguousarray(np.concatenate([mprev, mcur, mprev, mcur], 1))
    return cosF, sinF, rperm, mask

def rwkv_consts():
    p = np.arange(128)
    mreset = np.ones((128, 512), np.float32); mreset[:, ::64] = 0.0
    blk = (p[:, None] // 64 == p[None, :] // 64).astype(np.float32)
    s = (p % 64)[:, None]; t = np.arange(64)[None, :]
    MU = (s < t).astype(np.float32); MUI = (s <= t).astype(np.float32)
    maskA = np.concatenate([MU, MUI, MU, MUI, MU, MUI, MU, MUI], 1)
    ML = (t < s).astype(np.float32)
    identf = (s == t).astype(np.float32)
    return np.ascontiguousarray(np.concatenate([mreset, blk, maskA, ML, identf], 1))

def rwkv_inputs(inp, hg):
    w_in = inp["w_in"][0]
    cs = slice(hg * 256, (hg + 1) * 256)
    sh = 3072
    cols = np.concatenate([np.arange(sh + hg * 256, sh + hg * 256 + 256), np.arange(sh + 1024 + hg * 256, sh + 1024 + hg * 256 + 256),
                           np.arange(sh + 2048 + hg * 256, sh + 2048 + hg * 256 + 256), np.arange(sh + 3072, sh + 3072 + 256)])
    wr = np.ascontiguousarray(w_in[:, cols])
    mu = inp["shift_mu"][0][cols - sh]
    pv = np.zeros((128, 22), np.float32)
    pv[:, 0:8] = mu.reshape(8, 128).T
    for k, name in ((8, "decay_w0"), (10, "iclr_a0"), (12, "k_k"), (14, "k_a"), (16, "r_k"), (18, "ln_x_w"), (20, "ln_x_b")):
        v = inp[name][0].reshape(-1)[cs]
        pv[:, k:k + 2] = v.reshape(2, 128).T
    w2a2 = np.ascontiguousarray(np.concatenate([inp["decay_w2"][0][:, cs], inp["iclr_a2"][0][:, cs]], 0))
    g2 = np.ascontiguousarray(inp["gate_g2"][0][:, cs])
    return {"wr": wr, "pv": pv, "w2a2": w2a2, "g2": g2}

def attn_inputs(inp, hg):
    w_in = inp["w_in"][0]
    cols = np.concatenate([w_in[:, hg * 256:(hg + 1) * 256], w_in[:, 1024 + hg * 256:1024 + (hg + 1) * 256], w_in[:, 2048 + hg * 256:2048 + (hg + 1) * 256]], 1)
    return {"wqkv": np.ascontiguousarray(cols)}


def build_p1(SA):
    nc = bass.Bass("TRN2", target_bir_lowering=False)
    di = lambda n, s, d=F32: nc.dram_tensor(n, s, d, kind="ExternalInput").ap()
    x_d = di("x", [SA, 1024]); ident = di("ident", [128, 128]); g = di("g_mix", [1024])
    wqkv = di("wqkv", [1024, 768]); cosF = di("cosF", [128, SA]); sinF = di("sinF", [128, SA]); rperm = di("rperm", [128, 128]); mask = di("mask", [128, 512])
    wr = di("wr", [1024, 1024]); pv = di("pv", [128, 22]); w2a2 = di("w2a2", [128, 256]); g2 = di("g2", [128, 256]); cst = di("cst", [128, 1280])
    oaT = nc.dram_tensor("oaT", [256, SA], BF16, kind="ExternalOutput").ap()
    obT = nc.dram_tensor("obT", [256, SA], BF16, kind="ExternalOutput").ap()
    with contextlib.ExitStack() as st0:
        S = Sched(nc, st0)
        with contextlib.ExitStack() as st:
            C = Ctx(nc, S, st, "a1_")
            setup_common(C, ident)
            toks1 = phase_a1(C, SA, x_d, wqkv, g, cosF, sinF, rperm, mask, (lambda rows, g0, n: oaT[rows, g0:g0 + n]))
            S.run()
        S.barrier()
        with contextlib.ExitStack() as st:
            C = Ctx(nc, S, st, "a2_")
            setup_common(C, ident)
            toks2 = phase_a2(C, SA, x_d, wr, g, pv, w2a2, g2, cst, (lambda rows, g0, n: obT[rows, g0:g0 + n]))
            S.finish(list(toks1) + list(toks2))
            S.run()
    return nc


def build_p2(TB):
    nc = bass.Bass("TRN2", target_bir_lowering=False)
    di = lambda n, s, d=F32: nc.dram_tensor(n, s, d, kind="ExternalInput").ap()
    x_d = di("x", [TB, 1024]); oa = di("oaT", [1024, TB], BF16); ob = di("obT", [1024, TB], BF16)
    ident = di("ident", [128, 128])
    w = {"wg": di("wg", [1024, 2048]), "pa": di("pa", [1024, 1024]), "pb": di("pb", [1024, 1024]), "wo": di("wo", [1024, 1024]),
         "fg": di("fg", [1024, 2816]), "fu": di("fu", [1024, 2816]), "fd": di("fd", [2816, 1024]),
         "g_mix": di("g_mix", [1024]), "g_ffn": di("g_ffn", [1024]), "g_fin": di("g_fin", [1024])}
    out_d = nc.dram_tensor("out", [TB, 1024], F32, kind="ExternalOutput").ap()
    with contextlib.ExitStack() as st:
        S = Sched(nc, st)
        C = Ctx(nc, S, st)
        setup_common(C, ident)
        w16 = convert_weights(C, w)
        toks = phase_b(C, TB, x_d, oa, ob, out_d, w, w16)
        S.finish(toks)
        S.run()
    return nc


def build_fused(SA, TB):
    nc = bass.Bass("TRN2", target_bir_lowering=False)
    di = lambda n, s, d=F32: nc.dram_tensor(n, s, d, kind="ExternalInput").ap()
    x_d = di("x", [SA, 1024]); ident = di("ident", [128, 128]); g = di("g_mix", [1024])
    wqkv = di("wqkv", [1024, 768]); cosF = di("cosF", [128, SA]); sinF = di("sinF", [128, SA]); rperm = di("rperm", [128, 128]); mask = di("mask", [128, 512])
    wr = di("wr", [1024, 1024]); pv = di("pv", [128, 22]); w2a2 = di("w2a2", [128, 256]); g2 = di("g2", [128, 256]); cst = di("cst", [128, 1280])
    qoff = di("qoff", [1, 1], mybir.dt.int32)
    w = {"wg": di("wg", [1024, 2048]), "pa": di("pa", [1024, 1024]), "pb": di("pb", [1024, 1024]), "wo": di("wo", [1024, 1024]),
         "fg": di("fg", [1024, 2816]), "fu": di("fu", [1024, 2816]), "fd": di("fd", [2816, 1024]),
         "g_mix": g, "g_ffn": di("g_ffn", [1024]), "g_fin": di("g_fin", [1024])}
    out_d = nc.dram_tensor("out", [TB, 1024], F32, kind="ExternalOutput").ap()
    NG = 4
    CH = min(2048, TB)
    NCH = SA // CH
    send_a = nc.dram_tensor("send_a", [NCH, 256, CH], BF16, kind="Internal").ap()
    send_b = nc.dram_tensor("send_b", [NCH, 256, CH], BF16, kind="Internal").ap()
    recv_a = nc.dram_tensor("recv_a", [NCH, NG * 256, CH], BF16, kind="Internal").ap()
    recv_b = nc.dram_tensor("recv_b", [NCH, NG * 256, CH], BF16, kind="Internal").ap()
    qm = di("qm", [1, 1], mybir.dt.int32)
    groups = [[0, 1, 2, 3], [4, 5, 6, 7]]
    def chunk_ap(t):
        return lambda rows, g0, n: t[g0 // CH, rows, g0 % CH:g0 % CH + n]
    with contextlib.ExitStack() as st0:
        S = Sched(nc, st0)
        qreg = st0.enter_context(nc.sync.register("qreg"))
        mreg = st0.enter_context(nc.sync.register("mreg"))
        C0 = Ctx(nc, S, st0, "w_")
        w16 = convert_weights(C0, w)
        with contextlib.ExitStack() as st:
            C = Ctx(nc, S, st, "a1_")
            setup_common(C, ident)
            phase_a1(C, SA, x_d, wqkv, g, cosF, sinF, rperm, mask, chunk_ap(send_a), out_piece=min(CH, 2048))
            S.run()
        S.barrier()
        for m in range(NCH):
            S.cc(L("collective_compute", "AllGather", ALU.bypass, replica_groups=groups, ins=[send_a[m]], outs=[recv_a[m]]), key="cca")
        with contextlib.ExitStack() as st:
            C = Ctx(nc, S, st, "a2_")
            setup_common(C, ident)
            phase_a2(C, SA, x_d, wr, g, pv, w2a2, g2, cst, chunk_ap(send_b))
            S.run()
        S.barrier()
        for m in range(NCH):
            S.cc(L("collective_compute", "AllGather", ALU.bypass, replica_groups=groups, ins=[send_b[m]], outs=[recv_b[m]]), key="ccb")
        S.barrier()
        with contextlib.ExitStack() as st:
            C = Ctx(nc, S, st, "b_")
            C.dyn = {"reg": qreg, "qoff": qoff, "mreg": mreg, "qm": qm, "CH": CH}
            setup_common(C, ident)
            toks = phase_b(C, TB, x_d, recv_a, recv_b, out_d, w, w16)
            S.finish(toks)
            S.run()
    return nc


def kernel_unfused(**inputs):
    inp = {k: np.asarray(v) for k, v in inputs.items()}
    x = inp["x"]
    NBATCH, SEQ, _ = x.shape
    NQ = 8 // NBATCH
    TB = SEQ // NQ
    cosF, sinF, rperm, mask = attn_consts(SEQ)
    cst = rwkv_consts()
    ident = np.eye(128, dtype=np.float32)
    nc1 = build_p1(SEQ)
    maps1 = []
    for c in range(8):
        b, hg = c // NQ, c % NQ
        m = {"x": np.ascontiguousarray(x[b]), "ident": ident, "g_mix": inp["norm_mix_g"][0], "cosF": cosF, "sinF": sinF, "rperm": rperm, "mask": mask, "cst": cst}
        m.update(attn_inputs(inp, hg))
        m.update(rwkv_inputs(inp, hg))
        maps1.append(m)
    res1 = run_bass_kernel_spmd(nc1, maps1, core_ids=list(range(8)))
    nc2 = build_p2(TB)
    wg = np.ascontiguousarray(inp["w_in"][0][:, 8448 - 2048:])
    maps2 = []
    for c in range(8):
        b, q = c // NQ, c % NQ
        oa = np.concatenate([res1.results[b * NQ + hg]["oaT"][:, q * TB:(q + 1) * TB] for hg in range(NQ)], 0)
        ob = np.concatenate([res1.results[b * NQ + hg]["obT"][:, q * TB:(q + 1) * TB] for hg in range(NQ)], 0)
        maps2.append({"x": np.ascontiguousarray(x[b, q * TB:(q + 1) * TB]), "oaT": np.ascontiguousarray(oa), "obT": np.ascontiguousarray(ob), "ident": ident,
                      "wg": wg, "pa": inp["proj_attn"][0], "pb": inp["proj_rwkv"][0], "wo": inp["w_out"][0],
                      "fg": inp["ffn_w_gate"][0], "fu": inp["ffn_w_up"][0], "fd": inp["ffn_w_down"][0],
                      "g_mix": inp["norm_mix_g"][0], "g_ffn": inp["norm_ffn_g"][0], "g_fin": inp["norm_final_g"]})
    res2 = run_bass_kernel_spmd(nc2, maps2, core_ids=list(range(8)))
    out = np.zeros((NBATCH, SEQ, 1024), np.float32)
    for c in range(8):
        b, q = c // NQ, c % NQ
        out[b, q * TB:(q + 1) * TB] = res2.results[c]["out"]
    return out


def kernel(**inputs):
    inp = {k: np.asarray(v) for k, v in inputs.items()}
    x = inp["x"]
    NBATCH, SEQ, _ = x.shape
    NQ = 8 // NBATCH
    TB = SEQ // NQ
    cosF, sinF, rperm, mask = attn_consts(SEQ)
    cst = rwkv_consts()
    ident = np.eye(128, dtype=np.float32)
    nc = build_fused(SEQ, TB)
    wg = np.ascontiguousarray(inp["w_in"][0][:, 8448 - 2048:])
    maps = []
    for c in range(8):
        b, q = c // NQ, c % NQ
        m = {"x": np.ascontiguousarray(x[b]), "ident": ident, "g_mix": inp["norm_mix_g"][0], "cosF": cosF, "sinF": sinF, "rperm": rperm, "mask": mask, "cst": cst,
             "qoff": np.array([[q * TB]], np.int32), "qm": np.array([[q * (TB // min(2048, TB))]], np.int32),
             "wg": wg, "pa": inp["proj_attn"][0], "pb": inp["proj_rwkv"][0], "wo": inp["w_out"][0],
             "fg": inp["ffn_w_gate"][0], "fu": inp["ffn_w_up"][0], "fd": inp["ffn_w_down"][0],
             "g_ffn": inp["norm_ffn_g"][0], "g_fin": inp["norm_final_g"]}
        m.update(attn_inputs(inp, q))
        m.update(rwkv_inputs(inp, q))
        maps.append(m)
    res = run_bass_kernel_spmd(nc, maps, core_ids=list(range(8)))
    out = np.zeros((NBATCH, SEQ, 1024), np.float32)
    for c in range(8):
        b, q = c // NQ, c % NQ
        out[b, q * TB:(q + 1) * TB] = res.results[c]["out"]
    return out
```

```python
import os
import numpy as np
import concourse.bass as bass
import concourse.mybir as mybir
from concourse.bass_utils import run_bass_kernel_spmd

F32 = mybir.dt.float32
BF16 = mybir.dt.bfloat16
AF = mybir.ActivationFunctionType
ALU = mybir.AluOpType
AX = mybir.AxisListType

class Sched:
    COMPUTE = ("pe", "act", "dve", "pool")
    ALLQ = ("pe", "act", "dve", "pool", "sp")

    def __init__(self, nc, stack):
        self.nc = nc
        self.stack = stack
        self.ops = {e: [] for e in self.ALLQ}
        self.cnt = {e: 0 for e in self.COMPUTE}
        self.esem = {e: stack.enter_context(nc.semaphore("prog_" + e)) for e in self.COMPUTE}
        self.known = {e: {} for e in self.ALLQ}
        self.bufs = {}
        self.dsem = {}
        self.dcnt = {}
        self.final = []

    def _buf(self, k):
        b = self.bufs.get(k)
        if b is None:
            b = self.bufs[k] = {"w": [], "r": []}
        return b

    def _dma_sem(self, k):
        if k not in self.dsem:
            self.dsem[k] = self.stack.enter_context(self.nc.semaphore("d_" + str(len(self.dsem))))
            self.dcnt[k] = 0
        return self.dsem[k]

    def _emit(self, eng, reads, writes, fn, tok_fn):
        waits = {}
        def need(tok):
            s, v, src = tok
            if src == eng and eng == "pe":
                return
            if waits.get(s, (0,))[0] < v:
                waits[s] = (v, src)
        for k in reads:
            for t in self._buf(k)["w"]:
                need(t)
        for k in writes:
            b = self._buf(k)
            for t in b["w"]:
                need(t)
            for t in b["r"]:
                need(t)
        wl = []
        kn = self.known[eng]
        for s, (v, src) in waits.items():
            if kn.get(id(s), 0) >= v:
                continue
            kn[id(s)] = v
            wl.append((s, v))
        tok = tok_fn()
        self.ops[eng].append((wl, fn, tok))
        for k in reads:
            b = self._buf(k)
            b["r"] = [t for t in b["r"] if not (t[2] == eng and eng in self.COMPUTE)] + [tok]
        for k in writes:
            b = self._buf(k)
            b["w"] = [tok]
            b["r"] = []
        return tok

    def op(self, eng, fn, reads=(), writes=()):
        def tok_fn():
            self.cnt[eng] += 1
            return (self.esem[eng], self.cnt[eng], eng)
        return self._emit(eng, reads, writes, fn, tok_fn)

    def dma(self, q, fn, reads=(), writes=(), key=None):
        key = key if key is not None else (writes[0] if writes else reads[0])
        sem = self._dma_sem(("dma", key))
        def tok_fn():
            self.dcnt[("dma", key)] += 16
            return (sem, self.dcnt[("dma", key)], "dma")
        return self._emit(q, reads, writes, fn, tok_fn)

    def cc(self, fn, reads=(), writes=(), key="cc"):
        sem = self._dma_sem(("cc", key))
        def tok_fn():
            self.dcnt[("cc", key)] += 1
            return (sem, self.dcnt[("cc", key)], "cc")
        return self._emit("pool", reads, writes, fn, tok_fn)

    def finish(self, toks):
        self.final = list(toks)

    def barrier(self):
        for e in self.ALLQ:
            wl = []
            kn = self.known[e]
            for e2 in self.COMPUTE:
                if e2 == e or self.cnt[e2] == 0:
                    continue
                if kn.get(id(self.esem[e2]), 0) < self.cnt[e2]:
                    kn[id(self.esem[e2])] = self.cnt[e2]
                    wl.append((self.esem[e2], self.cnt[e2]))
            for k, sem in self.dsem.items():
                v = self.dcnt[k]
                if v and kn.get(id(sem), 0) < v:
                    kn[id(sem)] = v
                    wl.append((sem, v))
            self.ops[e].append((wl, None, None))
        self.bufs = {}

    def run(self):
        nc = self.nc
        engmap = {"pe": "tensor", "act": "scalar", "dve": "vector", "pool": "gpsimd", "sp": "sync"}
        with nc.Block() as block:
            for e in self.ALLQ:
                ops = self.ops[e]
                fin = self.final if e == "sp" else []
                if not ops and not fin:
                    continue
                def body(engine, ops=ops, fin=fin):
                    for wl, fn, tok in ops:
                        for s, v in wl:
                            engine.wait_ge(s, v)
                        if fn is not None:
                            fn(engine).then_inc(tok[0], 16 if tok[2] == "dma" else 1)
                    for s, v, _ in fin:
                        engine.wait_ge(s, v)
                getattr(block, engmap[e])(body)
        self.ops = {e: [] for e in self.ALLQ}
        self.final = []


D = 1024
DFF = 2816
NFT = DFF // 128
EPS = 1e-6

def L(f, *a, **k):
    return lambda e: getattr(e, f)(*a, **k)

class Ctx:
    def __init__(self, nc, S, st, pfx=""):
        self.nc, self.S, self.st, self.pfx = nc, S, st, pfx
        self.dyn = None
        self.n = 0
        self.rr = {}
    def sb(self, name, shape, dt):
        return self.st.enter_context(self.nc.sbuf_tensor("s_" + self.pfx + name, shape, dt))
    def ps(self, name, shape, dt):
        return self.st.enter_context(self.nc.psum_tensor("p_" + self.pfx + name, shape, dt))
    def dram(self, name, shape, dt):
        return self.nc.dram_tensor(name, shape, dt, kind="Internal").ap()
    def nxt(self, name, n):
        v = self.rr.get(name, 0)
        self.rr[name] = v + 1
        return v % n


def setup_common(C, ident_d):
    S = C.S
    C.ident = C.sb("ident", [128, 128], BF16)
    S.dma("pool", L("dma_start", out=C.ident[:], in_=ident_d[:, :]), writes=["ident"])
    C.stat = C.sb("stat", [128, 64], F32)
    C.NACC = 6
    C.acc = [C.ps(f"acc{i}", [128, 512], F32) for i in range(C.NACC)]
    C.pt = [C.ps(f"pt{i}", [128, 1024], BF16) for i in range(2)]
    C.hn = [C.sb(f"hn{i}", [128, 1024], BF16) for i in range(2)]


def load_gfull(C, name, g_d):
    S = C.S
    gcol = C.sb(name + "_col", [128, 8], F32)
    gfull = C.sb(name, [128, 8, 128], F32)
    S.dma("sp", L("dma_start", out=gcol[:], in_=g_d.rearrange("(c p) -> p c", p=128), allow_slow_non_contiguous=True), writes=[name + "_col"])
    S.op("dve", L("tensor_copy", out=gfull[:], in_=gcol[:].unsqueeze(2).to_broadcast([128, 8, 128])), reads=[name + "_col"], writes=[name])
    return gfull


def norm_transpose(C, x_ap, xkey, gfull, gkey, hT_ap, hkey):
    S = C.S
    i = C.nxt("stat", 16)
    ss, rs, rstd = C.stat[:, 3 * i:3 * i + 1], C.stat[:, 3 * i + 1:3 * i + 2], C.stat[:, 3 * i + 2:3 * i + 3]
    sk = f"stat{i}"
    b = C.nxt("hn", 2)
    hn, hnk = C.hn[b], f"hn{b}"
    pt, ptk = C.pt[b], f"pt{b}"
    S.op("act", L("activation", out=hn[:], in_=x_ap, func=AF.Square, accum_out=ss), reads=[xkey], writes=[hnk, sk])
    S.op("act", L("activation", out=rs, in_=ss, func=AF.Sqrt, scale=1.0 / D, bias=C.epsb[:, 0:1]), reads=[sk, "epsb"], writes=[sk])
    S.op("dve", L("reciprocal", out=rstd, in_=rs), reads=[sk], writes=[sk])
    S.op("act", L("activation", out=hn[:], in_=x_ap, func=AF.Copy, scale=rstd), reads=[xkey, sk], writes=[hnk])
    for c in range(8):
        S.op("pe", L("transpose", out=pt[:, c * 128:(c + 1) * 128], in_=hn[:, c * 128:(c + 1) * 128], identity=C.ident[:]), reads=[hnk, "ident"], writes=[ptk])
    S.op("dve", L("tensor_tensor", out=hT_ap, in0=pt[:].rearrange("p (c t) -> p c t", c=8), in1=gfull[:], op=ALU.mult), reads=[ptk, gkey], writes=[hkey])
    return rstd, sk


def convert_weights(C, w):
    S = C.S
    w16 = {}
    for k, N in (("wg", 2048), ("pa", D), ("pb", D), ("wo", D), ("fg", DFF), ("fu", DFF)):
        nblk = -(-N // 512)
        w16[k] = C.dram("b16_" + k, [nblk, 128, 8, 512], BF16)
        for nb in range(nblk):
            wc = min(512, N - nb * 512)
            for c in range(8):
                S.dma("pool", L("dma_start", out=w16[k][nb, :, c, 0:wc], in_=w[k][c * 128:(c + 1) * 128, nb * 512:nb * 512 + wc]), writes=["w16_" + k], key="w16_" + k)
    w16["fd"] = C.dram("b16_fd", [2, 128, NFT, 512], BF16)
    for n2 in range(2):
        for ft in range(NFT):
            S.dma("pool", L("dma_start", out=w16["fd"][n2, :, ft, :], in_=w["fd"][ft * 128:(ft + 1) * 128, n2 * 512:(n2 + 1) * 512]), writes=["w16_fd"], key="w16_fd")
    return w16


def dsl(C, e, t0, n):
    if C.dyn is None:
        return slice(t0, t0 + n)
    if "v" not in C.dyn:
        e.reg_load(C.dyn["reg"], C.dyn["qoff"][0:1, 0:1])
        C.dyn["v"] = e.snap(C.dyn["reg"])
    return bass.ds(C.dyn["v"] + t0, n)


def ab_src(C, e, src_d, t0, n):
    if C.dyn is None:
        return src_d[:, t0:t0 + n]
    if "vm" not in C.dyn:
        e.reg_load(C.dyn["mreg"], C.dyn["qm"][0:1, 0:1])
        C.dyn["vm"] = e.snap(C.dyn["mreg"])
    CH = C.dyn["CH"]
    c0 = t0 % CH
    return src_d[bass.ds(C.dyn["vm"] + (t0 // CH), 1), :, c0:c0 + n].rearrange("o r t -> (o r) t")


def phase_b(C, TB, x_d, oaT_d, obT_d, out_d, w, w16):
    nc, S = C.nc, C.S
    TT = 512
    NT = TB // TT
    gmix = load_gfull(C, "gmix", w["g_mix"])
    gffn = load_gfull(C, "gffn", w["g_ffn"])
    gfin = C.sb("gfin", [128, D], F32)
    S.dma("sp", L("dma_start", out=gfin[:], in_=w["g_fin"].partition_broadcast(128)), writes=["gfin"])
    C.epsb = C.sb("epsb", [128, 1], F32)
    S.op("pool", L("memset", C.epsb[:], EPS), writes=["epsb"])

    C.junk = C.sb("junk", [128, 1024], BF16)
    xres = C.sb("xres", [128, 4, D], F32)
    hT = C.sb("hT", [128, 8, TT], BF16)
    oaT = C.sb("oaT", [128, 8, TT], BF16)
    obT = C.sb("obT", [128, 8, TT], BF16)
    mT = C.sb("mT", [128, 8, TT], BF16)
    actT = C.sb("actT", [128, NFT, TT], BF16)
    NSL = 4
    slab = [C.sb(f"slab{i}", [128, 8, 512], BF16) for i in range(NSL)]
    dslab = [C.sb(f"dslab{i}", [128, NFT, 512], BF16) for i in range(2)]
    tmp = [C.sb(f"tmp{i}", [128, 512], F32) for i in range(4)]
    outb = C.sb("outb", [128, D], F32)

    def load_slab(wk, n0):
        i = C.nxt("slab", NSL)
        S.dma("sp", L("dma_start", out=slab[i][:], in_=w16[wk][n0 // 512]), reads=["w16_" + wk], writes=[f"slab{i}"])
        return slab[i], f"slab{i}"

    def mm_group(wk_slab, wkey, ncol, rhs_t, rkey):
        a = C.nxt("acc", C.NACC)
        for c in range(8):
            S.op("pe", L("matmul", C.acc[a][:, :], lhsT=wk_slab[:, c, ncol * 128:(ncol + 1) * 128], rhs=rhs_t[:, c, :], start=(c == 0), stop=(c == 7)),
                 reads=[wkey, rkey], writes=[f"acc{a}"])
        return C.acc[a], f"acc{a}"

    out_toks = []
    for t in range(NT):
        t0 = t * TT
        S.dma("sp", (lambda e, t0=t0: e.dma_start(out=xres[:], in_=x_d[dsl(C, e, t0, TT), :].rearrange("(j p) d -> p j d", p=128))), writes=["xres"])
        S.dma("sp", (lambda e, t0=t0: e.dma_start(out=oaT[:], in_=ab_src(C, e, oaT_d, t0, TT).rearrange("(c p) t -> p c t", p=128))), reads=["recv_a"], writes=["oaT"])
        S.dma("sp", (lambda e, t0=t0: e.dma_start(out=obT[:], in_=ab_src(C, e, obT_d, t0, TT).rearrange("(c p) t -> p c t", p=128))), reads=["recv_b"], writes=["obT"])
        for j in range(4):
            norm_transpose(C, xres[:, j, :], "xres", gmix, "gmix", hT[:, :, j * 128:(j + 1) * 128], "hT")
        for q4 in range(2):
            sga, kga = load_slab("wg", q4 * 512)
            sgb, kgb = load_slab("wg", 1024 + q4 * 512)
            spa, kpa = load_slab("pa", q4 * 512)
            spb, kpb = load_slab("pb", q4 * 512)
            for n in range(4):
                ct = q4 * 4 + n
                GA, kGA = mm_group(sga, kga, n, hT, "hT")
                PA, kPA = mm_group(spa, kpa, n, oaT, "oaT")
                GB, kGB = mm_group(sgb, kgb, n, hT, "hT")
                PB, kPB = mm_group(spb, kpb, n, obT, "obT")
                S.op("act", L("activation", out=tmp[0][:], in_=GA[:], func=AF.Sigmoid), reads=[kGA], writes=["tmp0"])
                S.op("act", L("activation", out=tmp[1][:], in_=GB[:], func=AF.Sigmoid), reads=[kGB], writes=["tmp1"])
                S.op("dve", L("tensor_tensor", out=tmp[2][:], in0=PA[:], in1=tmp[0][:], op=ALU.mult), reads=[kPA, "tmp0"], writes=["tmp2"])
                S.op("dve", L("tensor_tensor", out=tmp[3][:], in0=PB[:], in1=tmp[1][:], op=ALU.mult), reads=[kPB, "tmp1"], writes=["tmp3"])
                S.op("pool", L("tensor_tensor", out=mT[:, ct, :], in0=tmp[2][:], in1=tmp[3][:], op=ALU.add), reads=["tmp2", "tmp3"], writes=["mT"])
        for n2 in range(2):
            so, ko = load_slab("wo", n2 * 512)
            for j in range(4):
                a = C.nxt("acc", C.NACC)
                for c in range(8):
                    S.op("pe", L("matmul", C.acc[a][:, :], lhsT=mT[:, c, j * 128:(j + 1) * 128], rhs=so[:, c, :], start=(c == 0), stop=(c == 7)),
                         reads=["mT", ko], writes=[f"acc{a}"])
                S.op("dve", L("tensor_tensor", out=xres[:, j, n2 * 512:(n2 + 1) * 512], in0=C.acc[a][:], in1=xres[:, j, n2 * 512:(n2 + 1) * 512], op=ALU.add),
                     reads=[f"acc{a}", "xres"], writes=["xres"])
        for j in range(4):
            norm_transpose(C, xres[:, j, :], "xres", gffn, "gffn", hT[:, :, j * 128:(j + 1) * 128], "hT")
        for f4 in range(0, DFF, 512):
            wcols = min(512, DFF - f4)
            i1 = C.nxt("slab", NSL)
            S.dma("sp", L("dma_start", out=slab[i1][:, :, 0:wcols], in_=w16["fg"][f4 // 512][:, :, 0:wcols]), reads=["w16_fg"], writes=[f"slab{i1}"])
            i2 = C.nxt("slab", NSL)
            S.dma("sp", L("dma_start", out=slab[i2][:, :, 0:wcols], in_=w16["fu"][f4 // 512][:, :, 0:wcols]), reads=["w16_fu"], writes=[f"slab{i2}"])
            for n in range(wcols // 128):
                ft = f4 // 128 + n
                G, kG = mm_group(slab[i1], f"slab{i1}", n, hT, "hT")
                U, kU = mm_group(slab[i2], f"slab{i2}", n, hT, "hT")
                tb = C.nxt("tmpf", 2)
                S.op("act", L("activation", out=tmp[tb][:], in_=G[:], func=AF.Silu), reads=[kG], writes=[f"tmp{tb}"])
                S.op("dve", L("tensor_tensor", out=actT[:, ft, :], in0=U[:], in1=tmp[tb][:], op=ALU.mult), reads=[kU, f"tmp{tb}"], writes=["actT"])
        for n2 in range(2):
            di = C.nxt("dslab", 2)
            S.dma("sp", L("dma_start", out=dslab[di][:], in_=w16["fd"][n2]), reads=["w16_fd"], writes=[f"dslab{di}"])
            for j in range(4):
                a = C.nxt("acc", C.NACC)
                for ft in range(NFT):
                    S.op("pe", L("matmul", C.acc[a][:, :], lhsT=actT[:, ft, j * 128:(j + 1) * 128], rhs=dslab[di][:, ft, :], start=(ft == 0), stop=(ft == NFT - 1)),
                         reads=["actT", f"dslab{di}"], writes=[f"acc{a}"])
                S.op("dve", L("tensor_tensor", out=xres[:, j, n2 * 512:(n2 + 1) * 512], in0=C.acc[a][:], in1=xres[:, j, n2 * 512:(n2 + 1) * 512], op=ALU.add),
                     reads=[f"acc{a}", "xres"], writes=["xres"])
        for j in range(4):
            i = C.nxt("stat", 16)
            ss, rs, rstd = C.stat[:, 3 * i:3 * i + 1], C.stat[:, 3 * i + 1:3 * i + 2], C.stat[:, 3 * i + 2:3 * i + 3]
            sk = f"stat{i}"
            S.op("act", L("activation", out=C.junk[:], in_=xres[:, j, :], func=AF.Square, accum_out=ss), reads=["xres"], writes=["junk", sk])
            S.op("act", L("activation", out=rs, in_=ss, func=AF.Sqrt, scale=1.0 / D, bias=C.epsb[:, 0:1]), reads=[sk, "epsb"], writes=[sk])
            S.op("dve", L("reciprocal", out=rstd, in_=rs), reads=[sk], writes=[sk])
            S.op("dve", L("scalar_tensor_tensor", out=outb[:], in0=xres[:, j, :], scalar=rstd, in1=gfin[:], op0=ALU.mult, op1=ALU.mult), reads=["xres", sk, "gfin"], writes=["outb"])
            tk = S.dma("sp", L("dma_start", out=out_d[t0 + j * 128:t0 + (j + 1) * 128, :], in_=outb[:]), reads=["outb"], key="outd")
        out_toks = [tk]
    return out_toks


DILS = (1, 4, 16)

def blk_geom(di, b):
    d = DILS[di]
    if d == 1:
        return 128 * b, 1
    if d == 4:
        return 512 * (b // 4) + (b % 4), 4
    return b, 16

def prev_blk(di, b):
    d = DILS[di]
    if d == 1:
        return ("cur", b - 1) if b >= 1 else ("prev", 0)
    if d == 4:
        return ("cur", b - 4) if b >= 4 else ("prev", b)
    return ("prev", b)

def prev_src_blk(di, j):
    d = DILS[di]
    if d == 1:
        return 15
    if d == 4:
        return 12 + j
    return j
NPREV = (1, 4, 16)


def phase_a1(C, SA, x_d, wqkv_d, g_d, cosF_d, sinF_d, rperm_d, mask_d, out_ap, out_piece=2048, after_mc0=None):
    nc, S = C.nc, C.S
    MC = 2048
    NMC = SA // MC
    gmix = load_gfull(C, "gmix", g_d)
    C.epsb = C.sb("epsb", [128, 1], F32)
    S.op("pool", L("memset", C.epsb[:], EPS), writes=["epsb"])
    W = C.sb("Wqkv", [128, 8, 768], BF16)
    for c in range(8):
        S.dma("pool", L("dma_start", out=W[:, c, :], in_=wqkv_d[c * 128:(c + 1) * 128, :]), writes=["W"])
    rperm = C.sb("rperm", [128, 128], BF16)
    S.dma("pool", L("dma_start", out=rperm[:], in_=rperm_d[:, :]), writes=["rperm"])
    mask = C.sb("mask", [128, 512], BF16)
    S.dma("pool", L("dma_start", out=mask[:], in_=mask_d[:, :]), writes=["mask"])
    ones65 = C.sb("ones65", [128, 64], F32)
    S.op("pool", L("memset", ones65[:], 1.0), writes=["ones65"])

    xt = [C.sb(f"xt{i}", [128, D], F32) for i in range(2)]
    hT = C.sb("hT", [128, 8, 512], BF16)
    cosF = C.sb("cosF", [128, 512], F32)
    sinF = C.sb("sinF", [128, 512], F32)
    qraw = C.sb("qraw", [128, 512], BF16)
    t1 = C.sb("t1", [128, 512], F32)
    t2 = C.sb("t2", [128, 512], F32)
    qT = C.sb("qT", [128, 2, MC], BF16)
    kT = [C.sb(f"kT{i}", [128, 2, MC], BF16) for i in range(2)]
    vT = C.sb("vT", [128, 2, MC], BF16)
    Vc = [C.sb(f"Vc{di}", [128, 16, 4, 128], BF16) for di in range(3)]
    Vp = [C.sb(f"Vp{di}", [128, NPREV[di], 4, 128], BF16) for di in range(3)]
    ones3 = ones65[:].unsqueeze(1).to_broadcast([128, 4, 64])
    for di in range(3):
        for b in range(16):
            S.op("dve", L("tensor_copy", out=Vc[di][:, b, :, 64:128], in_=ones3), reads=["ones65"], writes=[f"Vc{di}"])
        for b in range(NPREV[di]):
            S.op("dve", L("tensor_copy", out=Vp[di][:, b, :, 64:128], in_=ones3), reads=["ones65"], writes=[f"Vp{di}"])
    oacc = [C.sb(f"oacc{i}", [128, MC], F32) for i in range(2)]
    pexp = [C.sb(f"pexp{i}", [128, 512], BF16) for i in range(2)]
    pm = [C.sb(f"pm{i}", [128, 512], BF16) for i in range(2)]
    rden = C.sb("rden", [128, 512], F32)
    rden2 = C.sb("rden2", [64, 512], F32)
    oout = [C.sb(f"oout{i}", [64, MC], BF16) for i in range(2)]

    toks = []
    for m in range(NMC):
        slot = m % 2
        kTc, kTk = kT[slot], f"kT{slot}"
        kTp, kTpk = kT[1 - slot], f"kT{1 - slot}"
        for sc in range(4):
            t0 = m * MC + sc * 512
            l0 = sc * 512
            for j in range(4):
                xb = C.nxt("xt", 2)
                S.dma("sp", L("dma_start", out=xt[xb][:], in_=x_d[t0 + j * 128:t0 + (j + 1) * 128, :]), writes=[f"xt{xb}"])
                norm_transpose(C, xt[xb][:], f"xt{xb}", gmix, "gmix", hT[:, :, j * 128:(j + 1) * 128], "hT")
            S.dma("act", L("dma_start", out=cosF[:], in_=cosF_d[:, t0:t0 + 512]), writes=["cosF"])
            S.dma("act", L("dma_start", out=sinF[:], in_=sinF_d[:, t0:t0 + 512]), writes=["sinF"])
            for ti in range(6):
                a = C.nxt("acc", C.NACC)
                for c in range(8):
                    S.op("pe", L("matmul", C.acc[a][:, :], lhsT=W[:, c, ti * 128:(ti + 1) * 128], rhs=hT[:, c, :], start=(c == 0), stop=(c == 7)),
                         reads=["W", "hT"], writes=[f"acc{a}"])
                hp = ti % 2
                if ti >= 4:
                    S.op("act", L("activation", out=vT[:, hp, l0:l0 + 512], in_=C.acc[a][:], func=AF.Copy), reads=[f"acc{a}"], writes=["vT"])
                    continue
                S.op("dve", L("tensor_copy", out=qraw[:], in_=C.acc[a][:]), reads=[f"acc{a}"], writes=["qraw"])
                a2 = C.nxt("acc", C.NACC)
                S.op("pe", L("matmul", C.acc[a2][:, :], lhsT=rperm[:], rhs=qraw[:], start=True, stop=True), reads=["rperm", "qraw"], writes=[f"acc{a2}"])
                S.op("dve", L("tensor_tensor", out=t1[:], in0=C.acc[a][:], in1=cosF[:], op=ALU.mult), reads=[f"acc{a}", "cosF"], writes=["t1"])
                S.op("dve", L("tensor_tensor", out=t2[:], in0=C.acc[a2][:], in1=sinF[:], op=ALU.mult), reads=[f"acc{a2}", "sinF"], writes=["t2"])
                if ti < 2:
                    dst, dk = qT[:, hp, l0:l0 + 512], "qT"
                else:
                    dst, dk = kTc[:, hp, l0:l0 + 512], kTk
                S.op("pool", L("tensor_tensor", out=dst, in0=t1[:], in1=t2[:], op=ALU.add), reads=["t1", "t2"], writes=[dk])
        for di in range(3):
            for b0 in range(0, 16, 4):
                pb = C.nxt("hn", 2)
                pt, ptk = C.pt[pb], f"pt{pb}"
                for bb in range(4):
                    base, st_ = blk_geom(di, b0 + bb)
                    for hp in range(2):
                        S.op("pe", L("transpose", out=pt[:, (bb * 2 + hp) * 128:(bb * 2 + hp + 1) * 128],
                                     in_=vT[:, hp, base:base + 127 * st_ + 1:st_], identity=C.ident[:]), reads=["vT", "ident"], writes=[ptk])
                for bb in range(4):
                    S.op("act", L("activation", out=Vc[di][:, b0 + bb, :, 0:64], in_=pt[:, bb * 256:(bb + 1) * 256].rearrange("p (h c) -> p h c", h=4), func=AF.Copy),
                         reads=[ptk], writes=[f"Vc{di}"])
        for h in range(4):
            hp, r0 = h // 2, 64 * (h % 2)
            ob = C.nxt("oacc", 2)
            oa, oak = oacc[ob], f"oacc{ob}"
            jobs = [(di, b0, pr) for di in range(3) for b0 in range(0, 16, 4) for pr in range(2)]
            state = {}

            def front(job):
                di, b0, pr = job
                if pr == 0:
                    pa = C.nxt("acc", C.NACC)
                    state[(di, b0)] = (C.acc[pa], f"acc{pa}")
                sa_ = C.nxt("acc", C.NACC)
                sc_, sck = C.acc[sa_], f"acc{sa_}"
                info = []
                for qq in range(2):
                    b = b0 + pr * 2 + qq
                    base, st_ = blk_geom(di, b)
                    qsl = qT[r0:r0 + 64, hp, base:base + 127 * st_ + 1:st_]
                    where, pbk = prev_blk(di, b)
                    has_prev = not (where == "prev" and m == 0)
                    if has_prev:
                        if where == "cur":
                            pbase, pst = blk_geom(di, pbk)
                            ksl, kk_ = kTc[r0:r0 + 64, hp, pbase:pbase + 127 * pst + 1:pst], kTk
                            vsl, vk_ = Vc[di][:, pbk, h, :], f"Vc{di}"
                        else:
                            pbase, pst = blk_geom(di, prev_src_blk(di, pbk))
                            ksl, kk_ = kTp[r0:r0 + 64, hp, pbase:pbase + 127 * pst + 1:pst], kTpk
                            vsl, vk_ = Vp[di][:, pbk, h, :], f"Vp{di}"
                        S.op("pe", L("matmul", sc_[:, qq * 256:qq * 256 + 128], lhsT=ksl, rhs=qsl, start=True, stop=True), reads=[kk_, "qT"], writes=[sck])
                    else:
                        vsl, vk_ = None, None
                    S.op("pe", L("matmul", sc_[:, qq * 256 + 128:qq * 256 + 256], lhsT=kTc[r0:r0 + 64, hp, base:base + 127 * st_ + 1:st_], rhs=qsl, start=True, stop=True),
                         reads=[kTk, "qT"], writes=[sck])
                    info.append((b, has_prev, vsl, vk_))
                pi = C.nxt("pexp", 2)
                S.op("act", L("activation", out=pexp[pi][:], in_=sc_[:], func=AF.Exp, scale=0.125), reads=[sck], writes=[f"pexp{pi}"])
                S.op("pool" if pi == 0 else "dve", L("tensor_tensor", out=pm[pi][:], in0=pexp[pi][:], in1=mask[:], op=ALU.mult), reads=[f"pexp{pi}", "mask"], writes=[f"pm{pi}"])
                state[job] = (pi, info)

            def back(job):
                di, b0, pr = job
                d = DILS[di]
                po, pok = state[(di, b0)]
                pi, info = state.pop(job)
                for qq in range(2):
                    b, has_prev, vsl, vk_ = info[qq]
                    col = (pr * 2 + qq) * 128
                    if has_prev:
                        S.op("pe", L("matmul", po[:, col:col + 128], lhsT=vsl, rhs=pm[pi][:, qq * 256:qq * 256 + 128], start=True, stop=False), reads=[vk_, f"pm{pi}"], writes=[pok])
                    S.op("pe", L("matmul", po[:, col:col + 128], lhsT=Vc[di][:, b, h, :], rhs=pm[pi][:, qq * 256 + 128:qq * 256 + 256], start=(not has_prev), stop=True),
                         reads=[f"Vc{di}", f"pm{pi}"], writes=[pok])
                if pr == 1:
                    if d == 1:
                        dst = oa[:, 128 * b0:128 * b0 + 512]
                        S.op("act", L("activation", out=dst, in_=po[:, :], func=AF.Copy), reads=[pok], writes=[oak])
                    else:
                        if d == 4:
                            dst = oa[:, 128 * b0:128 * b0 + 512].rearrange("p (i r) -> p r i", r=4)
                        else:
                            dst = oa[:, :].rearrange("p (i r) -> p r i", r=16)[:, b0:b0 + 4, :]
                        S.op("dve", L("tensor_tensor", out=dst, in0=po[:, :].rearrange("p (r i) -> p r i", r=4), in1=dst, op=ALU.add), reads=[pok, oak], writes=[oak])
                    state.pop((di, b0))

            for i in range(len(jobs) + 1):
                if i < len(jobs):
                    front(jobs[i])
                if i >= 1:
                    back(jobs[i - 1])
            oo, ook = oout[ob], f"oout{ob}"
            for q4 in range(4):
                S.op("dve", L("reciprocal", out=rden[64:128, :], in_=oa[64:128, q4 * 512:(q4 + 1) * 512]), reads=[oak], writes=["rden"])
                S.op("act", L("activation", out=rden2[:], in_=rden[64:128, :], func=AF.Copy), reads=["rden"], writes=["rden2"])
                S.op("dve", L("tensor_tensor", out=oo[:, q4 * 512:(q4 + 1) * 512], in0=oa[0:64, q4 * 512:(q4 + 1) * 512], in1=rden2[:], op=ALU.mult), reads=[oak, "rden2"], writes=[ook])
            for p0 in range(0, MC, out_piece):
                tk = S.dma("sp", L("dma_start", out=out_ap(slice(h * 64, (h + 1) * 64), m * MC + p0, out_piece), in_=oo[:, p0:p0 + out_piece]), reads=[ook], key="oaT_d" + str(ob))
                toks.append(tk)
        if m == 0 and after_mc0 is not None:
            after_mc0()
        for di in range(3):
            n = NPREV[di]
            sb0 = prev_src_blk(di, 0)
            for jj in range(n):
                S.op("dve", L("tensor_copy", out=Vp[di][:, jj, :, 0:64], in_=Vc[di][:, sb0 + jj, :, 0:64]), reads=[f"Vc{di}"], writes=[f"Vp{di}"])
    return toks[-8:]


C0 = 0.6065306597126334
GN_EPS = 64e-5
PV_MU, PV_W0, PV_A0, PV_KK, PV_KA, PV_RK, PV_LNW, PV_LNB, PV_1MKA, PV_N = 0, 8, 10, 12, 14, 16, 18, 20, 22, 24


def phase_a2(C, SA, x_d, wr_d, g_d, pv_d, w2a2_d, g2_d, cst_d, out_ap):
    nc, S = C.nc, C.S
    NSC = SA // 512
    gmix = load_gfull(C, "gmix", g_d)
    C.epsb = C.sb("epsb", [128, 1], F32)
    S.op("pool", L("memset", C.epsb[:], EPS), writes=["epsb"])
    gnb = C.sb("gnb", [128, 1], F32)
    S.op("pool", L("memset", gnb[:], GN_EPS), writes=["gnb"])
    W = C.sb("Wr", [128, 8, 1024], BF16)
    for c in range(8):
        S.dma("pool", L("dma_start", out=W[:, c, :], in_=wr_d[c * 128:(c + 1) * 128, :]), writes=["W"])
    pv = C.sb("pv", [128, PV_N], F32)
    S.dma("sp", L("dma_start", out=pv[:, 0:22], in_=pv_d[:, 0:22]), writes=["pv"])
    S.op("dve", L("tensor_scalar", out=pv[:, PV_1MKA:PV_1MKA + 2], in0=pv[:, PV_KA:PV_KA + 2], scalar1=-1.0, scalar2=1.0, op0=ALU.mult, op1=ALU.add), reads=["pv"], writes=["pv"])
    w2a2 = C.sb("w2a2", [128, 256], BF16)
    S.dma("pool", L("dma_start", out=w2a2[:], in_=w2a2_d[:, :]), writes=["w2a2"])
    g2 = C.sb("g2", [128, 256], BF16)
    S.dma("pool", L("dma_start", out=g2[:], in_=g2_d[:, :]), writes=["g2"])
    cst = C.sb("cst", [128, 1280], F32)
    S.dma("sp", L("dma_start", out=cst[:], in_=cst_d[:, :]), writes=["cst"])
    mreset = cst[:, 0:512]
    blkones = cst[:, 512:640]
    identf = cst[:, 1216:1280]
    maskA = C.sb("maskA", [128, 512], BF16)
    S.op("dve", L("tensor_copy", out=maskA[:], in_=cst[:, 640:1152]), reads=["cst"], writes=["maskA"])
    maskL = C.sb("maskL", [128, 8, 64], BF16)
    S.op("dve", L("tensor_copy", out=maskL[:], in_=cst[:, 1152:1216].unsqueeze(1).to_broadcast([128, 8, 64])), reads=["cst"], writes=["maskL"])
    identb = C.sb("identb", [128, 4, 64], BF16)
    S.op("dve", L("tensor_copy", out=identb[:], in_=identf.unsqueeze(1).to_broadcast([128, 4, 64])), reads=["cst"], writes=["identb"])

    xt = [C.sb(f"xt{i}", [128, D], F32) for i in range(2)]
    hT = C.sb("hT", [128, 8, 512], BF16)
    colsb = [C.sb(f"colsb{i}", [128, 513], F32) for i in range(8)]
    for i in range(8):
        S.op("pool", L("memset", colsb[i][:, 0:1], 0.0), writes=[f"colsb{i}"])
    xm = [C.sb(f"xm{i}", [128, 512], F32) for i in range(8)]
    tmpd = C.sb("tmpd", [128, 512], F32)
    lo1b = C.sb("lo1b", [128, 512], BF16)
    sg = C.sb("sg", [128, 512], BF16)

    def f32t(n):
        return C.sb(n, [128, 512], F32)
    def b16t(n):
        return C.sb(n, [128, 512], BF16)
    HPF32 = ("lwp", "ai", "gt", "kk", "sc1", "sc2", "kf", "bvec", "cwp", "cwm", "Epos", "Eprev", "Eneg", "yt")
    HPB16 = ("bT", "kTt", "BhT", "KhT", "vb")
    bufs = []
    for hp_ in range(2):
        d = {}
        for n in HPF32:
            d[n] = f32t(f"{n}_{hp_}")
        for n in HPB16:
            d[n] = b16t(f"{n}_{hp_}")
        d["AR"] = C.sb(f"AR_{hp_}", [128, 8, 128], BF16)
        d["AT"] = C.sb(f"AT_{hp_}", [128, 8, 256], BF16)
        d["PTt"] = C.sb(f"PTt_{hp_}", [128, 8, 64], BF16)
        d["Pl"] = [C.sb(f"Pl{i}_{hp_}", [128, 4, 64], BF16) for i in range(4)]
        d["PTl"] = [C.sb(f"PTl{i}_{hp_}", [128, 4, 64], BF16) for i in range(4)]
        d["Z"] = C.sb(f"Z_{hp_}", [128, 8, 64], BF16)
        for n in ("Bh_tok", "Kh_tok", "V_tok"):
            d[n] = C.sb(f"{n}_{hp_}", [128, 8, 64], BF16)
        d["YY"] = C.sb(f"YY_{hp_}", [128, 8, 128], BF16)
        d["XX"] = C.sb(f"XX_{hp_}", [128, 8, 128], BF16)
        d["MN"] = C.sb(f"MN_{hp_}", [128, 8, 128], F32)
        d["QcT"] = C.sb(f"QcT_{hp_}", [128, 8, 64], F32)
        bufs.append(d)
    HPKEYS = set(HPF32) | set(HPB16) | {"AR", "AT", "PTt", "Pl0", "Pl1", "Pl2", "Pl3", "PTl0", "PTl1", "PTl2", "PTl3", "Z", "Zh0", "Zh1", "Bh_tok", "Kh_tok", "V_tok", "YY", "XX", "MN", "QcT"}
    Hs = [C.sb(f"Hs{hp}", [128, 9, 64], F32) for hp in range(2)]
    for hp in range(2):
        S.op("pool", L("memset", Hs[hp][:], 0.0), writes=[f"Hs{hp}"])
    obuf = [C.sb(f"obuf{i}", [128, 512], BF16) for i in range(1)]

    def acc():
        a = C.nxt("acc", C.NACC)
        return C.acc[a], f"acc{a}"

    toks = []
    for sci in range(NSC):
        t0 = sci * 512
        for j in range(4):
            xb = C.nxt("xt", 2)
            S.dma("sp", L("dma_start", out=xt[xb][:], in_=x_d[t0 + j * 128:t0 + (j + 1) * 128, :]), writes=[f"xt{xb}"])
            norm_transpose(C, xt[xb][:], f"xt{xb}", gmix, "gmix", hT[:, :, j * 128:(j + 1) * 128], "hT")
        for ti in range(8):
            A, Ak = acc()
            for c in range(8):
                S.op("pe", L("matmul", A[:, :], lhsT=W[:, c, ti * 128:(ti + 1) * 128], rhs=hT[:, c, :], start=(c == 0), stop=(c == 7)), reads=["W", "hT"], writes=[Ak])
            cb, cbk = colsb[ti], f"colsb{ti}"
            S.op("act", L("activation", out=cb[:, 1:513], in_=A[:], func=AF.Copy), reads=[Ak], writes=[cbk])
            S.op("dve", L("tensor_tensor", out=tmpd[:], in0=cb[:, 0:512], in1=cb[:, 1:513], op=ALU.subtract), reads=[cbk], writes=["tmpd"])
            S.op("dve", L("scalar_tensor_tensor", out=xm[ti][:], in0=tmpd[:], scalar=pv[:, PV_MU + ti:PV_MU + ti + 1], in1=cb[:, 1:513], op0=ALU.mult, op1=ALU.add),
                 reads=["tmpd", "pv", cbk], writes=[f"xm{ti}"])
            S.op("pool", L("tensor_copy", out=cb[:, 0:1], in_=cb[:, 512:513]), reads=[cbk], writes=[cbk])
        S.op("act", L("activation", out=lo1b[0:64, :], in_=xm[6][0:64, :], func=AF.Tanh), reads=["xm6"], writes=["lo1b"])
        S.op("act", L("activation", out=lo1b[64:128, :], in_=xm[6][64:128, :], func=AF.Copy), reads=["xm6"], writes=["lo1b"])
        S.op("act", L("activation", out=sg[:], in_=xm[7][:], func=AF.Sigmoid), reads=["xm7"], writes=["sg"])
        def hp_body(hp):
            B_ = bufs[hp]
            lwp, ai, gt, kk, sc1, sc2, kf, bvec, cwp, cwm, Epos, Eprev, Eneg, yt = [B_[n] for n in HPF32]
            kkn, bonus = kk, lwp
            bT, kTt, BhT, KhT, vb = [B_[n] for n in HPB16]
            AR, AT, PTt, Pl, PTl, Z, Bh_tok, Kh_tok, V_tok, YY, XX, MN, QcT = [B_[n] for n in ("AR", "AT", "PTt", "Pl", "PTl", "Z", "Bh_tok", "Kh_tok", "V_tok", "YY", "XX", "MN", "QcT")]
            km = lambda k: (k + "_" + str(hp)) if k in HPKEYS else k
            def op(eng, fn, reads=(), writes=()):
                return S.op(eng, fn, [km(k) for k in reads], [km(k) for k in writes])
            r_, k_, v_ = xm[hp], xm[2 + hp], xm[4 + hp]
            rk_, kk_, vk_ = f"xm{hp}", f"xm{2 + hp}", f"xm{4 + hp}"
            col = lambda base: pv[:, base + hp:base + hp + 1]
            A, Ak = acc()
            op("pe", L("matmul", A[:, :], lhsT=w2a2[0:64, hp * 128:(hp + 1) * 128], rhs=lo1b[0:64, :], start=True, stop=True), reads=["w2a2", "lo1b"], writes=[Ak])
            op("act", L("activation", out=lwp[:], in_=A[:], func=AF.Sigmoid, bias=col(PV_W0)), reads=[Ak, "pv"], writes=["lwp"])
            A, Ak = acc()
            op("pe", L("matmul", A[:, :], lhsT=w2a2[64:128, hp * 128:(hp + 1) * 128], rhs=lo1b[64:128, :], start=True, stop=True), reads=["w2a2", "lo1b"], writes=[Ak])
            op("act", L("activation", out=ai[:], in_=A[:], func=AF.Sigmoid, bias=col(PV_A0)), reads=[Ak, "pv"], writes=["ai"])
            A, Ak = acc()
            op("pe", L("matmul", A[:, :], lhsT=g2[:, hp * 128:(hp + 1) * 128], rhs=sg[:], start=True, stop=True), reads=["g2", "sg"], writes=[Ak])
            op("act", L("activation", out=gt[:], in_=A[:], func=AF.Copy), reads=[Ak], writes=["gt"])
            yield
            op("dve", L("tensor_scalar", out=kk[:], in0=k_[:], scalar1=col(PV_KK), scalar2=None, op0=ALU.mult), reads=[kk_, "pv"], writes=["kk"])
            op("pool", L("tensor_tensor", out=sc1[:], in0=kk[:], in1=kk[:], op=ALU.mult), reads=["kk"], writes=["sc1"])
            A, Ak = acc()
            op("pe", L("matmul", A[:, :], lhsT=blkones, rhs=sc1[:], start=True, stop=True), reads=["cst", "sc1"], writes=[Ak])
            op("act", L("activation", out=sc2[:], in_=A[:], func=AF.Sqrt), reads=[Ak], writes=["sc2"])
            op("dve", L("tensor_scalar", out=sc2[:], in0=sc2[:], scalar1=1e-12, scalar2=None, op0=ALU.max), reads=["sc2"], writes=["sc2"])
            op("dve", L("reciprocal", out=sc1[:], in_=sc2[:]), reads=["sc2"], writes=["sc1"])
            op("dve", L("tensor_tensor", out=kkn[:], in0=kk[:], in1=sc1[:], op=ALU.mult), reads=["kk", "sc1"], writes=["kk"])
            yield
            op("dve", L("tensor_scalar", out=sc2[:], in0=ai[:], scalar1=col(PV_KA), scalar2=col(PV_1MKA), op0=ALU.mult, op1=ALU.add), reads=["ai", "pv"], writes=["sc2"])
            op("pool", L("tensor_tensor", out=kf[:], in0=sc2[:], in1=k_[:], op=ALU.mult), reads=["sc2", kk_], writes=["kf"])
            op("pool", L("tensor_tensor", out=bvec[:], in0=kkn[:], in1=ai[:], op=ALU.mult), reads=["kk", "ai"], writes=["bvec"])
            yield
            op("dve", L("tensor_tensor_scan", out=cwp[:], data0=mreset, data1=lwp[:], initial=0.0, op0=ALU.mult, op1=ALU.add), reads=["cst", "lwp"], writes=["cwp"])
            op("pool", L("tensor_tensor", out=cwm[:], in0=cwp[:], in1=lwp[:], op=ALU.subtract), reads=["cwp", "lwp"], writes=["cwm"])
            op("act", L("activation", out=Epos[:], in_=cwp[:], func=AF.Exp, scale=-C0), reads=["cwp"], writes=["Epos"])
            op("act", L("activation", out=Eprev[:], in_=cwm[:], func=AF.Exp, scale=-C0), reads=["cwm"], writes=["Eprev"])
            op("act", L("activation", out=Eneg[:], in_=cwp[:], func=AF.Exp, scale=C0), reads=["cwp"], writes=["Eneg"])
            v3 = lambda ap: ap.rearrange("p (c t) -> p c t", t=64)
            op("dve", L("scalar_tensor_tensor", out=AR[:, :, 0:64], in0=v3(kkn[:]), scalar=-1.0, in1=v3(Eprev[:]), op0=ALU.mult, op1=ALU.mult), reads=["kk", "Eprev"], writes=["AR"])
            op("pool", L("tensor_tensor", out=AR[:, :, 64:128], in0=v3(r_[:]), in1=v3(Epos[:]), op=ALU.mult), reads=[rk_, "Epos"], writes=["AR"])
            op("pool", L("tensor_tensor", out=bT[:], in0=bvec[:], in1=Eneg[:], op=ALU.mult), reads=["bvec", "Eneg"], writes=["bT"])
            op("dve", L("tensor_tensor", out=kTt[:], in0=kf[:], in1=Eneg[:], op=ALU.mult), reads=["kf", "Eneg"], writes=["kTt"])
            wcb = v3(Epos[:])[:, :, 63:64].to_broadcast([128, 8, 64])
            op("dve", L("tensor_tensor", out=v3(BhT[:]), in0=v3(bT[:]), in1=wcb, op=ALU.mult), reads=["bT", "Epos"], writes=["BhT"])
            op("pool", L("tensor_tensor", out=v3(KhT[:]), in0=v3(kTt[:]), in1=wcb, op=ALU.mult), reads=["kTt", "Epos"], writes=["KhT"])
            op("act", L("activation", out=vb[:], in_=v_[:], func=AF.Copy), reads=[vk_], writes=["vb"])
            yield
            op("dve", L("tensor_tensor", out=sc1[:], in0=r_[:], in1=kf[:], op=ALU.mult), reads=[rk_, "kf"], writes=["sc1"])
            op("dve", L("tensor_scalar", out=sc1[:], in0=sc1[:], scalar1=col(PV_RK), scalar2=None, op0=ALU.mult), reads=["sc1", "pv"], writes=["sc1"])
            A, Ak = acc()
            op("pe", L("matmul", A[:, :], lhsT=blkones, rhs=sc1[:], start=True, stop=True), reads=["cst", "sc1"], writes=[Ak])
            op("dve", L("tensor_tensor", out=bonus[:], in0=A[:], in1=v_[:], op=ALU.mult), reads=[Ak, vk_], writes=["lwp"])
            yield
            for src, sk, dst, dk, dsl in ((BhT, "BhT", Bh_tok, "Bh_tok", None), (KhT, "KhT", Kh_tok, "Kh_tok", None), (vb, "vb", V_tok, "V_tok", None), (None, "AR", YY, "YY", 1)):
                pb = C.nxt("hn", 2)
                pt, ptk = C.pt[pb], f"pt{pb}"
                for c in range(8):
                    for hd in range(2):
                        r0 = 64 * hd
                        in_ap = src[r0:r0 + 64, c * 64:(c + 1) * 64] if src is not None else AR[r0:r0 + 64, c, 0:64]
                        op("pe", L("transpose", out=pt[r0:r0 + 64, c * 64:(c + 1) * 64], in_=in_ap, identity=C.ident[r0:r0 + 64, r0:r0 + 64]), reads=[sk, "ident"], writes=[ptk])
                o = dst[:] if dsl is None else dst[:, :, 64:128]
                op("act", L("activation", out=o, in_=pt[:, 0:512].rearrange("p (c t) -> p c t", t=64), func=AF.Copy), reads=[ptk], writes=[dk])
                yield
            yield
            for c in range(0, 8, 2):
                A, Ak = acc()
                for cc in range(2):
                    for hd in range(2):
                        r0 = 64 * hd
                        op("pe", L("matmul", A[r0:r0 + 64, cc * 256:cc * 256 + 128], lhsT=bT[r0:r0 + 64, (c + cc) * 64:(c + cc + 1) * 64], rhs=AR[r0:r0 + 64, c + cc, :], start=True, stop=True), reads=["bT", "AR"], writes=[Ak])
                        op("pe", L("matmul", A[r0:r0 + 64, cc * 256 + 128:cc * 256 + 256], lhsT=kTt[r0:r0 + 64, (c + cc) * 64:(c + cc + 1) * 64], rhs=AR[r0:r0 + 64, c + cc, :], start=True, stop=True), reads=["kTt", "AR"], writes=[Ak])
                op("dve", L("tensor_tensor", out=AT[:, c:c + 2, :], in0=A[:].rearrange("p (c t) -> p c t", c=2), in1=maskA[:].rearrange("p (c t) -> p c t", c=2), op=ALU.mult), reads=[Ak, "maskA"], writes=["AT"])
            A, Ak = acc()
            for c in range(8):
                for hd in range(2):
                    r0 = 64 * hd
                    op("pe", L("matmul", A[r0:r0 + 64, c * 64:(c + 1) * 64], lhsT=AR[r0:r0 + 64, c, 0:64], rhs=bT[r0:r0 + 64, c * 64:(c + 1) * 64], start=True, stop=True), reads=["AR", "bT"], writes=[Ak])
            op("dve", L("tensor_tensor", out=PTt[:], in0=A[:].rearrange("p (c t) -> p c t", t=64), in1=maskL[:], op=ALU.mult), reads=[Ak, "maskL"], writes=["PTt"])
            yield
            hs = []
            for half in range(2):
                c0 = half * 4
                hs.append({"c0": c0, "Pv": (lambda c, c0=c0: AT[:, c0 + c, 0:64]), "Pk": "AT", "PTv": (lambda c, c0=c0: PTt[:, c0 + c, :]), "PTk": "PTt"})
                op("pool", L("tensor_tensor", out=Z[:, c0:c0 + 4, :], in0=AT[:, c0:c0 + 4, 0:64], in1=identb[:], op=ALU.add), reads=["AT", "identb"], writes=["Z"])
            for lvl in range(6):
                for half in range(2):
                    h_ = hs[half]
                    c0 = h_["c0"]
                    Pv, Pk, PTv, PTk = h_["Pv"], h_["Pk"], h_["PTv"], h_["PTk"]
                    Zh = Z[:, c0:c0 + 4, :]
                    Zk = f"Zh{half}"
                    AP_, APk = acc()
                    if lvl > 0:
                        AZ, AZk = acc()
                    for c in range(4):
                        for hd in range(2):
                            r0 = 64 * hd
                            p_, pt_ = Pv(c)[r0:r0 + 64], PTv(c)[r0:r0 + 64]
                            if lvl > 0:
                                op("pe", L("matmul", AZ[r0:r0 + 64, c * 64:(c + 1) * 64], lhsT=pt_, rhs=Z[r0:r0 + 64, c0 + c, :], start=True, stop=True), reads=[PTk, "Z", Zk], writes=[AZk])
                            if lvl < 5:
                                op("pe", L("matmul", AP_[r0:r0 + 64, c * 128:c * 128 + 64], lhsT=pt_, rhs=p_, start=True, stop=True), reads=[PTk, Pk], writes=[APk])
                                op("pe", L("matmul", AP_[r0:r0 + 64, c * 128 + 64:c * 128 + 128], lhsT=p_, rhs=pt_, start=True, stop=True), reads=[PTk, Pk], writes=[APk])
                    if lvl > 0:
                        op("dve", L("tensor_tensor", out=Zh, in0=AZ[:, 0:256].rearrange("p (c t) -> p c t", t=64), in1=Zh, op=ALU.add), reads=[AZk, "Z", Zk], writes=[Zk])
                    if lvl < 5:
                        nb = 2 * half + (lvl % 2)
                        op("act", L("activation", out=Pl[nb][:], in_=AP_[:].rearrange("p (c u t) -> p c u t", c=4, u=2)[:, :, 0, :], func=AF.Copy), reads=[APk], writes=[f"Pl{nb}"])
                        op("act", L("activation", out=PTl[nb][:], in_=AP_[:].rearrange("p (c u t) -> p c u t", c=4, u=2)[:, :, 1, :], func=AF.Copy), reads=[APk], writes=[f"PTl{nb}"])
                        h_["Pv"], h_["Pk"] = (lambda c, nb=nb: Pl[nb][:, c, :]), f"Pl{nb}"
                        h_["PTv"], h_["PTk"] = (lambda c, nb=nb: PTl[nb][:, c, :]), f"PTl{nb}"
                yield
            op("pool", L("tensor_copy", out=Z[:, 0:1, 0:1], in_=Z[:, 0:1, 0:1]), reads=["Zh0", "Zh1", "Z"], writes=["Z"])
            yield
            A, Ak = acc()
            for c in range(8):
                for hd in range(2):
                    r0 = 64 * hd
                    op("pe", L("matmul", A[r0:r0 + 64, c * 64:(c + 1) * 64], lhsT=AT[r0:r0 + 64, c, 128:192], rhs=V_tok[r0:r0 + 64, c, :], start=True, stop=True), reads=["AT", "V_tok"], writes=[Ak])
            op("act", L("activation", out=YY[:, :, 0:64], in_=A[:].rearrange("p (c t) -> p c t", t=64), func=AF.Copy), reads=[Ak], writes=["YY"])
            yield
            for c in range(0, 8, 4):
                A, Ak = acc()
                for cc in range(4):
                    for hd in range(2):
                        r0 = 64 * hd
                        op("pe", L("matmul", A[r0:r0 + 64, cc * 128:(cc + 1) * 128], lhsT=Z[r0:r0 + 64, c + cc, :], rhs=YY[r0:r0 + 64, c + cc, :], start=True, stop=True), reads=["Z", "YY"], writes=[Ak])
                op("dve", L("tensor_copy", out=XX[:, c:c + 4, :], in_=A[:].rearrange("p (c t) -> p c t", t=128)), reads=[Ak], writes=["XX"])
            yield
            for c in range(0, 8, 4):
                A, Ak = acc()
                for cc in range(4):
                    for hd in range(2):
                        r0 = 64 * hd
                        ci = c + cc
                        op("pe", L("matmul", A[r0:r0 + 64, cc * 128:cc * 128 + 64], lhsT=XX[r0:r0 + 64, ci, 64:128], rhs=Bh_tok[r0:r0 + 64, ci, :], start=True, stop=True), reads=["XX", "Bh_tok"], writes=[Ak])
                        op("pe", L("matmul", A[r0:r0 + 64, cc * 128 + 64:cc * 128 + 128], lhsT=Bh_tok[r0:r0 + 64, ci, :], rhs=XX[r0:r0 + 64, ci, 0:64], start=True, stop=False), reads=["XX", "Bh_tok"], writes=[Ak])
                        op("pe", L("matmul", A[r0:r0 + 64, cc * 128 + 64:cc * 128 + 128], lhsT=Kh_tok[r0:r0 + 64, ci, :], rhs=V_tok[r0:r0 + 64, ci, :], start=False, stop=True), reads=["Kh_tok", "V_tok"], writes=[Ak])
                op("act", L("activation", out=MN[:, c:c + 4, :], in_=A[:].rearrange("p (c t) -> p c t", t=128), func=AF.Copy), reads=[Ak], writes=["MN"])
            for c in range(8):
                op("dve", L("scalar_tensor_tensor", out=MN[:, c, 0:64], in0=identf, scalar=Epos[:, c * 64 + 63:c * 64 + 64], in1=MN[:, c, 0:64], op0=ALU.mult, op1=ALU.add), reads=["cst", "Epos", "MN"], writes=["MN"])
            yield
            A, Ak = acc()
            for c in range(8):
                for hd in range(2):
                    r0 = 64 * hd
                    op("pe", L("matmul", A[r0:r0 + 64, c * 64:(c + 1) * 64], lhsT=XX[r0:r0 + 64, c, 64:128], rhs=AT[r0:r0 + 64, c, 64:128], start=True, stop=True), reads=["XX", "AT"], writes=[Ak])
            op("dve", L("tensor_tensor", out=QcT[:], in0=A[:].rearrange("p (c t) -> p c t", t=64), in1=AR[:, :, 64:128], op=ALU.add), reads=[Ak, "AR"], writes=["QcT"])
            yield
            H, Hk = Hs[hp], f"Hs{hp}"
            for c in range(8):
                A, Ak = acc()
                for hd in range(2):
                    r0 = 64 * hd
                    op("pe", L("matmul", A[r0:r0 + 64, 0:64], lhsT=MN[r0:r0 + 64, c, 0:64], rhs=H[r0:r0 + 64, c, :], start=True, stop=True), reads=["MN", Hk], writes=[Ak])
                op("dve", L("tensor_tensor", out=H[:, c + 1, :], in0=A[:, 0:64], in1=MN[:, c, 64:128], op=ALU.add), reads=[Ak, "MN"], writes=[Hk])
                yield
            yield
            A, Ak = acc()
            for c in range(8):
                for hd in range(2):
                    r0 = 64 * hd
                    o = A[r0:r0 + 64, c * 64:(c + 1) * 64]
                    op("pe", L("matmul", o, lhsT=H[r0:r0 + 64, c, :], rhs=QcT[r0:r0 + 64, c, :], start=True, stop=False), reads=[Hk, "QcT"], writes=[Ak])
                    op("pe", L("matmul", o, lhsT=XX[r0:r0 + 64, c, 0:64], rhs=AT[r0:r0 + 64, c, 64:128], start=False, stop=False), reads=["XX", "AT"], writes=[Ak])
                    op("pe", L("matmul", o, lhsT=V_tok[r0:r0 + 64, c, :], rhs=AT[r0:r0 + 64, c, 192:256], start=False, stop=True), reads=["V_tok", "AT"], writes=[Ak])
            op("act", L("activation", out=yt[:], in_=A[:], func=AF.Copy), reads=[Ak], writes=["yt"])
            op("pool", L("tensor_copy", out=H[:, 0, :], in_=H[:, 8, :]), reads=[Hk], writes=[Hk])
            yield
            op("pool", L("tensor_tensor", out=sc1[:], in0=yt[:], in1=yt[:], op=ALU.mult), reads=["yt"], writes=["sc1"])
            A1, A1k = acc()
            op("pe", L("matmul", A1[:, :], lhsT=blkones, rhs=yt[:], start=True, stop=True), reads=["cst", "yt"], writes=[A1k])
            A2, A2k = acc()
            op("pe", L("matmul", A2[:, :], lhsT=blkones, rhs=sc1[:], start=True, stop=True), reads=["cst", "sc1"], writes=[A2k])
            op("act", L("activation", out=sc2[:], in_=A1[:], func=AF.Copy, scale=1.0 / 64), reads=[A1k], writes=["sc2"])
            op("pool", L("tensor_tensor", out=sc1[:], in0=sc2[:], in1=sc2[:], op=ALU.mult), reads=["sc2"], writes=["sc1"])
            op("dve", L("scalar_tensor_tensor", out=sc1[:], in0=A2[:], scalar=1.0 / 64, in1=sc1[:], op0=ALU.mult, op1=ALU.subtract), reads=[A2k, "sc1"], writes=["sc1"])
            op("act", L("activation", out=sc1[:], in_=sc1[:], func=AF.Sqrt, bias=gnb[:, 0:1]), reads=["sc1", "gnb"], writes=["sc1"])
            op("dve", L("reciprocal", out=cwm[:], in_=sc1[:]), reads=["sc1"], writes=["cwm"])
            op("pool", L("tensor_tensor", out=yt[:], in0=yt[:], in1=sc2[:], op=ALU.subtract), reads=["yt", "sc2"], writes=["yt"])
            op("dve", L("tensor_tensor", out=yt[:], in0=yt[:], in1=cwm[:], op=ALU.mult), reads=["yt", "cwm"], writes=["yt"])
            op("dve", L("tensor_scalar", out=yt[:], in0=yt[:], scalar1=col(PV_LNW), scalar2=col(PV_LNB), op0=ALU.mult, op1=ALU.add), reads=["yt", "pv"], writes=["yt"])
            op("pool", L("tensor_tensor", out=yt[:], in0=yt[:], in1=bonus[:], op=ALU.add), reads=["yt", "lwp"], writes=["yt"])
            ob = C.nxt("obuf", 1)
            op("pool", L("tensor_tensor", out=obuf[ob][:], in0=yt[:], in1=gt[:], op=ALU.mult), reads=["yt", "gt"], writes=[f"obuf{ob}"])
            tk = S.dma("sp", L("dma_start", out=out_ap(slice(hp * 128, (hp + 1) * 128), t0, 512), in_=obuf[ob][:]), reads=[f"obuf{ob}"], key=f"obT_d{ob}")
            toks.append(tk)
        gens = [hp_body(0), hp_body(1)]
        while gens:
            for g_ in list(gens):
                try:
                    next(g_)
                except StopIteration:
                    gens.remove(g_)
    return toks[-2:]


import contextlib

def attn_consts(SA):
    half = 8
    inv = (500000.0 ** (-np.arange(half, dtype=np.float32) * np.float32(2.0 / 16))).astype(np.float32)
    ang = (np.arange(SA, dtype=np.float32)[None, :] * inv[:, None]).astype(np.float32)
    cosF = np.ones((128, SA), np.float32); sinF = np.zeros((128, SA), np.float32)
    rperm = np.zeros((128, 128), np.float32)
    for hb in (0, 64):
        cosF[hb:hb + 8] = np.cos(ang); cosF[hb + 8:hb + 16] = np.cos(ang)
        sinF[hb:hb + 8] = -np.sin(ang); sinF[hb + 8:hb + 16] = np.sin(ang)
        for i in range(8):
            rperm[hb + i + 8, hb + i] = 1.0
            rperm[hb + i, hb + i + 8] = 1.0
    j = np.arange(128)[:, None]; i = np.arange(128)[None, :]
    mprev = (j >= i).astype(np.float32); mcur = (j <= i).astype(np.float32)
    mask = np.ascontiguousarray(np.concatenate([mprev, mcur, mprev, mcur], 1))
    return cosF, sinF, rperm, mask

def rwkv_consts():
    p = np.arange(128)
    mreset = np.ones((128, 512), np.float32); mreset[:, ::64] = 0.0
    blk = (p[:, None] // 64 == p[None, :] // 64).astype(np.float32)
    s = (p % 64)[:, None]; t = np.arange(64)[None, :]
    MU = (s < t).astype(np.float32); MUI = (s <= t).astype(np.float32)
    maskA = np.concatenate([MU, MUI, MU, MUI, MU, MUI, MU, MUI], 1)
    ML = (t < s).astype(np.float32)
    identf = (s == t).astype(np.float32)
    return np.ascontiguousarray(np.concatenate([mreset, blk, maskA, ML, identf], 1))

def rwkv_inputs(inp, hg):
    w_in = inp["w_in"][0]
    cs = slice(hg * 256, (hg + 1) * 256)
    sh = 3072
    cols = np.concatenate([np.arange(sh + hg * 256, sh + hg * 256 + 256), np.arange(sh + 1024 + hg * 256, sh + 1024 + hg * 256 + 256),
                           np.arange(sh + 2048 + hg * 256, sh + 2048 + hg * 256 + 256), np.arange(sh + 3072, sh + 3072 + 256)])
    wr = np.ascontiguousarray(w_in[:, cols])
    mu = inp["shift_mu"][0][cols - sh]
    pv = np.zeros((128, 22), np.float32)
    pv[:, 0:8] = mu.reshape(8, 128).T
    for k, name in ((8, "decay_w0"), (10, "iclr_a0"), (12, "k_k"), (14, "k_a"), (16, "r_k"), (18, "ln_x_w"), (20, "ln_x_b")):
        v = inp[name][0].reshape(-1)[cs]
        pv[:, k:k + 2] = v.reshape(2, 128).T
    w2a2 = np.ascontiguousarray(np.concatenate([inp["decay_w2"][0][:, cs], inp["iclr_a2"][0][:, cs]], 0))
    g2 = np.ascontiguousarray(inp["gate_g2"][0][:, cs])
    return {"wr": wr, "pv": pv, "w2a2": w2a2, "g2": g2}

def attn_inputs(inp, hg):
    w_in = inp["w_in"][0]
    cols = np.concatenate([w_in[:, hg * 256:(hg + 1) * 256], w_in[:, 1024 + hg * 256:1024 + (hg + 1) * 256], w_in[:, 2048 + hg * 256:2048 + (hg + 1) * 256]], 1)
    return {"wqkv": np.ascontiguousarray(cols)}


def build_p1(SA):
    nc = bass.Bass("TRN2", target_bir_lowering=False)
    di = lambda n, s, d=F32: nc.dram_tensor(n, s, d, kind="ExternalInput").ap()
    x_d = di("x", [SA, 1024]); ident = di("ident", [128, 128]); g = di("g_mix", [1024])
    wqkv = di("wqkv", [1024, 768]); cosF = di("cosF", [128, SA]); sinF = di("sinF", [128, SA]); rperm = di("rperm", [128, 128]); mask = di("mask", [128, 512])
    wr = di("wr", [1024, 1024]); pv = di("pv", [128, 22]); w2a2 = di("w2a2", [128, 256]); g2 = di("g2", [128, 256]); cst = di("cst", [128, 1280])
    oaT = nc.dram_tensor("oaT", [256, SA], BF16, kind="ExternalOutput").ap()
    obT = nc.dram_tensor("obT", [256, SA], BF16, kind="ExternalOutput").ap()
    with contextlib.ExitStack() as st0:
        S = Sched(nc, st0)
        with contextlib.ExitStack() as st:
            C = Ctx(nc, S, st, "a1_")
            setup_common(C, ident)
            toks1 = phase_a1(C, SA, x_d, wqkv, g, cosF, sinF, rperm, mask, (lambda rows, g0, n: oaT[rows, g0:g0 + n]))
            S.run()
        S.barrier()
        with contextlib.ExitStack() as st:
            C = Ctx(nc, S, st, "a2_")
            setup_common(C, ident)
            toks2 = phase_a2(C, SA, x_d, wr, g, pv, w2a2, g2, cst, (lambda rows, g0, n: obT[rows, g0:g0 + n]))
            S.finish(list(toks1) + list(toks2))
            S.run()
    return nc


def build_p2(TB):
    nc = bass.Bass("TRN2", target_bir_lowering=False)
    di = lambda n, s, d=F32: nc.dram_tensor(n, s, d, kind="ExternalInput").ap()
    x_d = di("x", [TB, 1024]); oa = di("oaT", [1024, TB], BF16); ob = di("obT", [1024, TB], BF16)
    ident = di("ident", [128, 128])
    w = {"wg": di("wg", [1024, 2048]), "pa": di("pa", [1024, 1024]), "pb": di("pb", [1024, 1024]), "wo": di("wo", [1024, 1024]),
         "fg": di("fg", [1024, 2816]), "fu": di("fu", [1024, 2816]), "fd": di("fd", [2816, 1024]),
         "g_mix": di("g_mix", [1024]), "g_ffn": di("g_ffn", [1024]), "g_fin": di("g_fin", [1024])}
    out_d = nc.dram_tensor("out", [TB, 1024], F32, kind="ExternalOutput").ap()
    with contextlib.ExitStack() as st:
        S = Sched(nc, st)
        C = Ctx(nc, S, st)
        setup_common(C, ident)
        w16 = convert_weights(C, w)
        toks = phase_b(C, TB, x_d, oa, ob, out_d, w, w16)
        S.finish(toks)
        S.run()
    return nc


def build_fused(SA, TB):
    nc = bass.Bass("TRN2", target_bir_lowering=False)
    di = lambda n, s, d=F32: nc.dram_tensor(n, s, d, kind="ExternalInput").ap()
    x_d = di("x", [SA, 1024]); ident = di("ident", [128, 128]); g = di("g_mix", [1024])
    wqkv = di("wqkv", [1024, 768]); cosF = di("cosF", [128, SA]); sinF = di("sinF", [128, SA]); rperm = di("rperm", [128, 128]); mask = di("mask", [128, 512])
    wr = di("wr", [1024, 1024]); pv = di("pv", [128, 22]); w2a2 = di("w2a2", [128, 256]); g2 = di("g2", [128, 256]); cst = di("cst", [128, 1280])
    qoff = di("qoff", [1, 1], mybir.dt.int32)
    w = {"wg": di("wg", [1024, 2048]), "pa": di("pa", [1024, 1024]), "pb": di("pb", [1024, 1024]), "wo": di("wo", [1024, 1024]),
         "fg": di("fg", [1024, 2816]), "fu": di("fu", [1024, 2816]), "fd": di("fd", [2816, 1024]),
         "g_mix": g, "g_ffn": di("g_ffn", [1024]), "g_fin": di("g_fin", [1024])}
    out_d = nc.dram_tensor("out", [TB, 1024], F32, kind="ExternalOutput").ap()
    NG = 4
    CH = min(2048, TB)
    NCH = SA // CH
    send_a = nc.dram_tensor("send_a", [NCH, 256, CH], BF16, kind="Internal").ap()
    send_b = nc.dram_tensor("send_b", [NCH, 256, CH], BF16, kind="Internal").ap()
    recv_a = nc.dram_tensor("recv_a", [NCH, NG * 256, CH], BF16, kind="Internal").ap()
    recv_b = nc.dram_tensor("recv_b", [NCH, NG * 256, CH], BF16, kind="Internal").ap()
    qm = di("qm", [1, 1], mybir.dt.int32)
    groups = [[0, 1, 2, 3], [4, 5, 6, 7]]
    def chunk_ap(t):
        return lambda rows, g0, n: t[g0 // CH, rows, g0 % CH:g0 % CH + n]
    with contextlib.ExitStack() as st0:
        S = Sched(nc, st0)
        qreg = st0.enter_context(nc.sync.register("qreg"))
        mreg = st0.enter_context(nc.sync.register("mreg"))
        C0 = Ctx(nc, S, st0, "w_")
        w16 = {}
        with contextlib.ExitStack() as st:
            C = Ctx(nc, S, st, "a1_")
            setup_common(C, ident)
            phase_a1(C, SA, x_d, wqkv, g, cosF, sinF, rperm, mask, chunk_ap(send_a), out_piece=min(CH, 2048),
                     after_mc0=lambda: w16.update(convert_weights(C0, w)))
            S.run()
        S.barrier()
        for m in range(NCH):
            S.cc(L("collective_compute", "AllGather", ALU.bypass, replica_groups=groups, ins=[send_a[m]], outs=[recv_a[m]]), key="cca")
        with contextlib.ExitStack() as st:
            C = Ctx(nc, S, st, "a2_")
            setup_common(C, ident)
            phase_a2(C, SA, x_d, wr, g, pv, w2a2, g2, cst, chunk_ap(send_b))
            S.run()
        S.barrier()
        for m in range(NCH):
            S.cc(L("collective_compute", "AllGather", ALU.bypass, replica_groups=groups, ins=[send_b[m]], outs=[recv_b[m]]), key="ccb")
        S.barrier()
        with contextlib.ExitStack() as st:
            C = Ctx(nc, S, st, "b_")
            C.dyn = {"reg": qreg, "qoff": qoff, "mreg": mreg, "qm": qm, "CH": CH}
            setup_common(C, ident)
            toks = phase_b(C, TB, x_d, recv_a, recv_b, out_d, w, w16)
            S.finish(toks)
            S.run()
    return nc


def kernel_unfused(**inputs):
    inp = {k: np.asarray(v) for k, v in inputs.items()}
    x = inp["x"]
    NBATCH, SEQ, _ = x.shape
    NQ = 8 // NBATCH
    TB = SEQ // NQ
    cosF, sinF, rperm, mask = attn_consts(SEQ)
    cst = rwkv_consts()
    ident = np.eye(128, dtype=np.float32)
    nc1 = build_p1(SEQ)
    maps1 = []
    for c in range(8):
        b, hg = c // NQ, c % NQ
        m = {"x": np.ascontiguousarray(x[b]), "ident": ident, "g_mix": inp["norm_mix_g"][0], "cosF": cosF, "sinF": sinF, "rperm": rperm, "mask": mask, "cst": cst}
        m.update(attn_inputs(inp, hg))
        m.update(rwkv_inputs(inp, hg))
        maps1.append(m)
    res1 = run_bass_kernel_spmd(nc1, maps1, core_ids=list(range(8)))
    nc2 = build_p2(TB)
    wg = np.ascontiguousarray(inp["w_in"][0][:, 8448 - 2048:])
    maps2 = []
    for c in range(8):
        b, q = c // NQ, c % NQ
        oa = np.concatenate([res1.results[b * NQ + hg]["oaT"][:, q * TB:(q + 1) * TB] for hg in range(NQ)], 0)
        ob = np.concatenate([res1.results[b * NQ + hg]["obT"][:, q * TB:(q + 1) * TB] for hg in range(NQ)], 0)
        maps2.append({"x": np.ascontiguousarray(x[b, q * TB:(q + 1) * TB]), "oaT": np.ascontiguousarray(oa), "obT": np.ascontiguousarray(ob), "ident": ident,
                      "wg": wg, "pa": inp["proj_attn"][0], "pb": inp["proj_rwkv"][0], "wo": inp["w_out"][0],
                      "fg": inp["ffn_w_gate"][0], "fu": inp["ffn_w_up"][0], "fd": inp["ffn_w_down"][0],
                      "g_mix": inp["norm_mix_g"][0], "g_ffn": inp["norm_ffn_g"][0], "g_fin": inp["norm_final_g"]})
    res2 = run_bass_kernel_spmd(nc2, maps2, core_ids=list(range(8)))
    out = np.zeros((NBATCH, SEQ, 1024), np.float32)
    for c in range(8):
        b, q = c // NQ, c % NQ
        out[b, q * TB:(q + 1) * TB] = res2.results[c]["out"]
    return out


def kernel(**inputs):
    inp = {k: np.asarray(v) for k, v in inputs.items()}
    x = inp["x"]
    NBATCH, SEQ, _ = x.shape
    NQ = 8 // NBATCH
    TB = SEQ // NQ
    cosF, sinF, rperm, mask = attn_consts(SEQ)
    cst = rwkv_consts()
    ident = np.eye(128, dtype=np.float32)
    nc = build_fused(SEQ, TB)
    wg = np.ascontiguousarray(inp["w_in"][0][:, 8448 - 2048:])
    maps = []
    for c in range(8):
        b, q = c // NQ, c % NQ
        m = {"x": np.ascontiguousarray(x[b]), "ident": ident, "g_mix": inp["norm_mix_g"][0], "cosF": cosF, "sinF": sinF, "rperm": rperm, "mask": mask, "cst": cst,
             "qoff": np.array([[q * TB]], np.int32), "qm": np.array([[q * (TB // min(2048, TB))]], np.int32),
             "wg": wg, "pa": inp["proj_attn"][0], "pb": inp["proj_rwkv"][0], "wo": inp["w_out"][0],
             "fg": inp["ffn_w_gate"][0], "fu": inp["ffn_w_up"][0], "fd": inp["ffn_w_down"][0],
             "g_ffn": inp["norm_ffn_g"][0], "g_fin": inp["norm_final_g"]}
        m.update(attn_inputs(inp, q))
        m.update(rwkv_inputs(inp, q))
        maps.append(m)
    res = run_bass_kernel_spmd(nc, maps, core_ids=list(range(8)))
    out = np.zeros((NBATCH, SEQ, 1024), np.float32)
    for c in range(8):
        b, q = c // NQ, c % NQ
        out[b, q * TB:(q + 1) * TB] = res.results[c]["out"]
    return out
```

```python
import os
import numpy as np
import concourse.bass as bass
import concourse.mybir as mybir
from concourse.bass_utils import run_bass_kernel_spmd

F32 = mybir.dt.float32
BF16 = mybir.dt.bfloat16
AF = mybir.ActivationFunctionType
ALU = mybir.AluOpType
AX = mybir.AxisListType

class Sched:
    COMPUTE = ("pe", "act", "dve", "pool")
    ALLQ = ("pe", "act", "dve", "pool", "sp")

    def __init__(self, nc, stack):
        self.nc = nc
        self.stack = stack
        self.ops = {e: [] for e in self.ALLQ}
        self.cnt = {e: 0 for e in self.COMPUTE}
        self.esem = {e: stack.enter_context(nc.semaphore("prog_" + e)) for e in self.COMPUTE}
        self.known = {e: {} for e in self.ALLQ}
        self.bufs = {}
        self.dsem = {}
        self.dcnt = {}
        self.final = []

    def _buf(self, k):
        b = self.bufs.get(k)
        if b is None:
            b = self.bufs[k] = {"w": [], "r": []}
        return b

    def _dma_sem(self, k):
        if k not in self.dsem:
            self.dsem[k] = self.stack.enter_context(self.nc.semaphore("d_" + str(len(self.dsem))))
            self.dcnt[k] = 0
        return self.dsem[k]

    def _emit(self, eng, reads, writes, fn, tok_fn):
        waits = {}
        def need(tok):
            s, v, src = tok
            if src == eng and eng == "pe":
                return
            if waits.get(s, (0,))[0] < v:
                waits[s] = (v, src)
        for k in reads:
            for t in self._buf(k)["w"]:
                need(t)
        for k in writes:
            b = self._buf(k)
            for t in b["w"]:
                need(t)
            for t in b["r"]:
                need(t)
        wl = []
        kn = self.known[eng]
        for s, (v, src) in waits.items():
            if kn.get(id(s), 0) >= v:
                continue
            kn[id(s)] = v
            wl.append((s, v))
        tok = tok_fn()
        self.ops[eng].append((wl, fn, tok))
        for k in reads:
            b = self._buf(k)
            b["r"] = [t for t in b["r"] if not (t[2] == eng and eng in self.COMPUTE)] + [tok]
        for k in writes:
            b = self._buf(k)
            b["w"] = [tok]
            b["r"] = []
        return tok

    def op(self, eng, fn, reads=(), writes=()):
        def tok_fn():
            self.cnt[eng] += 1
            return (self.esem[eng], self.cnt[eng], eng)
        return self._emit(eng, reads, writes, fn, tok_fn)

    def dma(self, q, fn, reads=(), writes=(), key=None):
        key = key if key is not None else (writes[0] if writes else reads[0])
        sem = self._dma_sem(("dma", key))
        def tok_fn():
            self.dcnt[("dma", key)] += 16
            return (sem, self.dcnt[("dma", key)], "dma")
        return self._emit(q, reads, writes, fn, tok_fn)

    def cc(self, fn, reads=(), writes=(), key="cc"):
        sem = self._dma_sem(("cc", key))
        def tok_fn():
            self.dcnt[("cc", key)] += 1
            return (sem, self.dcnt[("cc", key)], "cc")
        return self._emit("pool", reads, writes, fn, tok_fn)

    def finish(self, toks):
        self.final = list(toks)

    def barrier(self):
        for e in self.ALLQ:
            wl = []
            kn = self.known[e]
            for e2 in self.COMPUTE:
                if e2 == e or self.cnt[e2] == 0:
                    continue
                if kn.get(id(self.esem[e2]), 0) < self.cnt[e2]:
                    kn[id(self.esem[e2])] = self.cnt[e2]
                    wl.append((self.esem[e2], self.cnt[e2]))
            for k, sem in self.dsem.items():
                v = self.dcnt[k]
                if v and kn.get(id(sem), 0) < v:
                    kn[id(sem)] = v
                    wl.append((sem, v))
            self.ops[e].append((wl, None, None))
        self.bufs = {}

    def run(self):
        nc = self.nc
        engmap = {"pe": "tensor", "act": "scalar", "dve": "vector", "pool": "gpsimd", "sp": "sync"}
        with nc.Block() as block:
            for e in self.ALLQ:
                ops = self.ops[e]
                fin = self.final if e == "sp" else []
                if not ops and not fin:
                    continue
                def body(engine, ops=ops, fin=fin):
                    for wl, fn, tok in ops:
                        for s, v in wl:
                            engine.wait_ge(s, v)
                        if fn is not None:
                            fn(engine).then_inc(tok[0], 16 if tok[2] == "dma" else 1)
                    for s, v, _ in fin:
                        engine.wait_ge(s, v)
                getattr(block, engmap[e])(body)
        self.ops = {e: [] for e in self.ALLQ}
        self.final = []


D = 1024
DFF = 2816
NFT = DFF // 128
EPS = 1e-6

def L(f, *a, **k):
    return lambda e: getattr(e, f)(*a, **k)

class Ctx:
    def __init__(self, nc, S, st, pfx=""):
        self.nc, self.S, self.st, self.pfx = nc, S, st, pfx
        self.dyn = None
        self.n = 0
        self.rr = {}
    def sb(self, name, shape, dt):
        return self.st.enter_context(self.nc.sbuf_tensor("s_" + self.pfx + name, shape, dt))
    def ps(self, name, shape, dt):
        return self.st.enter_context(self.nc.psum_tensor("p_" + self.pfx + name, shape, dt))
    def dram(self, name, shape, dt):
        return self.nc.dram_tensor(name, shape, dt, kind="Internal").ap()
    def nxt(self, name, n):
        v = self.rr.get(name, 0)
        self.rr[name] = v + 1
        return v % n


def setup_common(C, ident_d):
    S = C.S
    C.ident = C.sb("ident", [128, 128], BF16)
    S.dma("pool", L("dma_start", out=C.ident[:], in_=ident_d[:, :]), writes=["ident"])
    C.stat = C.sb("stat", [128, 64], F32)
    C.NACC = 6
    C.acc = [C.ps(f"acc{i}", [128, 512], F32) for i in range(C.NACC)]
    C.pt = [C.ps(f"pt{i}", [128, 1024], BF16) for i in range(2)]
    C.hn = [C.sb(f"hn{i}", [128, 1024], BF16) for i in range(2)]


def load_gfull(C, name, g_d):
    S = C.S
    gcol = C.sb(name + "_col", [128, 8], F32)
    gfull = C.sb(name, [128, 8, 128], F32)
    S.dma("sp", L("dma_start", out=gcol[:], in_=g_d.rearrange("(c p) -> p c", p=128), allow_slow_non_contiguous=True), writes=[name + "_col"])
    S.op("dve", L("tensor_copy", out=gfull[:], in_=gcol[:].unsqueeze(2).to_broadcast([128, 8, 128])), reads=[name + "_col"], writes=[name])
    return gfull


def norm_transpose(C, x_ap, xkey, gfull, gkey, hT_ap, hkey):
    S = C.S
    i = C.nxt("stat", 16)
    ss, rs, rstd = C.stat[:, 3 * i:3 * i + 1], C.stat[:, 3 * i + 1:3 * i + 2], C.stat[:, 3 * i + 2:3 * i + 3]
    sk = f"stat{i}"
    b = C.nxt("hn", 2)
    hn, hnk = C.hn[b], f"hn{b}"
    pt, ptk = C.pt[b], f"pt{b}"
    S.op("act", L("activation", out=hn[:], in_=x_ap, func=AF.Square, accum_out=ss), reads=[xkey], writes=[hnk, sk])
    S.op("act", L("activation", out=rs, in_=ss, func=AF.Sqrt, scale=1.0 / D, bias=C.epsb[:, 0:1]), reads=[sk, "epsb"], writes=[sk])
    S.op("dve", L("reciprocal", out=rstd, in_=rs), reads=[sk], writes=[sk])
    S.op("act", L("activation", out=hn[:], in_=x_ap, func=AF.Copy, scale=rstd), reads=[xkey, sk], writes=[hnk])
    for c in range(8):
        S.op("pe", L("transpose", out=pt[:, c * 128:(c + 1) * 128], in_=hn[:, c * 128:(c + 1) * 128], identity=C.ident[:]), reads=[hnk, "ident"], writes=[ptk])
    S.op("dve", L("tensor_tensor", out=hT_ap, in0=pt[:].rearrange("p (c t) -> p c t", c=8), in1=gfull[:], op=ALU.mult), reads=[ptk, gkey], writes=[hkey])
    return rstd, sk


def convert_weights(C, w):
    S = C.S
    w16 = {}
    for k, N in (("wg", 2048), ("pa", D), ("pb", D), ("wo", D), ("fg", DFF), ("fu", DFF)):
        nblk = -(-N // 512)
        w16[k] = C.dram("b16_" + k, [nblk, 128, 8, 512], BF16)
        for nb in range(nblk):
            wc = min(512, N - nb * 512)
            for c in range(8):
                S.dma("pool", L("dma_start", out=w16[k][nb, :, c, 0:wc], in_=w[k][c * 128:(c + 1) * 128, nb * 512:nb * 512 + wc]), writes=["w16_" + k], key="w16_" + k)
    w16["fd"] = C.dram("b16_fd", [2, 128, NFT, 512], BF16)
    for n2 in range(2):
        for ft in range(NFT):
            S.dma("pool", L("dma_start", out=w16["fd"][n2, :, ft, :], in_=w["fd"][ft * 128:(ft + 1) * 128, n2 * 512:(n2 + 1) * 512]), writes=["w16_fd"], key="w16_fd")
    return w16


def dsl(C, e, t0, n):
    if C.dyn is None:
        return slice(t0, t0 + n)
    if "v" not in C.dyn:
        e.reg_load(C.dyn["reg"], C.dyn["qoff"][0:1, 0:1])
        C.dyn["v"] = e.snap(C.dyn["reg"])
    return bass.ds(C.dyn["v"] + t0, n)


def ab_src(C, e, src_d, t0, n):
    if C.dyn is None:
        return src_d[:, t0:t0 + n]
    if "vm" not in C.dyn:
        e.reg_load(C.dyn["mreg"], C.dyn["qm"][0:1, 0:1])
        C.dyn["vm"] = e.snap(C.dyn["mreg"])
    CH = C.dyn["CH"]
    c0 = t0 % CH
    return src_d[bass.ds(C.dyn["vm"] + (t0 // CH), 1), :, c0:c0 + n].rearrange("o r t -> (o r) t")


def phase_b(C, TB, x_d, oaT_d, obT_d, out_d, w, w16):
    nc, S = C.nc, C.S
    TT = 512
    NT = TB // TT
    gmix = load_gfull(C, "gmix", w["g_mix"])
    gffn = load_gfull(C, "gffn", w["g_ffn"])
    gfin = C.sb("gfin", [128, D], F32)
    S.dma("sp", L("dma_start", out=gfin[:], in_=w["g_fin"].partition_broadcast(128)), writes=["gfin"])
    C.epsb = C.sb("epsb", [128, 1], F32)
    S.op("pool", L("memset", C.epsb[:], EPS), writes=["epsb"])

    C.junk = C.sb("junk", [128, 1024], BF16)
    xres = C.sb("xres", [128, 4, D], F32)
    hT = C.sb("hT", [128, 8, TT], BF16)
    oaT = C.sb("oaT", [128, 8, TT], BF16)
    obT = C.sb("obT", [128, 8, TT], BF16)
    mT = C.sb("mT", [128, 8, TT], BF16)
    actT = C.sb("actT", [128, NFT, TT], BF16)
    NSL = 4
    slab = [C.sb(f"slab{i}", [128, 8, 512], BF16) for i in range(NSL)]
    dslab = [C.sb(f"dslab{i}", [128, NFT, 512], BF16) for i in range(2)]
    tmp = [C.sb(f"tmp{i}", [128, 512], F32) for i in range(4)]
    outb = C.sb("outb", [128, D], F32)

    def load_slab(wk, n0):
        i = C.nxt("slab", NSL)
        S.dma("sp", L("dma_start", out=slab[i][:], in_=w16[wk][n0 // 512]), reads=["w16_" + wk], writes=[f"slab{i}"])
        return slab[i], f"slab{i}"

    def mm_group(wk_slab, wkey, ncol, rhs_t, rkey):
        a = C.nxt("acc", C.NACC)
        for c in range(8):
            S.op("pe", L("matmul", C.acc[a][:, :], lhsT=wk_slab[:, c, ncol * 128:(ncol + 1) * 128], rhs=rhs_t[:, c, :], start=(c == 0), stop=(c == 7)),
                 reads=[wkey, rkey], writes=[f"acc{a}"])
        return C.acc[a], f"acc{a}"

    out_toks = []
    for t in range(NT):
        t0 = t * TT
        S.dma("sp", (lambda e, t0=t0: e.dma_start(out=xres[:], in_=x_d[dsl(C, e, t0, TT), :].rearrange("(j p) d -> p j d", p=128))), writes=["xres"])
        S.dma("sp", (lambda e, t0=t0: e.dma_start(out=oaT[:], in_=ab_src(C, e, oaT_d, t0, TT).rearrange("(c p) t -> p c t", p=128))), reads=["recv_a"], writes=["oaT"])
        S.dma("sp", (lambda e, t0=t0: e.dma_start(out=obT[:], in_=ab_src(C, e, obT_d, t0, TT).rearrange("(c p) t -> p c t", p=128))), reads=["recv_b"], writes=["obT"])
        for j in range(4):
            norm_transpose(C, xres[:, j, :], "xres", gmix, "gmix", hT[:, :, j * 128:(j + 1) * 128], "hT")
        for q4 in range(2):
            sga, kga = load_slab("wg", q4 * 512)
            sgb, kgb = load_slab("wg", 1024 + q4 * 512)
            spa, kpa = load_slab("pa", q4 * 512)
            spb, kpb = load_slab("pb", q4 * 512)
            for n in range(4):
                ct = q4 * 4 + n
                GA, kGA = mm_group(sga, kga, n, hT, "hT")
                PA, kPA = mm_group(spa, kpa, n, oaT, "oaT")
                GB, kGB = mm_group(sgb, kgb, n, hT, "hT")
                PB, kPB = mm_group(spb, kpb, n, obT, "obT")
                S.op("act", L("activation", out=tmp[0][:], in_=GA[:], func=AF.Sigmoid), reads=[kGA], writes=["tmp0"])
                S.op("act", L("activation", out=tmp[1][:], in_=GB[:], func=AF.Sigmoid), reads=[kGB], writes=["tmp1"])
                S.op("dve", L("tensor_tensor", out=tmp[2][:], in0=PA[:], in1=tmp[0][:], op=ALU.mult), reads=[kPA, "tmp0"], writes=["tmp2"])
                S.op("dve", L("tensor_tensor", out=tmp[3][:], in0=PB[:], in1=tmp[1][:], op=ALU.mult), reads=[kPB, "tmp1"], writes=["tmp3"])
                S.op("pool", L("tensor_tensor", out=mT[:, ct, :], in0=tmp[2][:], in1=tmp[3][:], op=ALU.add), reads=["tmp2", "tmp3"], writes=["mT"])
        for n2 in range(2):
            so, ko = load_slab("wo", n2 * 512)
            for j in range(4):
                a = C.nxt("acc", C.NACC)
                for c in range(8):
                    S.op("pe", L("matmul", C.acc[a][:, :], lhsT=mT[:, c, j * 128:(j + 1) * 128], rhs=so[:, c, :], start=(c == 0), stop=(c == 7)),
                         reads=["mT", ko], writes=[f"acc{a}"])
                S.op("dve", L("tensor_tensor", out=xres[:, j, n2 * 512:(n2 + 1) * 512], in0=C.acc[a][:], in1=xres[:, j, n2 * 512:(n2 + 1) * 512], op=ALU.add),
                     reads=[f"acc{a}", "xres"], writes=["xres"])
        for j in range(4):
            norm_transpose(C, xres[:, j, :], "xres", gffn, "gffn", hT[:, :, j * 128:(j + 1) * 128], "hT")
        for f4 in range(0, DFF, 512):
            wcols = min(512, DFF - f4)
            i1 = C.nxt("slab", NSL)
            S.dma("sp", L("dma_start", out=slab[i1][:, :, 0:wcols], in_=w16["fg"][f4 // 512][:, :, 0:wcols]), reads=["w16_fg"], writes=[f"slab{i1}"])
            i2 = C.nxt("slab", NSL)
            S.dma("sp", L("dma_start", out=slab[i2][:, :, 0:wcols], in_=w16["fu"][f4 // 512][:, :, 0:wcols]), reads=["w16_fu"], writes=[f"slab{i2}"])
            for n in range(wcols // 128):
                ft = f4 // 128 + n
                G, kG = mm_group(slab[i1], f"slab{i1}", n, hT, "hT")
                U, kU = mm_group(slab[i2], f"slab{i2}", n, hT, "hT")
                tb = C.nxt("tmpf", 2)
                S.op("act", L("activation", out=tmp[tb][:], in_=G[:], func=AF.Silu), reads=[kG], writes=[f"tmp{tb}"])
                S.op("dve", L("tensor_tensor", out=actT[:, ft, :], in0=U[:], in1=tmp[tb][:], op=ALU.mult), reads=[kU, f"tmp{tb}"], writes=["actT"])
        for n2 in range(2):
            di = C.nxt("dslab", 2)
            S.dma("sp", L("dma_start", out=dslab[di][:], in_=w16["fd"][n2]), reads=["w16_fd"], writes=[f"dslab{di}"])
            for j in range(4):
                a = C.nxt("acc", C.NACC)
                for ft in range(NFT):
                    S.op("pe", L("matmul", C.acc[a][:, :], lhsT=actT[:, ft, j * 128:(j + 1) * 128], rhs=dslab[di][:, ft, :], start=(ft == 0), stop=(ft == NFT - 1)),
                         reads=["actT", f"dslab{di}"], writes=[f"acc{a}"])
                S.op("dve", L("tensor_tensor", out=xres[:, j, n2 * 512:(n2 + 1) * 512], in0=C.acc[a][:], in1=xres[:, j, n2 * 512:(n2 + 1) * 512], op=ALU.add),
                     reads=[f"acc{a}", "xres"], writes=["xres"])
        for j in range(4):
            i = C.nxt("stat", 16)
            ss, rs, rstd = C.stat[:, 3 * i:3 * i + 1], C.stat[:, 3 * i + 1:3 * i + 2], C.stat[:, 3 * i + 2:3 * i + 3]
            sk = f"stat{i}"
            S.op("act", L("activation", out=C.junk[:], in_=xres[:, j, :], func=AF.Square, accum_out=ss), reads=["xres"], writes=["junk", sk])
            S.op("act", L("activation", out=rs, in_=ss, func=AF.Sqrt, scale=1.0 / D, bias=C.epsb[:, 0:1]), reads=[sk, "epsb"], writes=[sk])
            S.op("dve", L("reciprocal", out=rstd, in_=rs), reads=[sk], writes=[sk])
            S.op("dve", L("scalar_tensor_tensor", out=outb[:], in0=xres[:, j, :], scalar=rstd, in1=gfin[:], op0=ALU.mult, op1=ALU.mult), reads=["xres", sk, "gfin"], writes=["outb"])
            tk = S.dma("sp", L("dma_start", out=out_d[t0 + j * 128:t0 + (j + 1) * 128, :], in_=outb[:]), reads=["outb"], key="outd")
        out_toks = [tk]
    return out_toks


DILS = (1, 4, 16)

def blk_geom(di, b):
    d = DILS[di]
    if d == 1:
        return 128 * b, 1
    if d == 4:
        return 512 * (b // 4) + (b % 4), 4
    return b, 16

def prev_blk(di, b):
    d = DILS[di]
    if d == 1:
        return ("cur", b - 1) if b >= 1 else ("prev", 0)
    if d == 4:
        return ("cur", b - 4) if b >= 4 else ("prev", b)
    return ("prev", b)

def prev_src_blk(di, j):
    d = DILS[di]
    if d == 1:
        return 15
    if d == 4:
        return 12 + j
    return j
NPREV = (1, 4, 16)


def phase_a1(C, SA, x_d, wqkv_d, g_d, cosF_d, sinF_d, rperm_d, mask_d, out_ap, out_piece=2048, after_mc0=None):
    nc, S = C.nc, C.S
    MC = 2048
    NMC = SA // MC
    gmix = load_gfull(C, "gmix", g_d)
    C.epsb = C.sb("epsb", [128, 1], F32)
    S.op("pool", L("memset", C.epsb[:], EPS), writes=["epsb"])
    W = C.sb("Wqkv", [128, 8, 768], BF16)
    for c in range(8):
        S.dma("pool", L("dma_start", out=W[:, c, :], in_=wqkv_d[c * 128:(c + 1) * 128, :]), writes=["W"])
    rperm = C.sb("rperm", [128, 128], BF16)
    S.dma("pool", L("dma_start", out=rperm[:], in_=rperm_d[:, :]), writes=["rperm"])
    mask = C.sb("mask", [128, 512], BF16)
    S.dma("pool", L("dma_start", out=mask[:], in_=mask_d[:, :]), writes=["mask"])
    ones65 = C.sb("ones65", [128, 64], F32)
    S.op("pool", L("memset", ones65[:], 1.0), writes=["ones65"])

    xt = [C.sb(f"xt{i}", [128, D], F32) for i in range(2)]
    hT = C.sb("hT", [128, 8, 512], BF16)
    cosF = C.sb("cosF", [128, 512], F32)
    sinF = C.sb("sinF", [128, 512], F32)
    qraw = C.sb("qraw", [128, 512], BF16)
    t1 = C.sb("t1", [128, 512], F32)
    t2 = C.sb("t2", [128, 512], F32)
    qT = C.sb("qT", [128, 2, MC], BF16)
    kT = [C.sb(f"kT{i}", [128, 2, MC], BF16) for i in range(2)]
    vT = C.sb("vT", [128, 2, MC], BF16)
    Vc = [C.sb(f"Vc{di}", [128, 16, 4, 128], BF16) for di in range(3)]
    Vp = [C.sb(f"Vp{di}", [128, NPREV[di], 4, 128], BF16) for di in range(3)]
    ones3 = ones65[:].unsqueeze(1).to_broadcast([128, 4, 64])
    for di in range(3):
        for b in range(16):
            S.op("dve", L("tensor_copy", out=Vc[di][:, b, :, 64:128], in_=ones3), reads=["ones65"], writes=[f"Vc{di}"])
        for b in range(NPREV[di]):
            S.op("dve", L("tensor_copy", out=Vp[di][:, b, :, 64:128], in_=ones3), reads=["ones65"], writes=[f"Vp{di}"])
    oacc = [C.sb(f"oacc{i}", [128, MC], F32) for i in range(2)]
    pexp = [C.sb(f"pexp{i}", [128, 512], BF16) for i in range(2)]
    pm = [C.sb(f"pm{i}", [128, 512], BF16) for i in range(2)]
    rden = C.sb("rden", [128, 512], F32)
    rden2 = C.sb("rden2", [64, 512], F32)
    oout = [C.sb(f"oout{i}", [64, MC], BF16) for i in range(2)]

    toks = []
    for m in range(NMC):
        slot = m % 2
        kTc, kTk = kT[slot], f"kT{slot}"
        kTp, kTpk = kT[1 - slot], f"kT{1 - slot}"
        for sc in range(4):
            t0 = m * MC + sc * 512
            l0 = sc * 512
            for j in range(4):
                xb = C.nxt("xt", 2)
                S.dma("sp", L("dma_start", out=xt[xb][:], in_=x_d[t0 + j * 128:t0 + (j + 1) * 128, :]), writes=[f"xt{xb}"])
                norm_transpose(C, xt[xb][:], f"xt{xb}", gmix, "gmix", hT[:, :, j * 128:(j + 1) * 128], "hT")
            S.dma("act", L("dma_start", out=cosF[:], in_=cosF_d[:, t0:t0 + 512]), writes=["cosF"])
            S.dma("act", L("dma_start", out=sinF[:], in_=sinF_d[:, t0:t0 + 512]), writes=["sinF"])
            for ti in range(6):
                a = C.nxt("acc", C.NACC)
                for c in range(8):
                    S.op("pe", L("matmul", C.acc[a][:, :], lhsT=W[:, c, ti * 128:(ti + 1) * 128], rhs=hT[:, c, :], start=(c == 0), stop=(c == 7)),
                         reads=["W", "hT"], writes=[f"acc{a}"])
                hp = ti % 2
                if ti >= 4:
                    S.op("act", L("activation", out=vT[:, hp, l0:l0 + 512], in_=C.acc[a][:], func=AF.Copy), reads=[f"acc{a}"], writes=["vT"])
                    continue
                S.op("dve", L("tensor_copy", out=qraw[:], in_=C.acc[a][:]), reads=[f"acc{a}"], writes=["qraw"])
                a2 = C.nxt("acc", C.NACC)
                S.op("pe", L("matmul", C.acc[a2][:, :], lhsT=rperm[:], rhs=qraw[:], start=True, stop=True), reads=["rperm", "qraw"], writes=[f"acc{a2}"])
                S.op("dve", L("tensor_tensor", out=t1[:], in0=C.acc[a][:], in1=cosF[:], op=ALU.mult), reads=[f"acc{a}", "cosF"], writes=["t1"])
                S.op("dve", L("tensor_tensor", out=t2[:], in0=C.acc[a2][:], in1=sinF[:], op=ALU.mult), reads=[f"acc{a2}", "sinF"], writes=["t2"])
                if ti < 2:
                    dst, dk = qT[:, hp, l0:l0 + 512], "qT"
                else:
                    dst, dk = kTc[:, hp, l0:l0 + 512], kTk
                S.op("pool", L("tensor_tensor", out=dst, in0=t1[:], in1=t2[:], op=ALU.add), reads=["t1", "t2"], writes=[dk])
        for di in range(3):
            for b0 in range(0, 16, 4):
                pb = C.nxt("hn", 2)
                pt, ptk = C.pt[pb], f"pt{pb}"
                for bb in range(4):
                    base, st_ = blk_geom(di, b0 + bb)
                    for hp in range(2):
                        S.op("pe", L("transpose", out=pt[:, (bb * 2 + hp) * 128:(bb * 2 + hp + 1) * 128],
                                     in_=vT[:, hp, base:base + 127 * st_ + 1:st_], identity=C.ident[:]), reads=["vT", "ident"], writes=[ptk])
                for bb in range(4):
                    S.op("act", L("activation", out=Vc[di][:, b0 + bb, :, 0:64], in_=pt[:, bb * 256:(bb + 1) * 256].rearrange("p (h c) -> p h c", h=4), func=AF.Copy),
                         reads=[ptk], writes=[f"Vc{di}"])
        for h in range(4):
            hp, r0 = h // 2, 64 * (h % 2)
            ob = C.nxt("oacc", 2)
            oa, oak = oacc[ob], f"oacc{ob}"
            jobs = [(di, b0, pr) for di in range(3) for b0 in range(0, 16, 4) for pr in range(2)]
            state = {}

            def front(job):
                di, b0, pr = job
                if pr == 0:
                    pa = C.nxt("acc", C.NACC)
                    state[(di, b0)] = (C.acc[pa], f"acc{pa}")
                sa_ = C.nxt("acc", C.NACC)
                sc_, sck = C.acc[sa_], f"acc{sa_}"
                info = []
                for qq in range(2):
                    b = b0 + pr * 2 + qq
                    base, st_ = blk_geom(di, b)
                    qsl = qT[r0:r0 + 64, hp, base:base + 127 * st_ + 1:st_]
                    where, pbk = prev_blk(di, b)
                    has_prev = not (where == "prev" and m == 0)
                    if has_prev:
                        if where == "cur":
                            pbase, pst = blk_geom(di, pbk)
                            ksl, kk_ = kTc[r0:r0 + 64, hp, pbase:pbase + 127 * pst + 1:pst], kTk
                            vsl, vk_ = Vc[di][:, pbk, h, :], f"Vc{di}"
                        else:
                            pbase, pst = blk_geom(di, prev_src_blk(di, pbk))
                            ksl, kk_ = kTp[r0:r0 + 64, hp, pbase:pbase + 127 * pst + 1:pst], kTpk
                            vsl, vk_ = Vp[di][:, pbk, h, :], f"Vp{di}"
                        S.op("pe", L("matmul", sc_[:, qq * 256:qq * 256 + 128], lhsT=ksl, rhs=qsl, start=True, stop=True), reads=[kk_, "qT"], writes=[sck])
                    else:
                        vsl, vk_ = None, None
                    S.op("pe", L("matmul", sc_[:, qq * 256 + 128:qq * 256 + 256], lhsT=kTc[r0:r0 + 64, hp, base:base + 127 * st_ + 1:st_], rhs=qsl, start=True, stop=True),
                         reads=[kTk, "qT"], writes=[sck])
                    info.append((b, has_prev, vsl, vk_))
                pi = C.nxt("pexp", 2)
                S.op("act", L("activation", out=pexp[pi][:], in_=sc_[:], func=AF.Exp, scale=0.125), reads=[sck], writes=[f"pexp{pi}"])
                S.op("pool" if pi == 0 else "dve", L("tensor_tensor", out=pm[pi][:], in0=pexp[pi][:], in1=mask[:], op=ALU.mult), reads=[f"pexp{pi}", "mask"], writes=[f"pm{pi}"])
                state[job] = (pi, info)

            def back(job):
                di, b0, pr = job
                d = DILS[di]
                po, pok = state[(di, b0)]
                pi, info = state.pop(job)
                for qq in range(2):
                    b, has_prev, vsl, vk_ = info[qq]
                    col = (pr * 2 + qq) * 128
                    if has_prev:
                        S.op("pe", L("matmul", po[:, col:col + 128], lhsT=vsl, rhs=pm[pi][:, qq * 256:qq * 256 + 128], start=True, stop=False), reads=[vk_, f"pm{pi}"], writes=[pok])
                    S.op("pe", L("matmul", po[:, col:col + 128], lhsT=Vc[di][:, b, h, :], rhs=pm[pi][:, qq * 256 + 128:qq * 256 + 256], start=(not has_prev), stop=True),
                         reads=[f"Vc{di}", f"pm{pi}"], writes=[pok])
                if pr == 1:
                    if d == 1:
                        dst = oa[:, 128 * b0:128 * b0 + 512]
                        S.op("act", L("activation", out=dst, in_=po[:, :], func=AF.Copy), reads=[pok], writes=[oak])
                    else:
                        if d == 4:
                            dst = oa[:, 128 * b0:128 * b0 + 512].rearrange("p (i r) -> p r i", r=4)
                        else:
                            dst = oa[:, :].rearrange("p (i r) -> p r i", r=16)[:, b0:b0 + 4, :]
                        S.op("dve", L("tensor_tensor", out=dst, in0=po[:, :].rearrange("p (r i) -> p r i", r=4), in1=dst, op=ALU.add), reads=[pok, oak], writes=[oak])
                    state.pop((di, b0))

            for i in range(len(jobs) + 1):
                if i < len(jobs):
                    front(jobs[i])
                if i >= 1:
                    back(jobs[i - 1])
            oo, ook = oout[ob], f"oout{ob}"
            for q4 in range(4):
                S.op("dve", L("reciprocal", out=rden[64:128, :], in_=oa[64:128, q4 * 512:(q4 + 1) * 512]), reads=[oak], writes=["rden"])
                S.op("act", L("activation", out=rden2[:], in_=rden[64:128, :], func=AF.Copy), reads=["rden"], writes=["rden2"])
                S.op("dve", L("tensor_tensor", out=oo[:, q4 * 512:(q4 + 1) * 512], in0=oa[0:64, q4 * 512:(q4 + 1) * 512], in1=rden2[:], op=ALU.mult), reads=[oak, "rden2"], writes=[ook])
            for p0 in range(0, MC, out_piece):
                tk = S.dma("sp", L("dma_start", out=out_ap(slice(h * 64, (h + 1) * 64), m * MC + p0, out_piece), in_=oo[:, p0:p0 + out_piece]), reads=[ook], key="oaT_d" + str(ob))
                toks.append(tk)
        if m == 0 and after_mc0 is not None:
            after_mc0()
        for di in range(3):
            n = NPREV[di]
            sb0 = prev_src_blk(di, 0)
            for jj in range(n):
                S.op("dve", L("tensor_copy", out=Vp[di][:, jj, :, 0:64], in_=Vc[di][:, sb0 + jj, :, 0:64]), reads=[f"Vc{di}"], writes=[f"Vp{di}"])
    return toks[-8:]


C0 = 0.6065306597126334
GN_EPS = 64e-5
PV_MU, PV_W0, PV_A0, PV_KK, PV_KA, PV_RK, PV_LNW, PV_LNB, PV_1MKA, PV_N = 0, 8, 10, 12, 14, 16, 18, 20, 22, 24


def phase_a2(C, SA, x_d, wr_d, g_d, pv_d, w2a2_d, g2_d, cst_d, out_ap, after_sc=None, out_key=None):
    nc, S = C.nc, C.S
    NSC = SA // 512
    gmix = load_gfull(C, "gmix", g_d)
    C.epsb = C.sb("epsb", [128, 1], F32)
    S.op("pool", L("memset", C.epsb[:], EPS), writes=["epsb"])
    gnb = C.sb("gnb", [128, 1], F32)
    S.op("pool", L("memset", gnb[:], GN_EPS), writes=["gnb"])
    W = C.sb("Wr", [128, 8, 1024], BF16)
    for c in range(8):
        S.dma("pool", L("dma_start", out=W[:, c, :], in_=wr_d[c * 128:(c + 1) * 128, :]), writes=["W"])
    pv = C.sb("pv", [128, PV_N], F32)
    S.dma("sp", L("dma_start", out=pv[:, 0:22], in_=pv_d[:, 0:22]), writes=["pv"])
    S.op("dve", L("tensor_scalar", out=pv[:, PV_1MKA:PV_1MKA + 2], in0=pv[:, PV_KA:PV_KA + 2], scalar1=-1.0, scalar2=1.0, op0=ALU.mult, op1=ALU.add), reads=["pv"], writes=["pv"])
    w2a2 = C.sb("w2a2", [128, 256], BF16)
    S.dma("pool", L("dma_start", out=w2a2[:], in_=w2a2_d[:, :]), writes=["w2a2"])
    g2 = C.sb("g2", [128, 256], BF16)
    S.dma("pool", L("dma_start", out=g2[:], in_=g2_d[:, :]), writes=["g2"])
    cst = C.sb("cst", [128, 1280], F32)
    S.dma("sp", L("dma_start", out=cst[:], in_=cst_d[:, :]), writes=["cst"])
    mreset = cst[:, 0:512]
    blkones = cst[:, 512:640]
    identf = cst[:, 1216:1280]
    maskA = C.sb("maskA", [128, 512], BF16)
    S.op("dve", L("tensor_copy", out=maskA[:], in_=cst[:, 640:1152]), reads=["cst"], writes=["maskA"])
    maskL = C.sb("maskL", [128, 8, 64], BF16)
    S.op("dve", L("tensor_copy", out=maskL[:], in_=cst[:, 1152:1216].unsqueeze(1).to_broadcast([128, 8, 64])), reads=["cst"], writes=["maskL"])
    identb = C.sb("identb", [128, 4, 64], BF16)
    S.op("dve", L("tensor_copy", out=identb[:], in_=identf.unsqueeze(1).to_broadcast([128, 4, 64])), reads=["cst"], writes=["identb"])

    xt = [C.sb(f"xt{i}", [128, D], F32) for i in range(2)]
    hT = C.sb("hT", [128, 8, 512], BF16)
    colsb = [C.sb(f"colsb{i}", [128, 513], F32) for i in range(8)]
    for i in range(8):
        S.op("pool", L("memset", colsb[i][:, 0:1], 0.0), writes=[f"colsb{i}"])
    xm = [C.sb(f"xm{i}", [128, 512], F32) for i in range(8)]
    tmpd = C.sb("tmpd", [128, 512], F32)
    lo1b = C.sb("lo1b", [128, 512], BF16)
    sg = C.sb("sg", [128, 512], BF16)

    def f32t(n):
        return C.sb(n, [128, 512], F32)
    def b16t(n):
        return C.sb(n, [128, 512], BF16)
    HPF32 = ("lwp", "ai", "gt", "kk", "sc1", "sc2", "kf", "bvec", "cwp", "cwm", "Epos", "Eprev", "Eneg", "yt")
    HPB16 = ("bT", "kTt", "BhT", "KhT", "vb")
    bufs = []
    for hp_ in range(2):
        d = {}
        for n in HPF32:
            d[n] = f32t(f"{n}_{hp_}")
        for n in HPB16:
            d[n] = b16t(f"{n}_{hp_}")
        d["AR"] = C.sb(f"AR_{hp_}", [128, 8, 128], BF16)
        d["AT"] = C.sb(f"AT_{hp_}", [128, 8, 256], BF16)
        d["PTt"] = C.sb(f"PTt_{hp_}", [128, 8, 64], BF16)
        d["Pl"] = [C.sb(f"Pl{i}_{hp_}", [128, 4, 64], BF16) for i in range(4)]
        d["PTl"] = [C.sb(f"PTl{i}_{hp_}", [128, 4, 64], BF16) for i in range(4)]
        d["Z"] = C.sb(f"Z_{hp_}", [128, 8, 64], BF16)
        for n in ("Bh_tok", "Kh_tok", "V_tok"):
            d[n] = C.sb(f"{n}_{hp_}", [128, 8, 64], BF16)
        d["YY"] = C.sb(f"YY_{hp_}", [128, 8, 128], BF16)
        d["XX"] = C.sb(f"XX_{hp_}", [128, 8, 128], BF16)
        d["MN"] = C.sb(f"MN_{hp_}", [128, 8, 128], F32)
        d["QcT"] = C.sb(f"QcT_{hp_}", [128, 8, 64], F32)
        bufs.append(d)
    HPKEYS = set(HPF32) | set(HPB16) | {"AR", "AT", "PTt", "Pl0", "Pl1", "Pl2", "Pl3", "PTl0", "PTl1", "PTl2", "PTl3", "Z", "Zh0", "Zh1", "Bh_tok", "Kh_tok", "V_tok", "YY", "XX", "MN", "QcT"}
    Hs = [C.sb(f"Hs{hp}", [128, 9, 64], F32) for hp in range(2)]
    for hp in range(2):
        S.op("pool", L("memset", Hs[hp][:], 0.0), writes=[f"Hs{hp}"])
    obuf = [C.sb(f"obuf{i}", [128, 512], BF16) for i in range(1)]

    def acc():
        a = C.nxt("acc", C.NACC)
        return C.acc[a], f"acc{a}"

    toks = []
    for sci in range(NSC):
        t0 = sci * 512
        for j in range(4):
            xb = C.nxt("xt", 2)
            S.dma("sp", L("dma_start", out=xt[xb][:], in_=x_d[t0 + j * 128:t0 + (j + 1) * 128, :]), writes=[f"xt{xb}"])
            norm_transpose(C, xt[xb][:], f"xt{xb}", gmix, "gmix", hT[:, :, j * 128:(j + 1) * 128], "hT")
        for ti in range(8):
            A, Ak = acc()
            for c in range(8):
                S.op("pe", L("matmul", A[:, :], lhsT=W[:, c, ti * 128:(ti + 1) * 128], rhs=hT[:, c, :], start=(c == 0), stop=(c == 7)), reads=["W", "hT"], writes=[Ak])
            cb, cbk = colsb[ti], f"colsb{ti}"
            S.op("act", L("activation", out=cb[:, 1:513], in_=A[:], func=AF.Copy), reads=[Ak], writes=[cbk])
            S.op("dve", L("tensor_tensor", out=tmpd[:], in0=cb[:, 0:512], in1=cb[:, 1:513], op=ALU.subtract), reads=[cbk], writes=["tmpd"])
            S.op("dve", L("scalar_tensor_tensor", out=xm[ti][:], in0=tmpd[:], scalar=pv[:, PV_MU + ti:PV_MU + ti + 1], in1=cb[:, 1:513], op0=ALU.mult, op1=ALU.add),
                 reads=["tmpd", "pv", cbk], writes=[f"xm{ti}"])
            S.op("pool", L("tensor_copy", out=cb[:, 0:1], in_=cb[:, 512:513]), reads=[cbk], writes=[cbk])
        S.op("act", L("activation", out=lo1b[0:64, :], in_=xm[6][0:64, :], func=AF.Tanh), reads=["xm6"], writes=["lo1b"])
        S.op("act", L("activation", out=lo1b[64:128, :], in_=xm[6][64:128, :], func=AF.Copy), reads=["xm6"], writes=["lo1b"])
        S.op("act", L("activation", out=sg[:], in_=xm[7][:], func=AF.Sigmoid), reads=["xm7"], writes=["sg"])
        def hp_body(hp):
            B_ = bufs[hp]
            lwp, ai, gt, kk, sc1, sc2, kf, bvec, cwp, cwm, Epos, Eprev, Eneg, yt = [B_[n] for n in HPF32]
            kkn, bonus = kk, lwp
            bT, kTt, BhT, KhT, vb = [B_[n] for n in HPB16]
            AR, AT, PTt, Pl, PTl, Z, Bh_tok, Kh_tok, V_tok, YY, XX, MN, QcT = [B_[n] for n in ("AR", "AT", "PTt", "Pl", "PTl", "Z", "Bh_tok", "Kh_tok", "V_tok", "YY", "XX", "MN", "QcT")]
            km = lambda k: (k + "_" + str(hp)) if k in HPKEYS else k
            def op(eng, fn, reads=(), writes=()):
                return S.op(eng, fn, [km(k) for k in reads], [km(k) for k in writes])
            r_, k_, v_ = xm[hp], xm[2 + hp], xm[4 + hp]
            rk_, kk_, vk_ = f"xm{hp}", f"xm{2 + hp}", f"xm{4 + hp}"
            col = lambda base: pv[:, base + hp:base + hp + 1]
            A, Ak = acc()
            op("pe", L("matmul", A[:, :], lhsT=w2a2[0:64, hp * 128:(hp + 1) * 128], rhs=lo1b[0:64, :], start=True, stop=True), reads=["w2a2", "lo1b"], writes=[Ak])
            op("act", L("activation", out=lwp[:], in_=A[:], func=AF.Sigmoid, bias=col(PV_W0)), reads=[Ak, "pv"], writes=["lwp"])
            A, Ak = acc()
            op("pe", L("matmul", A[:, :], lhsT=w2a2[64:128, hp * 128:(hp + 1) * 128], rhs=lo1b[64:128, :], start=True, stop=True), reads=["w2a2", "lo1b"], writes=[Ak])
            op("act", L("activation", out=ai[:], in_=A[:], func=AF.Sigmoid, bias=col(PV_A0)), reads=[Ak, "pv"], writes=["ai"])
            A, Ak = acc()
            op("pe", L("matmul", A[:, :], lhsT=g2[:, hp * 128:(hp + 1) * 128], rhs=sg[:], start=True, stop=True), reads=["g2", "sg"], writes=[Ak])
            op("act", L("activation", out=gt[:], in_=A[:], func=AF.Copy), reads=[Ak], writes=["gt"])
            yield
            op("dve", L("tensor_scalar", out=kk[:], in0=k_[:], scalar1=col(PV_KK), scalar2=None, op0=ALU.mult), reads=[kk_, "pv"], writes=["kk"])
            op("pool", L("tensor_tensor", out=sc1[:], in0=kk[:], in1=kk[:], op=ALU.mult), reads=["kk"], writes=["sc1"])
            A, Ak = acc()
            op("pe", L("matmul", A[:, :], lhsT=blkones, rhs=sc1[:], start=True, stop=True), reads=["cst", "sc1"], writes=[Ak])
            op("act", L("activation", out=sc2[:], in_=A[:], func=AF.Sqrt), reads=[Ak], writes=["sc2"])
            op("dve", L("tensor_scalar", out=sc2[:], in0=sc2[:], scalar1=1e-12, scalar2=None, op0=ALU.max), reads=["sc2"], writes=["sc2"])
            op("dve", L("reciprocal", out=sc1[:], in_=sc2[:]), reads=["sc2"], writes=["sc1"])
            op("dve", L("tensor_tensor", out=kkn[:], in0=kk[:], in1=sc1[:], op=ALU.mult), reads=["kk", "sc1"], writes=["kk"])
            yield
            op("dve", L("tensor_scalar", out=sc2[:], in0=ai[:], scalar1=col(PV_KA), scalar2=col(PV_1MKA), op0=ALU.mult, op1=ALU.add), reads=["ai", "pv"], writes=["sc2"])
            op("pool", L("tensor_tensor", out=kf[:], in0=sc2[:], in1=k_[:], op=ALU.mult), reads=["sc2", kk_], writes=["kf"])
            op("pool", L("tensor_tensor", out=bvec[:], in0=kkn[:], in1=ai[:], op=ALU.mult), reads=["kk", "ai"], writes=["bvec"])
            yield
            op("dve", L("tensor_tensor_scan", out=cwp[:], data0=mreset, data1=lwp[:], initial=0.0, op0=ALU.mult, op1=ALU.add), reads=["cst", "lwp"], writes=["cwp"])
            op("pool", L("tensor_tensor", out=cwm[:], in0=cwp[:], in1=lwp[:], op=ALU.subtract), reads=["cwp", "lwp"], writes=["cwm"])
            op("act", L("activation", out=Epos[:], in_=cwp[:], func=AF.Exp, scale=-C0), reads=["cwp"], writes=["Epos"])
            op("act", L("activation", out=Eprev[:], in_=cwm[:], func=AF.Exp, scale=-C0), reads=["cwm"], writes=["Eprev"])
            op("act", L("activation", out=Eneg[:], in_=cwp[:], func=AF.Exp, scale=C0), reads=["cwp"], writes=["Eneg"])
            v3 = lambda ap: ap.rearrange("p (c t) -> p c t", t=64)
            op("dve", L("scalar_tensor_tensor", out=AR[:, :, 0:64], in0=v3(kkn[:]), scalar=-1.0, in1=v3(Eprev[:]), op0=ALU.mult, op1=ALU.mult), reads=["kk", "Eprev"], writes=["AR"])
            op("pool", L("tensor_tensor", out=AR[:, :, 64:128], in0=v3(r_[:]), in1=v3(Epos[:]), op=ALU.mult), reads=[rk_, "Epos"], writes=["AR"])
            op("pool", L("tensor_tensor", out=bT[:], in0=bvec[:], in1=Eneg[:], op=ALU.mult), reads=["bvec", "Eneg"], writes=["bT"])
            op("dve", L("tensor_tensor", out=kTt[:], in0=kf[:], in1=Eneg[:], op=ALU.mult), reads=["kf", "Eneg"], writes=["kTt"])
            wcb = v3(Epos[:])[:, :, 63:64].to_broadcast([128, 8, 64])
            op("dve", L("tensor_tensor", out=v3(BhT[:]), in0=v3(bT[:]), in1=wcb, op=ALU.mult), reads=["bT", "Epos"], writes=["BhT"])
            op("pool", L("tensor_tensor", out=v3(KhT[:]), in0=v3(kTt[:]), in1=wcb, op=ALU.mult), reads=["kTt", "Epos"], writes=["KhT"])
            op("act", L("activation", out=vb[:], in_=v_[:], func=AF.Copy), reads=[vk_], writes=["vb"])
            yield
            op("dve", L("tensor_tensor", out=sc1[:], in0=r_[:], in1=kf[:], op=ALU.mult), reads=[rk_, "kf"], writes=["sc1"])
            op("dve", L("tensor_scalar", out=sc1[:], in0=sc1[:], scalar1=col(PV_RK), scalar2=None, op0=ALU.mult), reads=["sc1", "pv"], writes=["sc1"])
            A, Ak = acc()
            op("pe", L("matmul", A[:, :], lhsT=blkones, rhs=sc1[:], start=True, stop=True), reads=["cst", "sc1"], writes=[Ak])
            op("dve", L("tensor_tensor", out=bonus[:], in0=A[:], in1=v_[:], op=ALU.mult), reads=[Ak, vk_], writes=["lwp"])
            yield
            for src, sk, dst, dk, dsl in ((BhT, "BhT", Bh_tok, "Bh_tok", None), (KhT, "KhT", Kh_tok, "Kh_tok", None), (vb, "vb", V_tok, "V_tok", None), (None, "AR", YY, "YY", 1)):
                pb = C.nxt("hn", 2)
                pt, ptk = C.pt[pb], f"pt{pb}"
                for c in range(8):
                    for hd in range(2):
                        r0 = 64 * hd
                        in_ap = src[r0:r0 + 64, c * 64:(c + 1) * 64] if src is not None else AR[r0:r0 + 64, c, 0:64]
                        op("pe", L("transpose", out=pt[r0:r0 + 64, c * 64:(c + 1) * 64], in_=in_ap, identity=C.ident[r0:r0 + 64, r0:r0 + 64]), reads=[sk, "ident"], writes=[ptk])
                o = dst[:] if dsl is None else dst[:, :, 64:128]
                op("act", L("activation", out=o, in_=pt[:, 0:512].rearrange("p (c t) -> p c t", t=64), func=AF.Copy), reads=[ptk], writes=[dk])
                yield
            yield
            for c in range(0, 8, 2):
                A, Ak = acc()
                for cc in range(2):
                    for hd in range(2):
                        r0 = 64 * hd
                        op("pe", L("matmul", A[r0:r0 + 64, cc * 256:cc * 256 + 128], lhsT=bT[r0:r0 + 64, (c + cc) * 64:(c + cc + 1) * 64], rhs=AR[r0:r0 + 64, c + cc, :], start=True, stop=True), reads=["bT", "AR"], writes=[Ak])
                        op("pe", L("matmul", A[r0:r0 + 64, cc * 256 + 128:cc * 256 + 256], lhsT=kTt[r0:r0 + 64, (c + cc) * 64:(c + cc + 1) * 64], rhs=AR[r0:r0 + 64, c + cc, :], start=True, stop=True), reads=["kTt", "AR"], writes=[Ak])
                op("dve", L("tensor_tensor", out=AT[:, c:c + 2, :], in0=A[:].rearrange("p (c t) -> p c t", c=2), in1=maskA[:].rearrange("p (c t) -> p c t", c=2), op=ALU.mult), reads=[Ak, "maskA"], writes=["AT"])
            A, Ak = acc()
            for c in range(8):
                for hd in range(2):
                    r0 = 64 * hd
                    op("pe", L("matmul", A[r0:r0 + 64, c * 64:(c + 1) * 64], lhsT=AR[r0:r0 + 64, c, 0:64], rhs=bT[r0:r0 + 64, c * 64:(c + 1) * 64], start=True, stop=True), reads=["AR", "bT"], writes=[Ak])
            op("dve", L("tensor_tensor", out=PTt[:], in0=A[:].rearrange("p (c t) -> p c t", t=64), in1=maskL[:], op=ALU.mult), reads=[Ak, "maskL"], writes=["PTt"])
            yield
            hs = []
            for half in range(2):
                c0 = half * 4
                hs.append({"c0": c0, "Pv": (lambda c, c0=c0: AT[:, c0 + c, 0:64]), "Pk": "AT", "PTv": (lambda c, c0=c0: PTt[:, c0 + c, :]), "PTk": "PTt"})
                op("pool", L("tensor_tensor", out=Z[:, c0:c0 + 4, :], in0=AT[:, c0:c0 + 4, 0:64], in1=identb[:], op=ALU.add), reads=["AT", "identb"], writes=["Z"])
            for lvl in range(6):
                for half in range(2):
                    h_ = hs[half]
                    c0 = h_["c0"]
                    Pv, Pk, PTv, PTk = h_["Pv"], h_["Pk"], h_["PTv"], h_["PTk"]
                    Zh = Z[:, c0:c0 + 4, :]
                    Zk = f"Zh{half}"
                    AP_, APk = acc()
                    if lvl > 0:
                        AZ, AZk = acc()
                    for c in range(4):
                        for hd in range(2):
                            r0 = 64 * hd
                            p_, pt_ = Pv(c)[r0:r0 + 64], PTv(c)[r0:r0 + 64]
                            if lvl > 0:
                                op("pe", L("matmul", AZ[r0:r0 + 64, c * 64:(c + 1) * 64], lhsT=pt_, rhs=Z[r0:r0 + 64, c0 + c, :], start=True, stop=True), reads=[PTk, "Z", Zk], writes=[AZk])
                            if lvl < 5:
                                op("pe", L("matmul", AP_[r0:r0 + 64, c * 128:c * 128 + 64], lhsT=pt_, rhs=p_, start=True, stop=True), reads=[PTk, Pk], writes=[APk])
                                op("pe", L("matmul", AP_[r0:r0 + 64, c * 128 + 64:c * 128 + 128], lhsT=p_, rhs=pt_, start=True, stop=True), reads=[PTk, Pk], writes=[APk])
                    if lvl > 0:
                        op("dve", L("tensor_tensor", out=Zh, in0=AZ[:, 0:256].rearrange("p (c t) -> p c t", t=64), in1=Zh, op=ALU.add), reads=[AZk, "Z", Zk], writes=[Zk])
                    if lvl < 5:
                        nb = 2 * half + (lvl % 2)
                        op("act", L("activation", out=Pl[nb][:], in_=AP_[:].rearrange("p (c u t) -> p c u t", c=4, u=2)[:, :, 0, :], func=AF.Copy), reads=[APk], writes=[f"Pl{nb}"])
                        op("act", L("activation", out=PTl[nb][:], in_=AP_[:].rearrange("p (c u t) -> p c u t", c=4, u=2)[:, :, 1, :], func=AF.Copy), reads=[APk], writes=[f"PTl{nb}"])
                        h_["Pv"], h_["Pk"] = (lambda c, nb=nb: Pl[nb][:, c, :]), f"Pl{nb}"
                        h_["PTv"], h_["PTk"] = (lambda c, nb=nb: PTl[nb][:, c, :]), f"PTl{nb}"
                yield
            op("pool", L("tensor_copy", out=Z[:, 0:1, 0:1], in_=Z[:, 0:1, 0:1]), reads=["Zh0", "Zh1", "Z"], writes=["Z"])
            yield
            A, Ak = acc()
            for c in range(8):
                for hd in range(2):
                    r0 = 64 * hd
                    op("pe", L("matmul", A[r0:r0 + 64, c * 64:(c + 1) * 64], lhsT=AT[r0:r0 + 64, c, 128:192], rhs=V_tok[r0:r0 + 64, c, :], start=True, stop=True), reads=["AT", "V_tok"], writes=[Ak])
            op("act", L("activation", out=YY[:, :, 0:64], in_=A[:].rearrange("p (c t) -> p c t", t=64), func=AF.Copy), reads=[Ak], writes=["YY"])
            yield
            for c in range(0, 8, 4):
                A, Ak = acc()
                for cc in range(4):
                    for hd in range(2):
                        r0 = 64 * hd
                        op("pe", L("matmul", A[r0:r0 + 64, cc * 128:(cc + 1) * 128], lhsT=Z[r0:r0 + 64, c + cc, :], rhs=YY[r0:r0 + 64, c + cc, :], start=True, stop=True), reads=["Z", "YY"], writes=[Ak])
                op("dve", L("tensor_copy", out=XX[:, c:c + 4, :], in_=A[:].rearrange("p (c t) -> p c t", t=128)), reads=[Ak], writes=["XX"])
            yield
            for c in range(0, 8, 4):
                A, Ak = acc()
                for cc in range(4):
                    for hd in range(2):
                        r0 = 64 * hd
                        ci = c + cc
                        op("pe", L("matmul", A[r0:r0 + 64, cc * 128:cc * 128 + 64], lhsT=XX[r0:r0 + 64, ci, 64:128], rhs=Bh_tok[r0:r0 + 64, ci, :], start=True, stop=True), reads=["XX", "Bh_tok"], writes=[Ak])
                        op("pe", L("matmul", A[r0:r0 + 64, cc * 128 + 64:cc * 128 + 128], lhsT=Bh_tok[r0:r0 + 64, ci, :], rhs=XX[r0:r0 + 64, ci, 0:64], start=True, stop=False), reads=["XX", "Bh_tok"], writes=[Ak])
                        op("pe", L("matmul", A[r0:r0 + 64, cc * 128 + 64:cc * 128 + 128], lhsT=Kh_tok[r0:r0 + 64, ci, :], rhs=V_tok[r0:r0 + 64, ci, :], start=False, stop=True), reads=["Kh_tok", "V_tok"], writes=[Ak])
                op("act", L("activation", out=MN[:, c:c + 4, :], in_=A[:].rearrange("p (c t) -> p c t", t=128), func=AF.Copy), reads=[Ak], writes=["MN"])
            for c in range(8):
                op("dve", L("scalar_tensor_tensor", out=MN[:, c, 0:64], in0=identf, scalar=Epos[:, c * 64 + 63:c * 64 + 64], in1=MN[:, c, 0:64], op0=ALU.mult, op1=ALU.add), reads=["cst", "Epos", "MN"], writes=["MN"])
            yield
            A, Ak = acc()
            for c in range(8):
                for hd in range(2):
                    r0 = 64 * hd
                    op("pe", L("matmul", A[r0:r0 + 64, c * 64:(c + 1) * 64], lhsT=XX[r0:r0 + 64, c, 64:128], rhs=AT[r0:r0 + 64, c, 64:128], start=True, stop=True), reads=["XX", "AT"], writes=[Ak])
            op("dve", L("tensor_tensor", out=QcT[:], in0=A[:].rearrange("p (c t) -> p c t", t=64), in1=AR[:, :, 64:128], op=ALU.add), reads=[Ak, "AR"], writes=["QcT"])
            yield
            H, Hk = Hs[hp], f"Hs{hp}"
            for c in range(8):
                A, Ak = acc()
                for hd in range(2):
                    r0 = 64 * hd
                    op("pe", L("matmul", A[r0:r0 + 64, 0:64], lhsT=MN[r0:r0 + 64, c, 0:64], rhs=H[r0:r0 + 64, c, :], start=True, stop=True), reads=["MN", Hk], writes=[Ak])
                op("dve", L("tensor_tensor", out=H[:, c + 1, :], in0=A[:, 0:64], in1=MN[:, c, 64:128], op=ALU.add), reads=[Ak, "MN"], writes=[Hk])
                yield
            yield
            A, Ak = acc()
            for c in range(8):
                for hd in range(2):
                    r0 = 64 * hd
                    o = A[r0:r0 + 64, c * 64:(c + 1) * 64]
                    op("pe", L("matmul", o, lhsT=H[r0:r0 + 64, c, :], rhs=QcT[r0:r0 + 64, c, :], start=True, stop=False), reads=[Hk, "QcT"], writes=[Ak])
                    op("pe", L("matmul", o, lhsT=XX[r0:r0 + 64, c, 0:64], rhs=AT[r0:r0 + 64, c, 64:128], start=False, stop=False), reads=["XX", "AT"], writes=[Ak])
                    op("pe", L("matmul", o, lhsT=V_tok[r0:r0 + 64, c, :], rhs=AT[r0:r0 + 64, c, 192:256], start=False, stop=True), reads=["V_tok", "AT"], writes=[Ak])
            op("act", L("activation", out=yt[:], in_=A[:], func=AF.Copy), reads=[Ak], writes=["yt"])
            op("pool", L("tensor_copy", out=H[:, 0, :], in_=H[:, 8, :]), reads=[Hk], writes=[Hk])
            yield
            op("pool", L("tensor_tensor", out=sc1[:], in0=yt[:], in1=yt[:], op=ALU.mult), reads=["yt"], writes=["sc1"])
            A1, A1k = acc()
            op("pe", L("matmul", A1[:, :], lhsT=blkones, rhs=yt[:], start=True, stop=True), reads=["cst", "yt"], writes=[A1k])
            A2, A2k = acc()
            op("pe", L("matmul", A2[:, :], lhsT=blkones, rhs=sc1[:], start=True, stop=True), reads=["cst", "sc1"], writes=[A2k])
            op("act", L("activation", out=sc2[:], in_=A1[:], func=AF.Copy, scale=1.0 / 64), reads=[A1k], writes=["sc2"])
            op("pool", L("tensor_tensor", out=sc1[:], in0=sc2[:], in1=sc2[:], op=ALU.mult), reads=["sc2"], writes=["sc1"])
            op("dve", L("scalar_tensor_tensor", out=sc1[:], in0=A2[:], scalar=1.0 / 64, in1=sc1[:], op0=ALU.mult, op1=ALU.subtract), reads=[A2k, "sc1"], writes=["sc1"])
            op("act", L("activation", out=sc1[:], in_=sc1[:], func=AF.Sqrt, bias=gnb[:, 0:1]), reads=["sc1", "gnb"], writes=["sc1"])
            op("dve", L("reciprocal", out=cwm[:], in_=sc1[:]), reads=["sc1"], writes=["cwm"])
            op("pool", L("tensor_tensor", out=yt[:], in0=yt[:], in1=sc2[:], op=ALU.subtract), reads=["yt", "sc2"], writes=["yt"])
            op("dve", L("tensor_tensor", out=yt[:], in0=yt[:], in1=cwm[:], op=ALU.mult), reads=["yt", "cwm"], writes=["yt"])
            op("dve", L("tensor_scalar", out=yt[:], in0=yt[:], scalar1=col(PV_LNW), scalar2=col(PV_LNB), op0=ALU.mult, op1=ALU.add), reads=["yt", "pv"], writes=["yt"])
            op("pool", L("tensor_tensor", out=yt[:], in0=yt[:], in1=bonus[:], op=ALU.add), reads=["yt", "lwp"], writes=["yt"])
            ob = C.nxt("obuf", 1)
            op("pool", L("tensor_tensor", out=obuf[ob][:], in0=yt[:], in1=gt[:], op=ALU.mult), reads=["yt", "gt"], writes=[f"obuf{ob}"])
            tk = S.dma("sp", L("dma_start", out=out_ap(slice(hp * 128, (hp + 1) * 128), t0, 512), in_=obuf[ob][:]), reads=[f"obuf{ob}"],
                       key=(out_key(t0) if out_key is not None else f"obT_d{ob}"))
            last_out[0] = tk
            toks.append(tk)
        last_out = [None]
        gens = [hp_body(0), hp_body(1)]
        while gens:
            for g_ in list(gens):
                try:
                    next(g_)
                except StopIteration:
                    gens.remove(g_)
        if after_sc is not None:
            after_sc(sci, last_out[0])
    return toks[-2:]


import contextlib

def attn_consts(SA):
    half = 8
    inv = (500000.0 ** (-np.arange(half, dtype=np.float32) * np.float32(2.0 / 16))).astype(np.float32)
    ang = (np.arange(SA, dtype=np.float32)[None, :] * inv[:, None]).astype(np.float32)
    cosF = np.ones((128, SA), np.float32); sinF = np.zeros((128, SA), np.float32)
    rperm = np.zeros((128, 128), np.float32)
    for hb in (0, 64):
        cosF[hb:hb + 8] = np.cos(ang); cosF[hb + 8:hb + 16] = np.cos(ang)
        sinF[hb:hb + 8] = -np.sin(ang); sinF[hb + 8:hb + 16] = np.sin(ang)
        for i in range(8):
            rperm[hb + i + 8, hb + i] = 1.0
            rperm[hb + i, hb + i + 8] = 1.0
    j = np.arange(128)[:, None]; i = np.arange(128)[None, :]
    mprev = (j >= i).astype(np.float32); mcur = (j <= i).astype(np.float32)
    mask = np.ascontiguousarray(np.concatenate([mprev, mcur, mprev, mcur], 1))
    return cosF, sinF, rperm, mask

def rwkv_consts():
    p = np.arange(128)
    mreset = np.ones((128, 512), np.float32); mreset[:, ::64] = 0.0
    blk = (p[:, None] // 64 == p[None, :] // 64).astype(np.float32)
    s = (p % 64)[:, None]; t = np.arange(64)[None, :]
    MU = (s < t).astype(np.float32); MUI = (s <= t).astype(np.float32)
    maskA = np.concatenate([MU, MUI, MU, MUI, MU, MUI, MU, MUI], 1)
    ML = (t < s).astype(np.float32)
    identf = (s == t).astype(np.float32)
    return np.ascontiguousarray(np.concatenate([mreset, blk, maskA, ML, identf], 1))

def rwkv_inputs(inp, hg):
    w_in = inp["w_in"][0]
    cs = slice(hg * 256, (hg + 1) * 256)
    sh = 3072
    cols = np.concatenate([np.arange(sh + hg * 256, sh + hg * 256 + 256), np.arange(sh + 1024 + hg * 256, sh + 1024 + hg * 256 + 256),
                           np.arange(sh + 2048 + hg * 256, sh + 2048 + hg * 256 + 256), np.arange(sh + 3072, sh + 3072 + 256)])
    wr = np.ascontiguousarray(w_in[:, cols])
    mu = inp["shift_mu"][0][cols - sh]
    pv = np.zeros((128, 22), np.float32)
    pv[:, 0:8] = mu.reshape(8, 128).T
    for k, name in ((8, "decay_w0"), (10, "iclr_a0"), (12, "k_k"), (14, "k_a"), (16, "r_k"), (18, "ln_x_w"), (20, "ln_x_b")):
        v = inp[name][0].reshape(-1)[cs]
        pv[:, k:k + 2] = v.reshape(2, 128).T
    w2a2 = np.ascontiguousarray(np.concatenate([inp["decay_w2"][0][:, cs], inp["iclr_a2"][0][:, cs]], 0))
    g2 = np.ascontiguousarray(inp["gate_g2"][0][:, cs])
    return {"wr": wr, "pv": pv, "w2a2": w2a2, "g2": g2}

def attn_inputs(inp, hg):
    w_in = inp["w_in"][0]
    cols = np.concatenate([w_in[:, hg * 256:(hg + 1) * 256], w_in[:, 1024 + hg * 256:1024 + (hg + 1) * 256], w_in[:, 2048 + hg * 256:2048 + (hg + 1) * 256]], 1)
    return {"wqkv": np.ascontiguousarray(cols)}


def build_p1(SA):
    nc = bass.Bass("TRN2", target_bir_lowering=False)
    di = lambda n, s, d=F32: nc.dram_tensor(n, s, d, kind="ExternalInput").ap()
    x_d = di("x", [SA, 1024]); ident = di("ident", [128, 128]); g = di("g_mix", [1024])
    wqkv = di("wqkv", [1024, 768]); cosF = di("cosF", [128, SA]); sinF = di("sinF", [128, SA]); rperm = di("rperm", [128, 128]); mask = di("mask", [128, 512])
    wr = di("wr", [1024, 1024]); pv = di("pv", [128, 22]); w2a2 = di("w2a2", [128, 256]); g2 = di("g2", [128, 256]); cst = di("cst", [128, 1280])
    oaT = nc.dram_tensor("oaT", [256, SA], BF16, kind="ExternalOutput").ap()
    obT = nc.dram_tensor("obT", [256, SA], BF16, kind="ExternalOutput").ap()
    with contextlib.ExitStack() as st0:
        S = Sched(nc, st0)
        with contextlib.ExitStack() as st:
            C = Ctx(nc, S, st, "a1_")
            setup_common(C, ident)
            toks1 = phase_a1(C, SA, x_d, wqkv, g, cosF, sinF, rperm, mask, (lambda rows, g0, n: oaT[rows, g0:g0 + n]))
            S.run()
        S.barrier()
        with contextlib.ExitStack() as st:
            C = Ctx(nc, S, st, "a2_")
            setup_common(C, ident)
            toks2 = phase_a2(C, SA, x_d, wr, g, pv, w2a2, g2, cst, (lambda rows, g0, n: obT[rows, g0:g0 + n]))
            S.finish(list(toks1) + list(toks2))
            S.run()
    return nc


def build_p2(TB):
    nc = bass.Bass("TRN2", target_bir_lowering=False)
    di = lambda n, s, d=F32: nc.dram_tensor(n, s, d, kind="ExternalInput").ap()
    x_d = di("x", [TB, 1024]); oa = di("oaT", [1024, TB], BF16); ob = di("obT", [1024, TB], BF16)
    ident = di("ident", [128, 128])
    w = {"wg": di("wg", [1024, 2048]), "pa": di("pa", [1024, 1024]), "pb": di("pb", [1024, 1024]), "wo": di("wo", [1024, 1024]),
         "fg": di("fg", [1024, 2816]), "fu": di("fu", [1024, 2816]), "fd": di("fd", [2816, 1024]),
         "g_mix": di("g_mix", [1024]), "g_ffn": di("g_ffn", [1024]), "g_fin": di("g_fin", [1024])}
    out_d = nc.dram_tensor("out", [TB, 1024], F32, kind="ExternalOutput").ap()
    with contextlib.ExitStack() as st:
        S = Sched(nc, st)
        C = Ctx(nc, S, st)
        setup_common(C, ident)
        w16 = convert_weights(C, w)
        toks = phase_b(C, TB, x_d, oa, ob, out_d, w, w16)
        S.finish(toks)
        S.run()
    return nc


def build_fused(SA, TB):
    nc = bass.Bass("TRN2", target_bir_lowering=False)
    di = lambda n, s, d=F32: nc.dram_tensor(n, s, d, kind="ExternalInput").ap()
    x_d = di("x", [SA, 1024]); ident = di("ident", [128, 128]); g = di("g_mix", [1024])
    wqkv = di("wqkv", [1024, 768]); cosF = di("cosF", [128, SA]); sinF = di("sinF", [128, SA]); rperm = di("rperm", [128, 128]); mask = di("mask", [128, 512])
    wr = di("wr", [1024, 1024]); pv = di("pv", [128, 22]); w2a2 = di("w2a2", [128, 256]); g2 = di("g2", [128, 256]); cst = di("cst", [128, 1280])
    qoff = di("qoff", [1, 1], mybir.dt.int32)
    w = {"wg": di("wg", [1024, 2048]), "pa": di("pa", [1024, 1024]), "pb": di("pb", [1024, 1024]), "wo": di("wo", [1024, 1024]),
         "fg": di("fg", [1024, 2816]), "fu": di("fu", [1024, 2816]), "fd": di("fd", [2816, 1024]),
         "g_mix": g, "g_ffn": di("g_ffn", [1024]), "g_fin": di("g_fin", [1024])}
    out_d = nc.dram_tensor("out", [TB, 1024], F32, kind="ExternalOutput").ap()
    NG = 4
    CH = min(2048, TB)
    NCH = SA // CH
    send_a = nc.dram_tensor("send_a", [NCH, 256, CH], BF16, kind="Internal").ap()
    send_b = nc.dram_tensor("send_b", [NCH, 256, CH], BF16, kind="Internal").ap()
    recv_a = nc.dram_tensor("recv_a", [NCH, NG * 256, CH], BF16, kind="Internal").ap()
    recv_b = nc.dram_tensor("recv_b", [NCH, NG * 256, CH], BF16, kind="Internal").ap()
    qm = di("qm", [1, 1], mybir.dt.int32)
    groups = [[0, 1, 2, 3], [4, 5, 6, 7]]
    def chunk_ap(t):
        return lambda rows, g0, n: t[g0 // CH, rows, g0 % CH:g0 % CH + n]
    with contextlib.ExitStack() as st0:
        S = Sched(nc, st0)
        qreg = st0.enter_context(nc.sync.register("qreg"))
        mreg = st0.enter_context(nc.sync.register("mreg"))
        C0 = Ctx(nc, S, st0, "w_")
        w16 = {}
        with contextlib.ExitStack() as st:
            C = Ctx(nc, S, st, "a1_")
            setup_common(C, ident)
            phase_a1(C, SA, x_d, wqkv, g, cosF, sinF, rperm, mask, chunk_ap(send_a), out_piece=min(CH, 2048),
                     after_mc0=lambda: w16.update(convert_weights(C0, w)))
            S.run()
        S.barrier()
        for m in range(NCH):
            S.cc(L("collective_compute", "AllGather", ALU.bypass, replica_groups=groups, ins=[send_a[m]], outs=[recv_a[m]]), key="cca")
        with contextlib.ExitStack() as st:
            C = Ctx(nc, S, st, "a2_")
            setup_common(C, ident)
            SCH = CH // 512
            pend = {}
            def gather_b(m):
                S._buf(f"sendb{m}")["w"] = [pend.pop(m)]
                S.cc(L("collective_compute", "AllGather", ALU.bypass, replica_groups=groups, ins=[send_b[m]], outs=[recv_b[m]]), reads=[f"sendb{m}"], key="ccb")
            def after_sc(sci, tk):
                if (sci + 1) % SCH == 0:
                    pend[sci // SCH] = tk
                for m in [m for m in pend if sci >= (m + 1) * SCH]:
                    gather_b(m)
            phase_a2(C, SA, x_d, wr, g, pv, w2a2, g2, cst, chunk_ap(send_b), after_sc=after_sc, out_key=lambda t0: f"obT_c{t0 // CH}")
            for m in sorted(pend):
                gather_b(m)
            S.run()
        S.barrier()
        with contextlib.ExitStack() as st:
            C = Ctx(nc, S, st, "b_")
            C.dyn = {"reg": qreg, "qoff": qoff, "mreg": mreg, "qm": qm, "CH": CH}
            setup_common(C, ident)
            toks = phase_b(C, TB, x_d, recv_a, recv_b, out_d, w, w16)
            S.finish(toks)
            S.run()
    return nc


def kernel_unfused(**inputs):
    inp = {k: np.asarray(v) for k, v in inputs.items()}
    x = inp["x"]
    NBATCH, SEQ, _ = x.shape
    NQ = 8 // NBATCH
    TB = SEQ // NQ
    cosF, sinF, rperm, mask = attn_consts(SEQ)
    cst = rwkv_consts()
    ident = np.eye(128, dtype=np.float32)
    nc1 = build_p1(SEQ)
    maps1 = []
    for c in range(8):
        b, hg = c // NQ, c % NQ
        m = {"x": np.ascontiguousarray(x[b]), "ident": ident, "g_mix": inp["norm_mix_g"][0], "cosF": cosF, "sinF": sinF, "rperm": rperm, "mask": mask, "cst": cst}
        m.update(attn_inputs(inp, hg))
        m.update(rwkv_inputs(inp, hg))
        maps1.append(m)
    res1 = run_bass_kernel_spmd(nc1, maps1, core_ids=list(range(8)))
    nc2 = build_p2(TB)
    wg = np.ascontiguousarray(inp["w_in"][0][:, 8448 - 2048:])
    maps2 = []
    for c in range(8):
        b, q = c // NQ, c % NQ
        oa = np.concatenate([res1.results[b * NQ + hg]["oaT"][:, q * TB:(q + 1) * TB] for hg in range(NQ)], 0)
        ob = np.concatenate([res1.results[b * NQ + hg]["obT"][:, q * TB:(q + 1) * TB] for hg in range(NQ)], 0)
        maps2.append({"x": np.ascontiguousarray(x[b, q * TB:(q + 1) * TB]), "oaT": np.ascontiguousarray(oa), "obT": np.ascontiguousarray(ob), "ident": ident,
                      "wg": wg, "pa": inp["proj_attn"][0], "pb": inp["proj_rwkv"][0], "wo": inp["w_out"][0],
                      "fg": inp["ffn_w_gate"][0], "fu": inp["ffn_w_up"][0], "fd": inp["ffn_w_down"][0],
                      "g_mix": inp["norm_mix_g"][0], "g_ffn": inp["norm_ffn_g"][0], "g_fin": inp["norm_final_g"]})
    res2 = run_bass_kernel_spmd(nc2, maps2, core_ids=list(range(8)))
    out = np.zeros((NBATCH, SEQ, 1024), np.float32)
    for c in range(8):
        b, q = c // NQ, c % NQ
        out[b, q * TB:(q + 1) * TB] = res2.results[c]["out"]
    return out


def kernel(**inputs):
    inp = {k: np.asarray(v) for k, v in inputs.items()}
    x = inp["x"]
    NBATCH, SEQ, _ = x.shape
    NQ = 8 // NBATCH
    TB = SEQ // NQ
    cosF, sinF, rperm, mask = attn_consts(SEQ)
    cst = rwkv_consts()
    ident = np.eye(128, dtype=np.float32)
    nc = build_fused(SEQ, TB)
    wg = np.ascontiguousarray(inp["w_in"][0][:, 8448 - 2048:])
    maps = []
    for c in range(8):
        b, q = c // NQ, c % NQ
        m = {"x": np.ascontiguousarray(x[b]), "ident": ident, "g_mix": inp["norm_mix_g"][0], "cosF": cosF, "sinF": sinF, "rperm": rperm, "mask": mask, "cst": cst,
             "qoff": np.array([[q * TB]], np.int32), "qm": np.array([[q * (TB // min(2048, TB))]], np.int32),
             "wg": wg, "pa": inp["proj_attn"][0], "pb": inp["proj_rwkv"][0], "wo": inp["w_out"][0],
             "fg": inp["ffn_w_gate"][0], "fu": inp["ffn_w_up"][0], "fd": inp["ffn_w_down"][0],
             "g_ffn": inp["norm_ffn_g"][0], "g_fin": inp["norm_final_g"]}
        m.update(attn_inputs(inp, q))
        m.update(rwkv_inputs(inp, q))
        maps.append(m)
    res = run_bass_kernel_spmd(nc, maps, core_ids=list(range(8)))
    out = np.zeros((NBATCH, SEQ, 1024), np.float32)
    for c in range(8):
        b, q = c // NQ, c % NQ
        out[b, q * TB:(q + 1) * TB] = res.results[c]["out"]
    return out
```
